# Optimizing a Trainium2 kernel written in Bass

```python
import math
import jax, jax.numpy as jnp
from jax import lax
import numpy as np


D_MODEL = 1024
BATCH = 4
SEQ = 8192
DEPTH = 2

GRID_W = 64
CTX_LEN = 256
MIX_W = D_MODEL
DA_HEAD_DIM = 64
DA_W = MIX_W // 2
DA_HEADS = DA_W // (2 * DA_HEAD_DIM)
FN_W = MIX_W - DA_W
FN_GROUPS = 4
FN_GW = FN_W // FN_GROUPS
Q_BLOCK = 128
ROPE_THETA = 10000.0
ROPE_AXIS_DIM = DA_HEAD_DIM // 2
HY_FILTER_EMB = 33
HY_FILTER_HIDDEN = 64
HY_DECAY_TARGET = 1e-2
HY_FAST_DECAY = 0.3
HY_SLOW_DECAY = 1.5
FFN_HIDDEN = ((8 * D_MODEL // 3 + 127) // 128) * 128
N_MOD = 6
LN_EPS = 1e-5
F32 = jnp.float32

kernel_name = 'hybrid_diffattn_fnet_hyena_convffn_dit'


def layer_norm(x, g=None, b=None):
    xf = x.astype(F32)
    mu = jnp.mean(xf, -1, keepdims=True)
    var = jnp.mean(jnp.square(xf - mu), -1, keepdims=True)
    y = (xf - mu) * lax.rsqrt(var + LN_EPS)
    if g is not None:
        y = y * g.astype(F32) + b.astype(F32)
    return y.astype(x.dtype)


def ada_params(cvec, w, b):
    m = jax.nn.silu(cvec) @ w + b
    return jnp.split(m[..., None, :], N_MOD, axis=-1)


def modulate(x, shift, scale):
    return layer_norm(x) * (1 + scale) + shift


def dwconv3(u, w, b):
    up = jnp.pad(u, ((0, 0), (1, 1), (0, 0)))
    return up[:, :-2] * w[0] + up[:, 1:-1] * w[1] + up[:, 2:] * w[2] + b


def axial_rope_angles(rows):
    pos_r = jnp.repeat(jnp.arange(rows, dtype=F32), GRID_W)
    pos_c = jnp.tile(jnp.arange(GRID_W, dtype=F32), rows)
    inv = ROPE_THETA ** (-jnp.arange(0, ROPE_AXIS_DIM, 2, dtype=F32) / ROPE_AXIS_DIM)
    return jnp.concatenate([pos_r[:, None] * inv, pos_c[:, None] * inv], -1)


def _rotate(xs, ang):
    x1, x2 = jnp.split(xs, 2, axis=-1)
    cos = jnp.cos(ang)[None, :, None, :].astype(xs.dtype)
    sin = jnp.sin(ang)[None, :, None, :].astype(xs.dtype)
    return jnp.concatenate([x1 * cos - x2 * sin, x2 * cos + x1 * sin], -1)


def rope_2d(x, ang):
    n = ang.shape[-1] // 2
    xr, xc = jnp.split(x, 2, axis=-1)
    return jnp.concatenate([_rotate(xr, ang[:, :n]), _rotate(xc, ang[:, n:])], -1)


def lambda_init(layer_idx):
    return 0.8 - 0.6 * math.exp(-0.3 * layer_idx)


def _split_qk(t):
    B, L, _ = t.shape
    t = t.reshape(B, L, DA_HEADS, 2, DA_HEAD_DIM)
    return t[..., 0, :], t[..., 1, :]


def diff_attention(q1, q2, k1, k2, v, lam):
    B, Lq, H, d = q1.shape
    nb = Lq // Q_BLOCK
    scale = d ** -0.5

    def to_blocks(q):
        return q.reshape(B, nb, Q_BLOCK, H, d).transpose(1, 0, 2, 3, 4)

    def block(qs):
        a1, a2 = qs
        s1 = jnp.einsum('bqhd,bkhd->bhqk', a1, k1).astype(F32) * scale
        s2 = jnp.einsum('bqhd,bkhd->bhqk', a2, k2).astype(F32) * scale
        p = jax.nn.softmax(s1, axis=-1) - lam * jax.nn.softmax(s2, axis=-1)
        return jnp.einsum('bhqk,bkhe->bqhe', p.astype(v.dtype), v)

    o = lax.map(block, (to_blocks(q1), to_blocks(q2)))
    return o.transpose(1, 0, 2, 3, 4).reshape(B, Lq, H, v.shape[-1])


def head_rmsnorm(o, g, post):
    of = o.astype(F32)
    y = of * lax.rsqrt(jnp.mean(of * of, -1, keepdims=True) + LN_EPS) * g.astype(F32) * post
    return y.astype(o.dtype)


def fourier_mix(f):
    B, L, _ = f.shape
    g = f.astype(F32).reshape(B, L, FN_GROUPS, FN_GW)
    return jnp.fft.fftn(g, axes=(1, 3), norm='ortho').real.reshape(B, L, FN_W).astype(f.dtype)


def diff_fourier_mixer(h_lat, h_ctx, w_in, w_out, lq1, lk1, lq2, lk2, subln_g, lam_init, ang, ctx_out):
    B, L, _ = h_lat.shape
    Lc = h_ctx.shape[1]
    lam = (jnp.exp(jnp.sum(lq1.astype(F32) * lk1.astype(F32)))
           - jnp.exp(jnp.sum(lq2.astype(F32) * lk2.astype(F32))) + lam_init)
    post = 1.0 - lam_init
    kv_c = h_ctx @ w_in[:, DA_W:3 * DA_W]
    k1c, k2c = _split_qk(kv_c[..., :DA_W])
    v_c = kv_c[..., DA_W:].reshape(B, Lc, DA_HEADS, 2 * DA_HEAD_DIM)
    p = h_lat @ w_in
    q, k, v, f = jnp.split(p, [DA_W, 2 * DA_W, 3 * DA_W], axis=-1)
    q1, q2 = _split_qk(q)
    k1, k2 = _split_qk(k)
    q1, q2, k1, k2 = rope_2d(q1, ang), rope_2d(q2, ang), rope_2d(k1, ang), rope_2d(k2, ang)
    v = v.reshape(B, L, DA_HEADS, 2 * DA_HEAD_DIM)
    K1 = jnp.concatenate([k1c, k1], axis=1)
    K2 = jnp.concatenate([k2c, k2], axis=1)
    V = jnp.concatenate([v_c, v], axis=1)
    o = head_rmsnorm(diff_attention(q1, q2, K1, K2, V, lam), subln_g, post).reshape(B, L, DA_W)
    y_lat = jnp.concatenate([o, fourier_mix(f)], axis=-1) @ w_out
    y_ctx = None
    if ctx_out:
        q1c, q2c = _split_qk(h_ctx @ w_in[:, :DA_W])
        oc = head_rmsnorm(diff_attention(q1c, q2c, k1c, k2c, v_c, lam), subln_g, post).reshape(B, Lc, DA_W)
        y_ctx = jnp.concatenate([oc, fourier_mix(h_ctx @ w_in[:, 3 * DA_W:])], axis=-1) @ w_out
    return y_lat, y_ctx


def hyena_filter_freq(L, w1, b1, w2, b2, w3, b3, freq, w4):
    t = jnp.linspace(0.0, 1.0, L, dtype=F32)[:, None]
    bands = (HY_FILTER_EMB - 1) // 2
    w = 2.0 * math.pi * jnp.arange(L, dtype=F32)[:, None] / L
    fb = jnp.linspace(1e-4, bands - 1, bands, dtype=F32)[None, :]
    z = jnp.concatenate([t, jnp.cos(fb * w), -jnp.sin(fb * w)], axis=-1)
    fr = freq.astype(F32)
    hdn = jnp.sin(fr * (z @ w1.astype(F32) + b1.astype(F32)))
    hdn = jnp.sin(fr * (hdn @ w2.astype(F32) + b2.astype(F32)))
    hdn = jnp.sin(fr * (hdn @ w3.astype(F32) + b3.astype(F32)))
    h = (hdn @ w4.astype(F32)).reshape(L, 2, D_MODEL)
    min_decay = math.log(HY_DECAY_TARGET) / HY_SLOW_DECAY
    max_decay = math.log(HY_DECAY_TARGET) / HY_FAST_DECAY
    deltas = jnp.abs(jnp.linspace(min_decay, max_decay, D_MODEL, dtype=F32))
    h = h * jnp.exp(-t * deltas)[:, None, :]
    hf, hb = h[:, 0], h[:, 1]
    k2 = jnp.concatenate([hf, jnp.zeros((1, D_MODEL), F32), hb[1:][::-1]], axis=0)
    k2 = k2 / jnp.sum(jnp.abs(k2), axis=0, keepdims=True)
    return jnp.fft.rfft(k2, axis=0)


def hyena_mixer(h, w_in, conv_w, conv_b, filt, d_skip, w_out):
    L = h.shape[1]
    u = dwconv3(h @ w_in, conv_w, conv_b)
    x0, x1, v = jnp.split(u, 3, axis=-1)
    v = v * x1
    k_f = hyena_filter_freq(L, *filt)
    vf = v.astype(F32)
    y = jnp.fft.irfft(jnp.fft.rfft(vf, n=2 * L, axis=1) * k_f[None], n=2 * L, axis=1)[:, :L]
    y = (y + vf * d_skip.astype(F32)).astype(h.dtype)
    return (y * x0) @ w_out


def conv_ffn(h, w_in, conv_w, conv_b, w_out):
    u = dwconv3(h @ w_in, conv_w, conv_b)
    a, g = jnp.split(u, 2, axis=-1)
    return (jax.nn.gelu(a, approximate=False) * g) @ w_out


def setup_inputs(seed: int = 0) -> dict:
    key = jax.random.key(seed)
    ks = jax.random.split(key, 34)
    n_even = (DEPTH + 1) // 2
    n_odd = DEPTH // 2
    beta = (8 * DEPTH) ** -0.25
    D = D_MODEL
    F = FFN_HIDDEN
    H = HY_FILTER_HIDDEN

    def nrm(k, shape, s):
        return jax.random.normal(k, shape, F32) * s

    return {
        'x': nrm(ks[0], (BATCH, SEQ, D), 1.0),
        'c': nrm(ks[1], (BATCH, D), 1.0),
        'ctx': nrm(ks[2], (BATCH, CTX_LEN, D), 1.0),
        'c_ctx': nrm(ks[3], (D,), 1.0),
        'mod_w': nrm(ks[4], (DEPTH, D, N_MOD * D), D ** -0.5),
        'mod_b': nrm(ks[5], (DEPTH, N_MOD * D), 0.02),
        'ln1_g': 1.0 + nrm(ks[6], (DEPTH, D), 0.02),
        'ln1_b': nrm(ks[7], (DEPTH, D), 0.02),
        'ln2_g': 1.0 + nrm(ks[8], (DEPTH, D), 0.02),
        'ln2_b': nrm(ks[9], (DEPTH, D), 0.02),
        'ffn_w_in': nrm(ks[10], (DEPTH, D, 2 * F), D ** -0.5),
        'ffn_conv_w': nrm(ks[11], (DEPTH, 3, 2 * F), 3 ** -0.5),
        'ffn_conv_b': nrm(ks[12], (DEPTH, 2 * F), 0.02),
        'ffn_w_out': nrm(ks[13], (DEPTH, F, D), beta * F ** -0.5),
        'da_w_in': nrm(ks[14], (n_even, D, 3 * DA_W + FN_W), D ** -0.5),
        'da_w_out': nrm(ks[15], (n_even, MIX_W, D), beta * MIX_W ** -0.5),
        'da_lam_q1': nrm(ks[16], (n_even, DA_HEAD_DIM), 0.1),
        'da_lam_k1': nrm(ks[17], (n_even, DA_HEAD_DIM), 0.1),
        'da_lam_q2': nrm(ks[18], (n_even, DA_HEAD_DIM), 0.1),
        'da_lam_k2': nrm(ks[19], (n_even, DA_HEAD_DIM), 0.1),
        'da_subln_g': 1.0 + nrm(ks[20], (n_even, 2 * DA_HEAD_DIM), 0.02),
        'hy_w_in': nrm(ks[21], (n_odd, D, 3 * MIX_W), D ** -0.5),
        'hy_conv_w': nrm(ks[22], (n_odd, 3, 3 * MIX_W), 3 ** -0.5),
        'hy_conv_b': nrm(ks[23], (n_odd, 3 * MIX_W), 0.02),
        'hy_f_w1': nrm(ks[24], (n_odd, HY_FILTER_EMB, H), HY_FILTER_EMB ** -0.5),
        'hy_f_b1': nrm(ks[25], (n_odd, H), 0.1),
        'hy_f_w2': nrm(ks[26], (n_odd, H, H), H ** -0.5),
        'hy_f_b2': nrm(ks[27], (n_odd, H), 0.1),
        'hy_f_w3': nrm(ks[28], (n_odd, H, H), H ** -0.5),
        'hy_f_b3': nrm(ks[29], (n_odd, H), 0.1),
        'hy_f_freq': 1.0 + nrm(ks[30], (n_odd, H), 0.1),
        'hy_f_w4': nrm(ks[31], (n_odd, H, 2 * MIX_W), H ** -0.5),
        'hy_d': nrm(ks[32], (n_odd, MIX_W), 1.0),
        'hy_w_out': nrm(ks[33], (n_odd, MIX_W, D), beta * MIX_W ** -0.5),
    }


def reference(x, c, ctx, c_ctx, mod_w, mod_b, ln1_g, ln1_b, ln2_g, ln2_b,
              ffn_w_in, ffn_conv_w, ffn_conv_b, ffn_w_out,
              da_w_in, da_w_out, da_lam_q1, da_lam_k1, da_lam_q2, da_lam_k2, da_subln_g,
              hy_w_in, hy_conv_w, hy_conv_b, hy_f_w1, hy_f_b1, hy_f_w2, hy_f_b2,
              hy_f_w3, hy_f_b3, hy_f_freq, hy_f_w4, hy_d, hy_w_out):
    n_lat = x.shape[1]
    rows = n_lat // GRID_W
    ang = axial_rope_angles(rows)
    alpha = (2 * DEPTH) ** 0.25
    for i in range(DEPTH):
        even = i % 2 == 0
        ctx_out = any(j % 2 == 0 for j in range(i + 1, DEPTH))
        sh1, sc1, g1, sh2, sc2, g2 = ada_params(c, mod_w[i], mod_b[i])
        h_lat = modulate(x, sh1, sc1)
        if even or ctx_out:
            csh1, csc1, cg1, csh2, csc2, cg2 = ada_params(c_ctx, mod_w[i], mod_b[i])
            h_ctx = modulate(ctx, csh1, csc1)
        if even:
            e = i // 2
            y_lat, y_ctx = diff_fourier_mixer(h_lat, h_ctx, da_w_in[e], da_w_out[e],
                                              da_lam_q1[e], da_lam_k1[e], da_lam_q2[e], da_lam_k2[e],
                                              da_subln_g[e], lambda_init(i), ang, ctx_out)
        else:
            o = i // 2
            filt = (hy_f_w1[o], hy_f_b1[o], hy_f_w2[o], hy_f_b2[o], hy_f_w3[o], hy_f_b3[o],
                    hy_f_freq[o], hy_f_w4[o])
            y_lat = hyena_mixer(h_lat, hy_w_in[o], hy_conv_w[o], hy_conv_b[o], filt, hy_d[o], hy_w_out[o])
            if ctx_out:
                y_ctx = hyena_mixer(h_ctx, hy_w_in[o], hy_conv_w[o], hy_conv_b[o], filt, hy_d[o], hy_w_out[o])
        ffn = (ffn_w_in[i], ffn_conv_w[i], ffn_conv_b[i], ffn_w_out[i])
        x = layer_norm(alpha * x + g1 * y_lat, ln1_g[i], ln1_b[i])
        x = layer_norm(alpha * x + g2 * conv_ffn(modulate(x, sh2, sc2), *ffn), ln2_g[i], ln2_b[i])
        if ctx_out:
            ctx = layer_norm(alpha * ctx + cg1 * y_ctx, ln1_g[i], ln1_b[i])
            ctx = layer_norm(alpha * ctx + cg2 * conv_ffn(modulate(ctx, csh2, csc2), *ffn), ln2_g[i], ln2_b[i])
    return x
```

```python
import contextlib
import math
import numpy as np
import ml_dtypes
import concourse.bass as bass
import concourse.mybir as mybir
from concourse.bass_utils import run_bass_kernel_spmd

F32 = mybir.dt.float32
BF16 = mybir.dt.bfloat16
AF = mybir.ActivationFunctionType
ALU = mybir.AluOpType

D = 1024
T = 8192
NCTX = 256
NK = T + NCTX
FF = 2816
EPS = 1e-5
ALPHA = 4 ** 0.25
DEBUG = None
DEBUG_OUT = set()


class Buf:
    __slots__ = ("name", "w", "r")

    def __init__(self, name):
        self.name = name
        self.w = None
        self.r = []


class Prog:
    ENGS = ("pe", "act", "dve", "pool", "sp")
    NDMASEM = 8

    def __init__(self, nc):
        self.nc = nc
        self.ops = []
        self.bar_deps = set()
        self.bar_pending = set()
        self.since_bar = {}

    def barrier(self):
        deps = set(self.bar_deps)
        for k, v in self.since_bar.items():
            if k == "dma":
                deps.update(v)
            else:
                deps.add(v)
        self.bar_deps = deps
        self.bar_pending = set(self.ENGS)
        self.since_bar = {}

    def buf(self, name="b"):
        return Buf(name)

    def op(self, eng, fn, reads=(), writes=(), dma=False):
        oid = len(self.ops)
        deps = set()
        for b in reads:
            if b.w is not None:
                deps.add(b.w)
        for b in writes:
            if b.w is not None:
                deps.add(b.w)
            last = {}
            for r in b.r:
                o = self.ops[r]
                if o["dma"]:
                    deps.add(r)
                else:
                    last[o["eng"]] = r
            deps.update(last.values())
        for b in reads:
            b.r.append(oid)
        for b in writes:
            b.w = oid
            b.r = []
        if eng in self.bar_pending:
            deps.update(self.bar_deps)
            self.bar_pending.discard(eng)
        deps.discard(oid)
        if dma:
            self.since_bar.setdefault("dma", []).append(oid)
        else:
            self.since_bar[eng] = oid
        self.ops.append(dict(id=oid, eng=eng, fn=fn, deps=deps, dma=dma))
        return oid

    def dma(self, q, out, in_, reads=(), writes=(), **kw):
        return self.op(q, lambda e: e.dma_start(out=out, in_=in_, **kw), reads, writes, dma=True)

    def emit(self, final_deps):
        nc = self.nc
        ops = self.ops
        ops.append(dict(id=len(ops), eng="sp", fn=None, deps=set(final_deps), dma=False))
        needed = set()
        for o in ops:
            for d in list(o["deps"]):
                od = ops[d]
                if o["eng"] == "pe" and od["eng"] == "pe" and not od["dma"] and not o["dma"]:
                    o["deps"].discard(d)
            needed |= o["deps"]
        st = contextlib.ExitStack()
        sems = {e: st.enter_context(nc.semaphore("s_" + e)) for e in self.ENGS}
        dsems = {q: [st.enter_context(nc.semaphore(f"d_{q}{i}")) for i in range(self.NDMASEM)]
                 for q in ("sp", "act", "pool")}
        cnt = {e: 0 for e in self.ENGS}
        dcnt = {q: 0 for q in dsems}
        for o in ops:
            o["sig"] = None
            o["pre"] = None
            if o["dma"]:
                q = o["eng"]
                k = dcnt[q]
                dcnt[q] += 1
                s = dsems[q][k % self.NDMASEM]
                v = 16 * (k // self.NDMASEM + 1)
                o["sig"] = (s, v, 16)
                if k >= self.NDMASEM:
                    o["pre"] = (s, v - 16)
            elif o["id"] in needed:
                cnt[o["eng"]] += 1
                o["sig"] = (sems[o["eng"]], cnt[o["eng"]], 1)
        self.stats = dict(n=len(ops), cnt=cnt, dcnt=dcnt)
        per = {e: [o for o in ops if o["eng"] == e] for e in self.ENGS}

        def replay(eng_name):
            def run(e):
                waited = {}

                def w(s, v):
                    if waited.get(id(s), 0) < v:
                        e.wait_ge(s, v)
                        waited[id(s)] = v
                for o in per[eng_name]:
                    if o["pre"] is not None:
                        w(*o["pre"])
                    for d in sorted(o["deps"]):
                        s, v, _ = ops[d]["sig"]
                        w(s, v)
                    if o["fn"] is None:
                        continue
                    ins = o["fn"](e)
                    if o["sig"] is not None:
                        ins.then_inc(o["sig"][0], o["sig"][2])
            return run

        with nc.Block() as block:
            block.tensor(replay("pe"))
            block.scalar(replay("act"))
            block.vector(replay("dve"))
            block.gpsimd(replay("pool"))
            block.sync(replay("sp"))
        st.close()


class Ctx:
    pass


def build_program():
    nc = bass.Bass("TRN2", target_bir_lowering=False)
    g = Ctx()
    g.nc = nc
    P = Prog(nc)
    g.P = P
    g.final = []
    g.dbg_stores = []
    g.b_scrB = P.buf("scrB")
    g.b_scrX = P.buf("scrX")
    g.out_stores = []
    g.bk = [P.buf(f"bank{i}") for i in range(8)]

    def din(name, shape, dt=F32):
        return nc.dram_tensor(name, list(shape), dt, kind="ExternalInput").ap()

    def dscr(name, shape, dt):
        kind = "ExternalOutput" if name in DEBUG_OUT else "Internal"
        return nc.dram_tensor(name, list(shape), dt, kind=kind).ap()

    I = Ctx()
    g.I = I
    I.x = din("x", [T, D])
    I.ctx = din("ctx", [NCTX, D])
    I.cvec = din("cvec", [128, 8, 2])
    I.mod_w = din("mod_w", [2, D, 6 * D])
    I.modb_col = din("modb_col", [2, 128, 48])
    I.rows = din("rows", [NROWS, 128, D])
    I.cols = din("cols", [NCOLS, 128, 1])
    I.ident = din("ident", [128, 128])
    I.da_w_in = din("da_w_in", [D, 2048])
    I.da_w_perm = din("da_w_perm", [D, 1024])
    I.da_w_out = din("da_w_out", [D, D])
    I.rope_c = din("rope_c", [128, NK])
    I.rope_s = din("rope_s", [128, NK])
    I.lam = din("lam", [128, 4, 64])
    I.hy_w_in = din("hy_w_in", [D, 3 * D])
    I.hy_w_out = din("hy_w_out", [D, D])
    I.hy_cw = din("hy_cw", [128, 24, 4])
    I.dft_f1 = din("dft_f1", [128, 2, 256])
    I.dft_mt = din("dft_mt", [128, 128, 512])
    I.hyf_z = din("hyf_z", [2, 33, T])
    I.hyf_t = din("hyf_t", [2, 128, T])
    I.hyf_w1 = din("hyf_w1", [33, 64])
    I.hyf_w23 = din("hyf_w23", [2, 64, 64])
    I.hyf_w4 = din("hyf_w4", [64, 2 * D])
    I.hyf_cols = din("hyf_cols", [64, 4])
    I.hy_cols = din("hy_cols", [128, 8, 2])
    I.ffn_w_in = din("ffn_w_in", [2, D, 2 * FF])
    I.ffn_w_out = din("ffn_w_out", [2, FF, D])
    I.ffn_cw = din("ffn_cw", [2, 128, 44, 4])
    I.f64 = din("f64", [64, 128])
    I.fmt = din("fmt", [128, 64, 2, 256])
    I.fcs = din("fcs", [128, 2, 128])
    g.out = nc.dram_tensor("out", [T, D], F32, kind="ExternalOutput").ap()

    S = Ctx()
    g.S = S
    S.QT = dscr("QT", [4, 128, T], BF16)
    S.KT = dscr("KT", [4, 128, NK], BF16)
    S.V = dscr("V", [NK, 512], BF16)
    S.F = dscr("F", [T, 512], BF16)
    S.mixT = dscr("mixT", [8, 128, T], BF16)
    S.x1 = dscr("x1", [T, D], F32)
    S.x2 = dscr("x2", [T, D], F32)
    S.h2T = dscr("h2T", [8, 128, T + 2], BF16)
    S.x0T = dscr("x0T", [8, 128, T], BF16)
    S.vT = dscr("vT", [8, 128, T], BF16)
    S.vtok = dscr("vtok", [T, D], BF16)
    S.ktok = dscr("ktok", [2 * T, D], BF16)
    S.KS = dscr("KS", [8, 128, 128, 256], BF16)
    S.Ztok = dscr("Ztok", [2, 2 * T, D], BF16)
    S.MTb = dscr("MTb", [128, 128, 512], BF16)
    S.y_dbg = dscr("y_dbg", [128, T], F32) if "y_dbg" in DEBUG_OUT else None
    S.knorm_dbg = dscr("knorm_dbg", [128, 8], F32) if "knorm_dbg" in DEBUG_OUT else None
    S.hT_dbg = dscr("hT_dbg", [128, 8, T], F32) if "hT_dbg" in DEBUG_OUT else None

    arena = nc.alloc_sbuf_tensor("arena", [128, ARENA_BYTES // 4], F32)
    g.arena = arena
    g.ps = nc.alloc_psum_tensor("ps", [128, 8, 512], F32)
    g.sb_off = 0

    def sb(shape, dt, reset_to=None):
        n = int(np.prod(shape))
        nbytes = n * (4 if dt == F32 else 2)
        nbytes = (nbytes + 31) // 32 * 32
        off = g.sb_off
        assert off + nbytes <= ARENA_BYTES, (off, nbytes)
        g.sb_off += nbytes
        v = arena[:, off // 4:(off + nbytes) // 4]
        if dt != F32:
            v = v.bitcast(dt)
        v = v[:, 0:n]
        if len(shape) == 2:
            v = v.rearrange("p (a b) -> p a b", b=shape[1])
        elif len(shape) == 3:
            v = v.rearrange("p (a b c) -> p a b c", b=shape[1], c=shape[2])
        return v
    g.sb = sb

    g.ident = sb([128], BF16)
    g.identf = sb([128], F32)
    g.epsT = sb([1], F32)
    bconst = P.buf("const")
    g.bconst = bconst
    P.dma("sp", g.identf, I.ident, writes=[bconst])
    P.op("dve", lambda e: e.tensor_copy(out=g.ident, in_=g.identf), reads=[bconst], writes=[bconst])
    P.op("pool", lambda e: e.memset(g.epsT, EPS), writes=[bconst])
    g.persist_off = g.sb_off

    layer0(g)
    if DEBUG is None:
        g.final.extend(g.out_stores)
    elif not g.final:
        g.final.extend(g.dbg_stores + g.out_stores)

    P.emit(g.final)
    return nc, P


ROWS = ["ln1_g0", "ln1_b0", "ln2_g0", "ln2_b0", "ln1_g1", "ln1_b1", "ln2_g1", "ln2_b1",
        "modb_g1_0", "modb_g2_0", "modb_g1_1", "modb_g2_1"]
NROWS = len(ROWS)
NCOLS = 64
ARENA_BYTES = 212480


def mod_params(g, l):
    nc, P, I, sb = g.nc, g.P, g.I, g.sb
    modA = sb([48, 2], F32)
    G = sb([2, D], F32)
    mark = g.sb_off
    cv = sb([8, 2], F32)
    sc = sb([8, 2], F32)
    screp = sb([8, 128], F32)
    mbc = sb([48], F32)
    wblk = [sb([8, 512], F32) for _ in range(2)]
    grow = sb([2, D], F32)
    b_modA, b_G, b_cv, b_rep = P.buf(), P.buf(), P.buf(), P.buf()
    b_w = [P.buf(), P.buf()]
    b_pm, b_pg = P.buf(), P.buf()
    pM = g.ps[:, 0, 0:96].rearrange("p (j s) -> p j s", s=2)
    P.dma("sp", cv, I.cvec, writes=[b_cv])
    P.dma("sp", mbc, I.modb_col[l], writes=[b_cv])
    r1 = ROWS.index(f"modb_g1_{l}")
    P.dma("sp", grow[:, 0, :], I.rows[r1], writes=[b_cv])
    P.dma("sp", grow[:, 1, :], I.rows[r1 + 1], writes=[b_cv])
    P.op("act", lambda e: e.activation(out=sc, in_=cv, func=AF.Silu), reads=[b_cv], writes=[b_cv])
    for k in range(8):
        P.op("dve", lambda e, k=k: e.tensor_copy(out=screp[:, k, :], in_=sc[:, k, 0:1].to_broadcast([128, 128])),
             reads=[b_cv], writes=[b_rep])
    wv = I.mod_w[l].rearrange("(k p) n -> p k n", p=128)
    for blk in range(12):
        w = wblk[blk % 2]
        bw = b_w[blk % 2]
        P.dma("sp", w, wv[:, :, blk * 512:(blk + 1) * 512], writes=[bw])
        for jj in range(4):
            j = blk * 4 + jj
            for k in range(8):
                P.op("pe", lambda e, k=k, jj=jj, j=j, w=w: e.matmul(
                    pM[:, j, :], lhsT=w[:, k, jj * 128:(jj + 1) * 128], rhs=sc[:, k, :],
                    start=(k == 0), stop=(k == 7)), reads=[bw, b_cv], writes=[b_pm])
        if blk in (4, 5, 10, 11):
            gi = 0 if blk < 6 else 1
            half = blk % 2
            pG = g.ps[:, 1 + half, :]
            for k in range(8):
                P.op("pe", lambda e, k=k, w=w, pG=pG: e.matmul(
                    pG, lhsT=screp[:, k, :], rhs=w[:, k, :], start=(k == 0), stop=(k == 7)),
                    reads=[bw, b_rep], writes=[b_pg])
            P.op("dve", lambda e, gi=gi, half=half, pG=pG: e.tensor_tensor(
                out=G[:, gi, half * 512:(half + 1) * 512], in0=pG, in1=grow[:, gi, half * 512:(half + 1) * 512],
                op=ALU.add), reads=[b_pg, b_cv], writes=[b_G])
    P.op("dve", lambda e: e.tensor_tensor(out=modA, in0=pM, in1=mbc.unsqueeze(2).to_broadcast([128, 48, 2]),
                                          op=ALU.add), reads=[b_pm, b_cv], writes=[b_modA])
    for lo in (8, 32):
        P.op("dve", lambda e, lo=lo: e.tensor_scalar(out=modA[:, lo:lo + 8, :], in0=modA[:, lo:lo + 8, :],
                                                    scalar1=1.0, scalar2=None, op0=ALU.add),
             reads=[b_modA], writes=[b_modA])
    g.sb_off = mark
    P.barrier()
    return modA, G, b_modA, b_G


def ln_to_featmajor(g, src_tiles, ntile, hT, b_hT, scale_cols, bias_cols, b_mod, xt, b_xt, xn, b_xn, tagbufs):
    nc, P = g.nc, g.P
    b_st, b_mv, b_pT = tagbufs
    st, mv, sd = g.ln_st, g.ln_mv, g.ln_sd
    for i in range(ntile):
        for hh in range(2):
            P.op("dve", lambda e, i=i, hh=hh: e.bn_stats(out=st[:, hh, :], in_=xt[i][:, hh * 512:(hh + 1) * 512]),
                 reads=[b_xt[i]], writes=[b_st])
        P.op("dve", lambda e, i=i: e.bn_aggr(out=mv[:, i, :], in_=st), reads=[b_st], writes=[b_mv])
    P.op("act", lambda e: e.activation(out=sd[:, 0:ntile], in_=mv[:, 0:ntile, 1], func=AF.Sqrt,
                                       bias=g.epsT[:, 0:1], scale=1.0), reads=[b_mv, g.bconst], writes=[b_mv])
    P.op("dve", lambda e: e.reciprocal(out=sd[:, 0:ntile], in_=sd[:, 0:ntile]), reads=[b_mv], writes=[b_mv])
    for i in range(ntile):
        P.op("dve", lambda e, i=i: e.tensor_scalar(out=xn[i], in0=xt[i], scalar1=mv[:, i, 0:1],
                                                   scalar2=sd[:, i:i + 1], op0=ALU.subtract, op1=ALU.mult),
             reads=[b_xt[i], b_mv], writes=[b_xn[i]])
    pT = g.ps[:, 0:4, :].bitcast(BF16).rearrange("p b (h n) -> p (b h) n", h=2)
    n = ntile * 128
    for j in range(4):
        for k in (2 * j, 2 * j + 1):
            for i in range(ntile):
                P.op("pe", lambda e, k=k, i=i: e.transpose(out=pT[:, k, i * 128:(i + 1) * 128],
                                                          in_=xn[i][:, k * 128:(k + 1) * 128], identity=g.ident),
                     reads=[b_xn[i], g.bconst], writes=[b_pT[j]])
        for k in (2 * j, 2 * j + 1):
            P.op("act", lambda e, k=k, n=n: e.activation(out=hT[:, k, 0:n], in_=pT[:, k, 0:n], func=AF.Identity,
                                                        scale=scale_cols[k], bias=bias_cols[k]),
                 reads=[b_pT[j], b_mod], writes=[b_hT])


def layer0(g):
    nc, P, I, S, sb = g.nc, g.P, g.I, g.S, g.sb
    g.sb_off = g.persist_off
    modA, G, b_modA, b_G = mod_params(g, 0)
    g.modA0, g.G0, g.b_modA0, g.b_G0 = modA, G, b_modA, b_G
    if DEBUG == "mod":
        d = nc.dram_tensor("dbg_modA", [128, 96], F32, kind="ExternalOutput").ap()
        d2 = nc.dram_tensor("dbg_G", [128, 2 * D], F32, kind="ExternalOutput").ap()
        g.final.append(P.dma("sp", d, modA.rearrange("p j s -> p (j s)"), reads=[b_modA]))
        g.final.append(P.dma("sp", d2, G.rearrange("p a b -> p (a b)"), reads=[b_G]))
        return
    phase_mark = g.sb_off
    NCOLW = 3072
    W = sb([8, NCOLW], BF16)
    b_W = P.buf()
    if DEBUG != "A0":
        P.dma("pool", W[:, :, 0:2048], I.da_w_in.rearrange("(k p) n -> p k n", p=128), writes=[b_W])
        P.dma("pool", W[:, :, 2048:3072], I.da_w_perm.rearrange("(k p) n -> p k n", p=128), writes=[b_W])
    g.ln_st = sb([2, 6], F32)
    g.ln_mv = sb([4, 2], F32)
    g.ln_sd = sb([4], F32)
    xts = [[sb([D], F32) for _ in range(4)] for _ in range(2)]
    b_xts = [[P.buf() for _ in range(4)] for _ in range(2)]
    xn = [sb([D], BF16) for _ in range(4)]
    b_xn = [P.buf() for _ in range(4)]
    hT = sb([8, 512], BF16)
    b_hT = P.buf()
    tag = (P.buf(), P.buf(), g.bk)
    ctab = [sb([512], F32) for _ in range(2)]
    stab = [sb([512], F32) for _ in range(2)]
    b_tab = [P.buf(), P.buf()]
    t1 = sb([512], F32)
    t2 = sb([512], F32)
    b_t1, b_t2 = P.buf(), P.buf()
    qo = [sb([512], BF16) for _ in range(2)]
    b_qo = [P.buf(), P.buf()]
    vo = [sb([512], BF16) for _ in range(2)]
    b_vo = [P.buf(), P.buf()]
    pA, pB = g.ps[:, 4, :], g.ps[:, 5, :]
    b_pA, b_pB = g.bk[4], g.bk[5]
    pV = [g.ps[:, 6, :], g.ps[:, 7, :]]
    b_pV = [g.bk[6], g.bk[7]]
    b_scr = P.buf("scrA")
    g.b_scrA = b_scr
    nblk = 17
    if DEBUG in ("A1", "A0"):
        nblk = 2
    qcnt = 0
    vcnt = 0
    for blk in range(nblk):
        ctxb = (blk == 0)
        ntile = 2 if ctxb else 4
        n = ntile * 128
        par = blk % 2
        src = I.ctx if ctxb else I.x
        t0 = 0 if ctxb else (blk - 1) * 512
        kpos = 0 if ctxb else NCTX + t0
        for i in range(ntile):
            P.dma("sp", xts[par][i], src[t0 + i * 128:t0 + (i + 1) * 128, :], writes=[b_xts[par][i]])
        P.dma("sp", ctab[par][:, 0:n], I.rope_c[:, kpos:kpos + n], writes=[b_tab[par]])
        P.dma("sp", stab[par][:, 0:n], I.rope_s[:, kpos:kpos + n], writes=[b_tab[par]])
        s = 1 if ctxb else 0
        ln_to_featmajor(g, None, ntile, hT, b_hT,
                        [modA[:, 8 + k, s:s + 1] for k in range(8)], [modA[:, k, s:s + 1] for k in range(8)],
                        b_modA, xts[par], b_xts[par], xn, b_xn, tag)
        if S.hT_dbg is not None and not ctxb:
            if not hasattr(g, "hTf"):
                g.hTf = sb([8, 512], F32)
                g.b_hTf = P.buf()
            P.op("dve", lambda e: e.tensor_copy(out=g.hTf, in_=hT), reads=[b_hT], writes=[g.b_hTf])
            g.final.append(P.dma("sp", S.hT_dbg[:, :, t0:t0 + n], g.hTf[:, :, 0:n], reads=[g.b_hTf]))
        if DEBUG == "A0":
            continue
        for which in ((1,) if ctxb else (0, 1)):
            for h in range(4):
                c0 = which * 512 + h * 128
                c1 = 2048 + which * 512 + h * 128
                for k in range(8):
                    P.op("pe", lambda e, k=k, c0=c0, n=n: e.matmul(pA[:, 0:n], lhsT=W[:, k, c0:c0 + 128], rhs=hT[:, k, 0:n],
                                                               start=(k == 0), stop=(k == 7)),
                         reads=[b_W, b_hT], writes=[b_pA])
                for k in range(8):
                    P.op("pe", lambda e, k=k, c1=c1, n=n: e.matmul(pB[:, 0:n], lhsT=W[:, k, c1:c1 + 128], rhs=hT[:, k, 0:n],
                                                               start=(k == 0), stop=(k == 7)),
                         reads=[b_W, b_hT], writes=[b_pB])
                P.op("dve", lambda e, n=n, par=par: e.tensor_tensor(out=t1[:, 0:n], in0=pA[:, 0:n], in1=ctab[par][:, 0:n], op=ALU.mult),
                     reads=[b_pA, b_tab[par]], writes=[b_t1])
                P.op("dve", lambda e, n=n, par=par: e.tensor_tensor(out=t2[:, 0:n], in0=pB[:, 0:n], in1=stab[par][:, 0:n], op=ALU.mult),
                     reads=[b_pB, b_tab[par]], writes=[b_t2])
                qb = qcnt % 2
                qcnt += 1
                P.op("pool", lambda e, n=n, qb=qb: e.tensor_tensor(out=qo[qb][:, 0:n], in0=t1[:, 0:n], in1=t2[:, 0:n], op=ALU.add),
                     reads=[b_t1, b_t2], writes=[b_qo[qb]])
                dst = S.QT[h][:, t0:t0 + n] if which == 0 else S.KT[h][:, kpos:kpos + n]
                g.dbg_stores.append(P.dma("pool", dst, qo[qb][:, 0:n], reads=[b_qo[qb]], writes=[b_scr]))
        for i in range(ntile):
            for which in ((0,) if ctxb else (0, 1)):
                c0 = 1024 + which * 512
                vb = vcnt % 2
                vcnt += 1
                for k in range(8):
                    P.op("pe", lambda e, k=k, i=i, c0=c0, vb=vb: e.matmul(pV[vb], lhsT=hT[:, k, i * 128:(i + 1) * 128],
                                                                     rhs=W[:, k, c0:c0 + 512], start=(k == 0), stop=(k == 7)),
                         reads=[b_W, b_hT], writes=[b_pV[vb]])
                P.op("act", lambda e, vb=vb: e.activation(out=vo[vb], in_=pV[vb], func=AF.Copy),
                     reads=[b_pV[vb]], writes=[b_vo[vb]])
                if which == 0:
                    dst = S.V[kpos + i * 128:kpos + (i + 1) * 128, :]
                else:
                    dst = S.F[t0 + i * 128:t0 + (i + 1) * 128, :]
                g.dbg_stores.append(P.dma("pool", dst, vo[vb], reads=[b_vo[vb]], writes=[b_scr]))
    if DEBUG in ("A1", "A0"):
        g.final.extend(g.dbg_stores)
        return
    g.sb_off = phase_mark
    P.barrier()
    attention(g)


def attention(g):
    nc, P, I, S, sb = g.nc, g.P, g.I, g.S, g.sb
    mark = g.sb_off
    KT = sb([NK], BF16)
    Vh = sb([66, 128], BF16)
    b_KT, b_Vh = P.buf(), P.buf()
    QTb = [sb([512], BF16) for _ in range(2)]
    b_Q = [P.buf(), P.buf()]
    Pm = [[sb([512], BF16) for _ in range(2)] for _ in range(2)]
    b_Pm = [[P.buf(), P.buf()], [P.buf(), P.buf()]]
    ones = sb([128], BF16)
    onesf = sb([128], F32)
    lamt = sb([4, 64], F32)
    lt = sb([2, 64], F32)
    ls = sb([2], F32)
    nlam = sb([1], F32)
    gp = sb([1], F32)
    b_c = P.buf()
    P.op("pool", lambda e: e.memset(ones, 1.0), writes=[b_c])
    P.op("pool", lambda e: e.memset(onesf, 1.0 / 128.0), writes=[b_c])
    P.dma("sp", lamt, I.lam, writes=[b_c])
    P.dma("sp", gp, I.cols[0], writes=[b_c])
    P.op("dve", lambda e: e.tensor_tensor(out=lt, in0=lamt[:, 0:4:2, :], in1=lamt[:, 1:4:2, :], op=ALU.mult),
         reads=[b_c], writes=[b_c])
    P.op("dve", lambda e: e.tensor_reduce(out=ls, in_=lt, axis=mybir.AxisListType.X, op=ALU.add), reads=[b_c], writes=[b_c])
    P.op("act", lambda e: e.activation(out=ls, in_=ls, func=AF.Exp), reads=[b_c], writes=[b_c])
    P.op("dve", lambda e: e.tensor_tensor(out=nlam, in0=ls[:, 1:2], in1=ls[:, 0:1], op=ALU.subtract), reads=[b_c], writes=[b_c])
    P.op("dve", lambda e: e.tensor_scalar(out=nlam, in0=nlam, scalar1=-(0.8 - 0.6), scalar2=None, op0=ALU.add), reads=[b_c], writes=[b_c])
    P.op("dve", lambda e: e.tensor_scalar(out=gp, in0=gp, scalar1=1.0 - (0.8 - 0.6), scalar2=None, op0=ALU.mult), reads=[b_c], writes=[b_c])
    r1 = sb([512], F32)
    o1 = sb([512], F32)
    o2 = sb([512], F32)
    sq = sb([512], F32)
    rs = sb([512], F32)
    ob = [sb([512], BF16) for _ in range(2)]
    b_r1, b_o1, b_o2, b_sq, b_rs = P.buf(), P.buf(), P.buf(), P.buf(), P.buf()
    b_ob = [P.buf(), P.buf()]
    bk = [P.buf() for _ in range(8)]
    ps = g.ps
    heads = range(4)
    qbs = range(16)
    if DEBUG == "B1":
        heads, qbs = [0], [0]
    if DEBUG in ("C1", "D1", "H1", "H2", "H3"):
        heads = []
    cnt = 0
    for h in heads:
        P.dma("sp", KT, S.KT[h], reads=[g.b_scrA], writes=[b_KT])
        P.dma("sp", Vh, S.V[:, h * 128:(h + 1) * 128].rearrange("(t p) e -> p t e", p=128), reads=[g.b_scrA], writes=[b_Vh])
        for qb in qbs:
            Q = QTb[cnt % 2]
            bq = b_Q[cnt % 2]
            obuf = ob[cnt % 2]
            b_obuf = b_ob[cnt % 2]
            cnt += 1
            P.dma("sp", Q, S.QT[h][:, qb * 512:(qb + 1) * 512], reads=[g.b_scrA], writes=[bq])
            def s_ops(kt):
                for c in range(2):
                    sbk = c * 2 + kt % 2
                    P.op("pe", lambda e, c=c, kt=kt, sbk=sbk, Q=Q: e.matmul(
                        ps[:, sbk, :], lhsT=KT[c * 64:(c + 1) * 64, kt * 128:(kt + 1) * 128], rhs=Q[c * 64:(c + 1) * 64, :],
                        start=True, stop=True), reads=[b_KT, bq], writes=[bk[sbk]])

            def e_ops(kt):
                for c in range(2):
                    sbk = c * 2 + kt % 2
                    pm = Pm[c][kt % 2]
                    P.op("act", lambda e, sbk=sbk, pm=pm: e.activation(out=pm, in_=ps[:, sbk, :], func=AF.Exp, scale=0.125),
                         reads=[bk[sbk]], writes=[b_Pm[c][kt % 2]])

            def pv_ops(kt):
                for c in range(2):
                    pm = Pm[c][kt % 2]
                    bpm = b_Pm[c][kt % 2]
                    P.op("pe", lambda e, c=c, kt=kt, pm=pm: e.matmul(ps[:, 4 + 2 * c, :], lhsT=Vh[:, kt, :], rhs=pm,
                                                                  start=(kt == 0), stop=(kt == 65)),
                         reads=[b_Vh, bpm], writes=[bk[4 + 2 * c]])
                    P.op("pe", lambda e, c=c, kt=kt, pm=pm: e.matmul(ps[:, 5 + 2 * c, :], lhsT=ones, rhs=pm,
                                                                  start=(kt == 0), stop=(kt == 65)),
                         reads=[b_c, bpm], writes=[bk[5 + 2 * c]])
            s_ops(0)
            for kt in range(66):
                if kt + 1 < 66:
                    s_ops(kt + 1)
                e_ops(kt)
                pv_ops(kt)
            P.op("dve", lambda e: e.reciprocal(out=r1, in_=ps[:, 5, :]), reads=[bk[5]], writes=[b_r1])
            P.op("dve", lambda e: e.tensor_tensor(out=o1, in0=ps[:, 4, :], in1=r1, op=ALU.mult), reads=[bk[4], b_r1], writes=[b_o1])
            P.op("dve", lambda e: e.reciprocal(out=r1, in_=ps[:, 7, :]), reads=[bk[7], b_r1], writes=[b_r1])
            P.op("dve", lambda e: e.tensor_tensor(out=o2, in0=ps[:, 6, :], in1=r1, op=ALU.mult), reads=[bk[6], b_r1], writes=[b_o2])
            P.op("dve", lambda e: e.scalar_tensor_tensor(out=o1, in0=o2, scalar=nlam[:, 0:1], in1=o1, op0=ALU.mult, op1=ALU.add),
                 reads=[b_o2, b_o1, b_c], writes=[b_o1])
            P.op("pool", lambda e: e.tensor_tensor(out=sq, in0=o1, in1=o1, op=ALU.mult), reads=[b_o1], writes=[b_sq])
            P.op("pe", lambda e: e.matmul(ps[:, 0, :], lhsT=onesf, rhs=sq, start=True, stop=True), reads=[b_sq, b_c], writes=[bk[0]])
            P.op("act", lambda e: e.activation(out=rs, in_=ps[:, 0, :], func=AF.Ln, bias=g.epsT[:, 0:1], scale=1.0),
                 reads=[bk[0], g.bconst], writes=[b_rs])
            P.op("act", lambda e: e.activation(out=rs, in_=rs, func=AF.Exp, scale=-0.5), reads=[b_rs], writes=[b_rs])
            P.op("dve", lambda e, obuf=obuf: e.scalar_tensor_tensor(out=obuf, in0=o1, scalar=gp[:, 0:1], in1=rs, op0=ALU.mult, op1=ALU.mult),
                 reads=[b_o1, b_rs, b_c], writes=[b_obuf])
            g.dbg_stores.append(P.dma("pool", S.mixT[h][:, qb * 512:(qb + 1) * 512], obuf, reads=[b_obuf], writes=[g.b_scrB]))
    if DEBUG == "B1":
        g.final.extend(g.dbg_stores)
        return
    g.sb_off = mark
    P.barrier()
    fourier(g)


def fourier(g):
    nc, P, I, S, sb = g.nc, g.P, g.I, g.S, g.sb
    mark = g.sb_off
    F64 = sb([128], BF16)
    MT = sb([64, 2, 256], BF16)
    CS = sb([2, 128], BF16)
    b_t = P.buf()
    P.dma("pool", F64[0:64, :], I.f64, writes=[b_t])
    P.dma("pool", MT, I.fmt, writes=[b_t])
    P.dma("pool", CS, I.fcs, writes=[b_t])
    G = sb([128, 128], BF16)
    Y = sb([2, 64, 128], BF16)
    X = sb([64, 256], BF16)
    fm = sb([T], BF16)
    b_G, b_Y, b_X, b_fm = P.buf(), P.buf(), P.buf(), P.buf()
    bk = [P.buf() for _ in range(8)]
    ps = g.ps
    groups = range(4)
    if DEBUG == "C1":
        groups = [0]
    if DEBUG in ("D1", "H1", "H2", "H3"):
        groups = []
    for gi in groups:
        P.dma("sp", G[0:64], S.F[:, gi * 128:(gi + 1) * 128].rearrange("(a p) c -> a p c", p=128),
              reads=[g.b_scrA], writes=[b_G])
        for c0 in range(0, 128, 4):
            b = (c0 // 4) % 2
            for cc in range(4):
                P.op("pe", lambda e, c=c0 + cc, cc=cc, b=b: e.matmul(ps[:, b, cc * 128:(cc + 1) * 128], lhsT=G[0:64, :, c],
                                                                 rhs=F64[0:64, :], start=True, stop=True),
                     reads=[b_G, b_t], writes=[bk[b]])
            P.op("act", lambda e, c0=c0, b=b: e.activation(
                out=Y[:, :, :, c0:c0 + 4], in_=ps[:, b, :].rearrange("p (cc r k) -> p r k cc", cc=4, r=2),
                func=AF.Copy), reads=[bk[b]], writes=[b_Y])
        for k1 in range(64):
            b = 2 + (k1 // 2) % 2
            o = (k1 % 2) * 256
            P.op("pe", lambda e, k1=k1, b=b, o=o: e.matmul(ps[:, b, o:o + 256], lhsT=Y[:, 0, k1, :], rhs=MT[:, k1, 0, :],
                                                       start=True, stop=False), reads=[b_Y, b_t], writes=[bk[b]])
            P.op("pe", lambda e, k1=k1, b=b, o=o: e.matmul(ps[:, b, o:o + 256], lhsT=Y[:, 1, k1, :], rhs=MT[:, k1, 1, :],
                                                       start=False, stop=True), reads=[b_Y, b_t], writes=[bk[b]])
            if k1 % 2 == 1:
                P.op("dve", lambda e, k1=k1, b=b: e.tensor_copy(out=X[:, k1 - 1:k1 + 1, :],
                                                              in_=ps[:, b, :].rearrange("p (a x) -> p a x", a=2)),
                     reads=[bk[b]], writes=[b_X])
        fmv = fm.rearrange("p (k2 k1) -> p k1 k2", k1=64)
        for q in range(16):
            b = 4 + q % 2
            P.op("pe", lambda e, q=q, b=b: e.matmul(ps[:, b, :], lhsT=CS[:, 0, :], rhs=X[:, 4 * q:4 * q + 4, 0:128],
                                                start=True, stop=False), reads=[b_X, b_t], writes=[bk[b]])
            P.op("pe", lambda e, q=q, b=b: e.matmul(ps[:, b, :], lhsT=CS[:, 1, :], rhs=X[:, 4 * q:4 * q + 4, 128:256],
                                                start=False, stop=True), reads=[b_X, b_t], writes=[bk[b]])
            P.op("act", lambda e, q=q, b=b: e.activation(out=fmv[:, 4 * q:4 * q + 4, :],
                                                     in_=ps[:, b, :].rearrange("p (a x) -> p a x", a=4), func=AF.Copy),
                 reads=[bk[b]], writes=[b_fm])
        g.dbg_stores.append(P.dma("pool", S.mixT[4 + gi], fm, reads=[b_fm], writes=[g.b_scrB]))
    if DEBUG == "C1":
        g.final.extend(g.dbg_stores)
        return
    g.sb_off = mark
    P.barrier()
    phase_D(g, 0, I.x, I.da_w_out, g.modA0, g.G0, g.b_modA0, g.b_G0)
    phase_E(g, 0, S.x2, g.modA0, g.G0, g.b_modA0, g.b_G0)
    if DEBUG in ("D1",):
        return
    layer1(g)


def fm_to_tok(g, src, b_src, dst_rows, tb, b_tb, bank, b_scr):
    P = g.P
    pT = g.ps[:, bank, :].bitcast(BF16)[:, 0:512].rearrange("p (j c) -> p j c", c=128)
    for j in range(4):
        P.op("pe", lambda e, j=j: e.transpose(out=pT[:, j, :], in_=src[:, j * 128:(j + 1) * 128], identity=g.ident),
             reads=[b_src, g.bconst], writes=[g.bk[bank]])
    P.op("act", lambda e: e.activation(out=tb, in_=pT, func=AF.Copy), reads=[g.bk[bank]], writes=[b_tb])
    for j in range(4):
        g.dbg_stores.append(P.dma("pool", dst_rows[j], tb[:, j, :], reads=[b_tb], writes=[b_scr]))


def layer1(g):
    nc, P, I, S, sb = g.nc, g.P, g.I, g.S, g.sb
    g.sb_off = g.persist_off
    P.barrier()
    modA, G, b_modA, b_G = mod_params(g, 1)
    g.knorm = sb([8], F32)
    g.b_knorm = P.buf()
    mark = g.sb_off
    g.ln_st = sb([2, 6], F32)
    g.ln_mv = sb([4, 2], F32)
    g.ln_sd = sb([4], F32)
    xts = [sb([D], F32) for _ in range(4)]
    b_xts = [P.buf() for _ in range(4)]
    xn = [sb([D], BF16) for _ in range(4)]
    b_xn = [P.buf() for _ in range(4)]
    hT = sb([8, 512], BF16)
    b_hT = P.buf()
    nblk = 16 if DEBUG not in ("H1", "H2", "H3") else 1
    g.b_scrH = P.buf()
    for blk in range(nblk):
        t0 = blk * 512
        for i in range(4):
            P.dma("sp", xts[i], S.x2[t0 + i * 128:t0 + (i + 1) * 128, :], reads=[g.b_scrX], writes=[b_xts[i]])
        ln_to_featmajor(g, None, 4, hT, b_hT, [modA[:, 8 + k, 0:1] for k in range(8)], [modA[:, k, 0:1] for k in range(8)],
                        b_modA, xts, b_xts, xn, b_xn, (P.buf(), P.buf(), g.bk))
        P.dma("pool", S.h2T[:, :, 1 + t0:1 + t0 + 512].rearrange("k p n -> p k n"), hT, reads=[b_hT], writes=[g.b_scrH])
    g.sb_off = mark
    P.barrier()
    W = sb([8, 3 * D], BF16)
    cw = sb([24, 4], F32)
    b_w, b_rows = P.buf(), P.buf()
    for k in range(8):
        P.dma("pool", W[:, k, :], I.hy_w_in[k * 128:(k + 1) * 128, :], writes=[b_w])
    P.dma("sp", cw, I.hy_cw, writes=[b_rows])
    hb = sb([8, 514], BF16)
    b_hb = P.buf()
    cb = [sb([512], F32) for _ in range(3)]
    b_cb = [P.buf() for _ in range(3)]
    x0b = sb([512], BF16)
    vvb = sb([512], BF16)
    b_x0b, b_vvb = P.buf(), P.buf()
    tb = sb([4, 128], BF16)
    b_tb = P.buf()
    ps, bk = g.ps, g.bk
    for blk in range(nblk):
        t0 = blk * 512
        P.dma("sp", hb, S.h2T[:, :, t0:t0 + 514].rearrange("k p n -> p k n"), reads=[g.b_scrH], writes=[b_hb])
        for i in range(8):
            for part in range(3):
                ch = part * 8 + i
                ub = (3 * i + part) % 2
                hcol = (3 * i + part) % 64
                for k in range(8):
                    P.op("pe", lambda e, k=k, ch=ch, ub=ub: e.matmul(ps[:, ub, :], lhsT=W[:, k, ch * 128:(ch + 1) * 128],
                                                                 rhs=hb[:, k, 1:513], start=(k == 0), stop=(k == 7)),
                         reads=[b_w, b_hb], writes=[bk[ub]])
                for k in range(8):
                    P.op("pe", lambda e, k=k, ch=ch, hcol=hcol: e.matmul(ps[:, 2, 2 * hcol:2 * hcol + 2], lhsT=W[:, k, ch * 128:(ch + 1) * 128],
                                                                     rhs=hb[:, k, 0:514:513], start=(k == 0), stop=(k == 7)),
                         reads=[b_w, b_hb], writes=[bk[2]])
                c = cb[part]
                bc = b_cb[part]
                uA = ps[:, ub, :]
                uB = ps[:, 2, 2 * hcol:2 * hcol + 2]
                P.op("act", lambda e, c=c, uA=uA, ch=ch: e.activation(out=c, in_=uA, func=AF.Identity, scale=cw[:, ch, 1:2], bias=cw[:, ch, 3:4]),
                     reads=[bk[ub], b_rows], writes=[bc])
                P.op("dve", lambda e, c=c, uA=uA, ch=ch: e.scalar_tensor_tensor(out=c[:, 1:512], in0=uA[:, 0:511], scalar=cw[:, ch, 0:1], in1=c[:, 1:512],
                                                                          op0=ALU.mult, op1=ALU.add), reads=[bk[ub], b_rows, bc], writes=[bc])
                P.op("dve", lambda e, c=c, uA=uA, ch=ch: e.scalar_tensor_tensor(out=c[:, 0:511], in0=uA[:, 1:512], scalar=cw[:, ch, 2:3], in1=c[:, 0:511],
                                                                          op0=ALU.mult, op1=ALU.add), reads=[bk[ub], b_rows, bc], writes=[bc])
                P.op("dve", lambda e, c=c, uB=uB, ch=ch: e.scalar_tensor_tensor(out=c[:, 0:1], in0=uB[:, 0:1], scalar=cw[:, ch, 0:1], in1=c[:, 0:1],
                                                                          op0=ALU.mult, op1=ALU.add), reads=[bk[2], b_rows, bc], writes=[bc])
                P.op("dve", lambda e, c=c, uB=uB, ch=ch: e.scalar_tensor_tensor(out=c[:, 511:512], in0=uB[:, 1:2], scalar=cw[:, ch, 2:3], in1=c[:, 511:512],
                                                                          op0=ALU.mult, op1=ALU.add), reads=[bk[2], b_rows, bc], writes=[bc])
            P.op("act", lambda e: e.activation(out=x0b, in_=cb[0], func=AF.Copy), reads=[b_cb[0]], writes=[b_x0b])
            P.op("pool", lambda e: e.tensor_tensor(out=vvb, in0=cb[2], in1=cb[1], op=ALU.mult), reads=[b_cb[2], b_cb[1]], writes=[b_vvb])
            g.dbg_stores.append(P.dma("pool", S.x0T[i][:, t0:t0 + 512], x0b, reads=[b_x0b], writes=[g.b_scrH]))
            g.dbg_stores.append(P.dma("pool", S.vT[i][:, t0:t0 + 512], vvb, reads=[b_vvb], writes=[g.b_scrH]))
            fm_to_tok(g, vvb, b_vvb, [S.vtok[t0 + j * 128:t0 + (j + 1) * 128, i * 128:(i + 1) * 128] for j in range(4)],
                      tb, b_tb, 3, g.b_scrH)
    if DEBUG == "H3":
        zt_ = sb([T - 512], BF16)
        b_z_ = P.buf()
        P.op("pool", lambda e: e.memset(zt_, 0.0), writes=[b_z_])
        for r in range(4, 64):
            P.dma("pool", S.vtok[r * 128:(r + 1) * 128, :], zt_[:, 0:1024], reads=[b_z_], writes=[g.b_scrH])
        P.dma("pool", S.x0T[0][:, 512:T], zt_, reads=[b_z_], writes=[g.b_scrH])
        P.dma("pool", S.vT[0][:, 512:T], zt_, reads=[b_z_], writes=[g.b_scrH])
    g.sb_off = mark
    P.barrier()
    if DEBUG == "H1":
        return
    hyena_filter(g)
    if DEBUG == "H2":
        return
    hyena_conv(g)
    if DEBUG == "H3":
        return
    phase_D(g, 1, S.x2, I.hy_w_out, modA, G, b_modA, b_G)
    phase_E(g, 1, g.out, modA, G, b_modA, b_G)


def dft16k(g, planes, A, consumer, groups, b_src):
    nc, P, I, S, sb = g.nc, g.P, g.I, g.S, g.sb
    npl = len(planes)
    G = [sb([128, 128], BF16) for _ in range(npl)]
    Y = sb([2, 128, 128], BF16)
    mt = [sb([2, 256], BF16) for _ in range(4)]
    b_G, b_Y = P.buf(), P.buf()
    b_mt = [P.buf() for _ in range(4)]
    ps, bk = g.ps, g.bk
    for gi in groups:
        for pl in range(npl):
            v = planes[pl][:, gi * 128:(gi + 1) * 128].rearrange("(a p) c -> a p c", p=128)
            for q in range(4):
                P.dma("sp", G[pl][0:A, q * 32:(q + 1) * 32, :], v[:, q * 32:(q + 1) * 32, :], reads=[b_src], writes=[b_G])
        for c0 in range(0, 128, 2):
            b = (c0 // 2) % 2
            for cc in range(2):
                for pl in range(npl):
                    P.op("pe", lambda e, c=c0 + cc, cc=cc, b=b, pl=pl: e.matmul(
                        ps[:, b, cc * 256:(cc + 1) * 256], lhsT=G[pl][0:A, :, c], rhs=g.F1[0:A, pl, :],
                        start=(pl == 0), stop=(pl == npl - 1)), reads=[b_G, g.b_dft], writes=[bk[b]])
            P.op("act", lambda e, c0=c0, b=b: e.activation(
                out=Y[:, :, :, c0:c0 + 2], in_=ps[:, b, :].rearrange("p (cc r k) -> p r k cc", cc=2, r=2),
                func=AF.Copy), reads=[bk[b]], writes=[b_Y])
        for k1 in range(128):
            par = k1 % 4
            b = 2 + k1 % 2
            P.dma("sp", mt[par], S.MTb[k1].rearrange("p (v n) -> p v n", v=2), reads=[g.b_dft], writes=[b_mt[par]])
            P.op("pe", lambda e, k1=k1, b=b, par=par: e.matmul(ps[:, b, 0:256], lhsT=Y[:, 0, k1, :], rhs=mt[par][:, 0, :],
                                                           start=True, stop=False), reads=[b_Y, b_mt[par]], writes=[bk[b]])
            P.op("pe", lambda e, k1=k1, b=b, par=par: e.matmul(ps[:, b, 0:256], lhsT=Y[:, 1, k1, :], rhs=mt[par][:, 1, :],
                                                           start=False, stop=True), reads=[b_Y, b_mt[par]], writes=[bk[b]])
            consumer(gi, k1, ps[:, b, 0:256], b)


def hyena_conv(g):
    nc, P, I, S, sb = g.nc, g.P, g.I, g.S, g.sb
    ps, bk = g.ps, g.bk
    base = g.sb_off
    groups = range(8) if DEBUG != "H3" else [0]
    g.F1 = sb([2, 256], BF16)
    g.b_dft = P.buf()
    P.dma("pool", g.F1, I.dft_f1, writes=[g.b_dft])
    mark = g.sb_off
    mf = [sb([512], F32) for _ in range(2)]
    mb = [sb([512], BF16) for _ in range(2)]
    b_mf, b_mb = [P.buf(), P.buf()], [P.buf(), P.buf()]
    for k1 in range(128):
        par = k1 % 2
        P.dma("sp", mf[par], I.dft_mt[k1], writes=[b_mf[par]])
        P.op("act", lambda e, par=par: e.activation(out=mb[par], in_=mf[par], func=AF.Copy), reads=[b_mf[par]], writes=[b_mb[par]])
        P.dma("pool", S.MTb[k1], mb[par], reads=[b_mb[par]], writes=[g.b_dft])
    g.sb_off = mark
    P.barrier()
    kt = [sb([256], BF16) for _ in range(2)]
    b_kt = [P.buf(), P.buf()]
    g.b_scrKS = P.buf()

    def cons_filter(gi, k1, pz, bank):
        par = k1 % 2
        P.op("act", lambda e: e.activation(out=kt[par], in_=pz, func=AF.Copy), reads=[bk[bank]], writes=[b_kt[par]])
        P.dma("pool", S.KS[gi][:, k1, :], kt[par], reads=[b_kt[par]], writes=[g.b_scrKS])
    dft16k(g, [S.ktok], 128, cons_filter, groups, g.b_scrK)
    g.sb_off = mark
    P.barrier()
    ksb = [sb([256], BF16) for _ in range(4)]
    b_ksb = [P.buf() for _ in range(4)]
    ta, tb_, tc, td = sb([128], F32), sb([128], F32), sb([128], F32), sb([128], F32)
    b_ta, b_tb2, b_tc, b_td = P.buf(), P.buf(), P.buf(), P.buf()
    zz = [sb([2, 128], BF16) for _ in range(2)]
    b_zz = [P.buf(), P.buf()]
    zt = [sb([2, 128], BF16) for _ in range(2)]
    b_zt = [P.buf(), P.buf()]
    g.b_scrZ = P.buf()

    def cons_signal(gi, k1, pz, bank):
        par = k1 % 2
        K = ksb[k1 % 4]
        b_K = b_ksb[k1 % 4]
        P.dma("sp", K, S.KS[gi][:, k1, :], reads=[g.b_scrKS], writes=[b_K])
        xr, xi = pz[:, 0:128], pz[:, 128:256]
        kr, ki = K[:, 0:128], K[:, 128:256]
        P.op("dve", lambda e: e.tensor_tensor(out=ta, in0=xr, in1=kr, op=ALU.mult), reads=[bk[bank], b_K], writes=[b_ta])
        P.op("dve", lambda e: e.tensor_tensor(out=tb_, in0=xi, in1=ki, op=ALU.mult), reads=[bk[bank], b_K], writes=[b_tb2])
        P.op("dve", lambda e: e.tensor_tensor(out=tc, in0=xr, in1=ki, op=ALU.mult), reads=[bk[bank], b_K], writes=[b_tc])
        P.op("dve", lambda e: e.tensor_tensor(out=td, in0=xi, in1=kr, op=ALU.mult), reads=[bk[bank], b_K], writes=[b_td])
        Z = zz[par]
        P.op("pool", lambda e: e.tensor_tensor(out=Z[:, 0, :], in0=ta, in1=tb_, op=ALU.subtract), reads=[b_ta, b_tb2], writes=[b_zz[par]])
        P.op("dve", lambda e: e.scalar_tensor_tensor(out=Z[:, 1, :], in0=tc, scalar=-1.0, in1=td, op0=ALU.mult, op1=ALU.subtract),
             reads=[b_tc, b_td], writes=[b_zz[par]])
        tbank = 4 + par
        pT = ps[:, tbank, :].bitcast(BF16)[:, 0:256].rearrange("p (v c) -> p v c", v=2)
        for v in range(2):
            P.op("pe", lambda e, v=v: e.transpose(out=pT[:, v, :], in_=Z[:, v, :], identity=g.ident),
                 reads=[b_zz[par], g.bconst], writes=[bk[tbank]])
        P.op("act", lambda e: e.activation(out=zt[par], in_=pT, func=AF.Copy), reads=[bk[tbank]], writes=[b_zt[par]])
        for v in range(2):
            dst = S.Ztok[v][:, gi * 128:(gi + 1) * 128].rearrange("(k2 k1) c -> k1 k2 c", k1=128)[k1]
            P.dma("pool", dst, zt[par][:, v, :], reads=[b_zt[par]], writes=[g.b_scrZ])
    dft16k(g, [S.vtok], 64, cons_signal, groups, g.b_scrH)
    g.sb_off = mark
    P.barrier()
    yt = sb([T], F32)
    vt = sb([T], BF16)
    x0t = sb([T], BF16)
    ot = vt
    hc = sb([8, 2], F32)
    rk = sb([8], F32)
    b_yt, b_vt, b_ot, b_c = P.buf(), P.buf(), P.buf(), P.buf()
    P.dma("sp", hc, I.hy_cols, writes=[b_c])
    P.op("dve", lambda e: e.reciprocal(out=rk, in_=g.knorm), reads=[g.b_knorm], writes=[b_c])
    P.op("dve", lambda e: e.tensor_scalar(out=rk, in0=rk, scalar1=1.0 / 16384.0, scalar2=None, op0=ALU.mult), reads=[b_c], writes=[b_c])
    ytv = yt.rearrange("p (n2 n1) -> p n1 n2", n1=128)

    def cons_inv(gi, k1, pz, bank):
        P.op("act", lambda e: e.activation(out=ytv[:, k1, :], in_=pz[:, 0:64], func=AF.Copy), reads=[bk[bank]], writes=[b_yt])
        if k1 == 127:
            P.dma("sp", vt, S.vT[gi], reads=[g.b_scrH], writes=[b_vt])
            P.dma("sp", x0t, S.x0T[gi], reads=[g.b_scrH], writes=[b_vt])
            if S.y_dbg is not None and gi == 0:
                g.dbg_stores.append(P.dma("sp", S.y_dbg, yt, reads=[b_yt]))
            P.op("dve", lambda e: e.tensor_scalar(out=yt, in0=yt, scalar1=rk[:, gi:gi + 1], scalar2=None, op0=ALU.mult),
                 reads=[b_yt, b_c], writes=[b_yt])
            P.op("dve", lambda e: e.scalar_tensor_tensor(out=yt, in0=vt, scalar=hc[:, gi, 0:1], in1=yt, op0=ALU.mult, op1=ALU.add),
                 reads=[b_vt, b_yt, b_c], writes=[b_yt])
            P.op("dve", lambda e: e.tensor_tensor(out=ot, in0=yt, in1=x0t, op=ALU.mult), reads=[b_yt, b_vt], writes=[b_vt])
            g.dbg_stores.append(P.dma("pool", S.mixT[gi], ot, reads=[b_vt], writes=[g.b_scrB]))
    dft16k(g, [S.Ztok[0], S.Ztok[1]], 128, cons_inv, groups, g.b_scrZ)
    g.sb_off = base
    P.barrier()


def hyena_filter(g):
    nc, P, I, S, sb = g.nc, g.P, g.I, g.S, g.sb
    mark = g.sb_off
    w1 = sb([64], F32)
    w23 = sb([2, 64], F32)
    w4 = sb([2 * D], F32)
    fc = sb([4], F32)
    sc = sb([4], F32)
    hc = sb([8, 2], F32)
    b_w = P.buf()
    P.dma("sp", w1[0:33, :], I.hyf_w1, writes=[b_w])
    P.dma("sp", w23[0:64], I.hyf_w23.rearrange("l k n -> k l n"), writes=[b_w])
    P.dma("sp", w4[0:64, :], I.hyf_w4, writes=[b_w])
    P.dma("sp", fc[0:64, :], I.hyf_cols, writes=[b_w])
    P.dma("sp", hc, I.hy_cols, writes=[b_w])
    P.op("dve", lambda e: e.tensor_scalar(out=sc[0:64, 0:1], in0=fc[0:64, 0:1], scalar1=1.0 / 3.0, scalar2=None, op0=ALU.mult),
         reads=[b_w], writes=[b_w])
    P.op("dve", lambda e: e.tensor_scalar(out=sc[0:64, 1:4], in0=fc[0:64, 1:4], scalar1=sc[0:64, 0:1], scalar2=None, op0=ALU.mult),
         reads=[b_w], writes=[b_w])
    P.op("pool", lambda e: e.memset(g.knorm, 0.0), writes=[g.b_knorm])
    zt = [sb([512], F32) for _ in range(2)]
    tv = [sb([512], F32) for _ in range(2)]
    b_zt = [P.buf(), P.buf()]
    hs = sb([512], F32)
    ht = sb([512], F32)
    hx = sb([512], F32)
    b_hs, b_ht, b_hx = P.buf(), P.buf(), P.buf()
    dec = sb([512], F32)
    kr = sb([512], F32)
    krb = sb([512], BF16)
    part = sb([1], F32)
    b_dec, b_kr, b_krb, b_part = P.buf(), P.buf(), P.buf(), P.buf()
    tb = sb([4, 128], BF16)
    b_tb = P.buf()
    ps, bk = g.ps, g.bk
    g.b_scrK = P.buf()
    it = 0
    for dr in range(2):
        for blk in range(16):
            n0 = blk * 512
            par = it % 2
            it += 1
            P.dma("sp", zt[par][0:33, :], I.hyf_z[dr][:, n0:n0 + 512], writes=[b_zt[par]])
            P.dma("sp", tv[par], I.hyf_t[dr][:, n0:n0 + 512], writes=[b_zt[par]])
            src, b_src = zt[par], b_zt[par]
            kdim = 33
            for layer in range(3):
                lw = w1[0:33, :] if layer == 0 else w23[0:64, layer - 1, :]
                P.op("pe", lambda e, lw=lw, src=src, kdim=kdim: e.matmul(ps[0:64, 0, :], lhsT=lw, rhs=src[0:kdim, :], start=True, stop=True),
                     reads=[b_w, b_src], writes=[bk[0]])
                P.op("act", lambda e, layer=layer: e.activation(out=hs[0:64, :], in_=ps[0:64, 0, :], func=AF.Sin,
                                                              scale=sc[0:64, 0:1], bias=sc[0:64, 1 + layer:2 + layer]),
                     reads=[bk[0], b_w], writes=[b_hs])
                P.op("dve", lambda e: e.tensor_tensor(out=ht[0:64, :], in0=hs[0:64, :], in1=hs[0:64, :], op=ALU.mult), reads=[b_hs], writes=[b_ht])
                P.op("dve", lambda e: e.tensor_scalar(out=ht[0:64, :], in0=ht[0:64, :], scalar1=-4.0, scalar2=3.0, op0=ALU.mult, op1=ALU.add),
                     reads=[b_ht], writes=[b_ht])
                P.op("dve", lambda e: e.tensor_tensor(out=hx[0:64, :], in0=hs[0:64, :], in1=ht[0:64, :], op=ALU.mult), reads=[b_hs, b_ht], writes=[b_hx])
                src, b_src, kdim = hx, b_hx, 64
            for ci in range(8):
                bank = 1 + ci % 2
                c0 = dr * D + ci * 128
                P.op("pe", lambda e, c0=c0, bank=bank: e.matmul(ps[:, bank, :], lhsT=w4[0:64, c0:c0 + 128], rhs=hx[0:64, :], start=True, stop=True),
                     reads=[b_w, b_hx], writes=[bk[bank]])
                P.op("act", lambda e, ci=ci, par=par: e.activation(out=dec, in_=tv[par], func=AF.Exp, scale=hc[:, ci, 1:2]),
                     reads=[b_zt[par], b_w], writes=[b_dec])
                P.op("dve", lambda e, bank=bank: e.tensor_tensor(out=kr, in0=ps[:, bank, :], in1=dec, op=ALU.mult), reads=[bk[bank], b_dec], writes=[b_kr])
                if dr == 1 and blk == 0:
                    P.op("dve", lambda e: e.memset(kr[:, 0:1], 0.0), reads=[b_kr], writes=[b_kr])
                P.op("dve", lambda e: e.tensor_reduce(out=part, in_=kr, axis=mybir.AxisListType.X, op=ALU.add, apply_absolute_value=True),
                     reads=[b_kr], writes=[b_part])
                P.op("dve", lambda e, ci=ci: e.tensor_tensor(out=g.knorm[:, ci:ci + 1], in0=g.knorm[:, ci:ci + 1], in1=part, op=ALU.add),
                     reads=[b_part, g.b_knorm], writes=[g.b_knorm])
                P.op("act", lambda e: e.activation(out=krb, in_=kr, func=AF.Copy), reads=[b_kr], writes=[b_krb])
                r0 = dr * T + n0
                fm_to_tok(g, krb, b_krb, [S.ktok[r0 + j * 128:r0 + (j + 1) * 128, ci * 128:(ci + 1) * 128] for j in range(4)],
                          tb, b_tb, 3, g.b_scrK)
    if S.knorm_dbg is not None:
        g.dbg_stores.append(P.dma("sp", S.knorm_dbg, g.knorm, reads=[g.b_knorm]))
    g.sb_off = mark
    P.barrier()


def ln_stats(g, tile, b_tile, mv, sd, b_s):
    P = g.P
    st = g.ln_st
    for hh in range(2):
        P.op("dve", lambda e, hh=hh: e.bn_stats(out=st[:, hh, :], in_=tile[:, hh * 512:(hh + 1) * 512]),
             reads=[b_tile], writes=[b_s])
    P.op("dve", lambda e: e.bn_aggr(out=mv, in_=st), reads=[b_s], writes=[b_s])
    P.op("act", lambda e: e.activation(out=sd, in_=mv[:, 1:2], func=AF.Sqrt, bias=g.epsT[:, 0:1], scale=1.0),
         reads=[b_s, g.bconst], writes=[b_s])
    P.op("dve", lambda e: e.reciprocal(out=sd, in_=sd), reads=[b_s], writes=[b_s])


def resid_ln(g, pY, b_pY, xt, b_xt, Grow, b_G, lng, lnb, b_rows, tmp, b_tmp, mv, sd, b_s):
    P = g.P
    P.op("dve", lambda e: e.tensor_tensor(out=tmp, in0=pY, in1=Grow, op=ALU.mult), reads=b_pY + [b_G], writes=[b_tmp])
    P.op("dve", lambda e: e.scalar_tensor_tensor(out=xt, in0=xt, scalar=ALPHA, in1=tmp, op0=ALU.mult, op1=ALU.add),
         reads=[b_xt, b_tmp], writes=[b_xt])
    ln_stats(g, xt, b_xt, mv, sd, b_s)
    P.op("dve", lambda e: e.tensor_scalar(out=tmp, in0=xt, scalar1=mv[:, 0:1], scalar2=sd[:, 0:1],
                                          op0=ALU.subtract, op1=ALU.mult), reads=[b_xt, b_s], writes=[b_tmp])
    P.op("pool", lambda e: e.tensor_tensor(out=tmp, in0=tmp, in1=lng, op=ALU.mult), reads=[b_tmp, b_rows], writes=[b_tmp])
    P.op("pool", lambda e: e.tensor_tensor(out=xt, in0=tmp, in1=lnb, op=ALU.add), reads=[b_tmp, b_rows], writes=[b_xt])


def phase_D(g, l, x_src, wout_ap, modA, G, b_modA, b_G):
    nc, P, I, S, sb = g.nc, g.P, g.I, g.S, g.sb
    mark = g.sb_off
    Wo = sb([8, D], BF16)
    lng, lnb = sb([D], F32), sb([D], F32)
    b_w, b_rows = P.buf(), P.buf()
    P.dma("pool", Wo, wout_ap.rearrange("(k p) n -> p k n", p=128), writes=[b_w])
    r0 = ROWS.index(f"ln1_g{l}")
    P.dma("sp", lng, I.rows[r0], writes=[b_rows])
    P.dma("sp", lnb, I.rows[r0 + 1], writes=[b_rows])
    g.ln_st = sb([2, 6], F32)
    g.ln_mv = sb([4, 2], F32)
    g.ln_sd = sb([4], F32)
    mv, sd = sb([2], F32), sb([1], F32)
    b_s = P.buf()
    mixb = sb([8, 512], BF16)
    b_mix = P.buf()
    xts = [sb([D], F32) for _ in range(4)]
    b_xts = [P.buf() for _ in range(4)]
    xn = [sb([D], BF16) for _ in range(4)]
    b_xn = [P.buf() for _ in range(4)]
    tmp = sb([D], F32)
    b_tmp = P.buf()
    hT = sb([8, 512], BF16)
    b_hT = P.buf()
    zt = sb([8, 2], BF16)
    b_z = P.buf()
    P.op("pool", lambda e: e.memset(zt, 0.0), writes=[b_z])
    g.b_scrD = P.buf()
    P.dma("pool", S.h2T[:, :, 0:1].rearrange("k p o -> p k o"), zt[:, :, 0:1], reads=[b_z], writes=[g.b_scrD], allow_slow_non_contiguous=True)
    P.dma("pool", S.h2T[:, :, T + 1:T + 2].rearrange("k p o -> p k o"), zt[:, :, 1:2], reads=[b_z], writes=[g.b_scrD], allow_slow_non_contiguous=True)
    bk = g.bk
    pY = g.ps[:, 6:8, :].rearrange("p a b -> p (a b)")
    nblk = 16 if DEBUG not in ("D1", "H1", "H2", "H3") else 1
    if DEBUG in ("D1", "H1", "H2", "H3") and l == 0:
        zz = sb([8, 512], BF16)
        P.op("pool", lambda e: e.memset(zz, 0.0), writes=[b_z])
        P.dma("pool", S.mixT[:, :, 0:512].rearrange("k p n -> p k n"), zz, reads=[b_z], writes=[g.b_scrB])
        P.dma("pool", S.h2T[:, :, 513:514].rearrange("k p o -> p k o"), zt[:, :, 0:1], reads=[b_z], writes=[g.b_scrD], allow_slow_non_contiguous=True)
    for blk in range(nblk):
        t0 = blk * 512
        P.dma("sp", mixb, S.mixT[:, :, t0:t0 + 512].rearrange("k p n -> p k n"), reads=[g.b_scrB], writes=[b_mix])
        for i in range(4):
            P.dma("sp", xts[i], x_src[t0 + i * 128:t0 + (i + 1) * 128, :], reads=[g.b_scrX], writes=[b_xts[i]])
            for half in range(2):
                for k in range(8):
                    P.op("pe", lambda e, k=k, i=i, half=half: e.matmul(
                        g.ps[:, 6 + half, :], lhsT=mixb[:, k, i * 128:(i + 1) * 128], rhs=Wo[:, k, half * 512:(half + 1) * 512],
                        start=(k == 0), stop=(k == 7)), reads=[b_mix, b_w], writes=[bk[6 + half]])
            resid_ln(g, pY, [bk[6], bk[7]], xts[i], b_xts[i], G[:, 0, :], b_G, lng, lnb, b_rows, tmp, b_tmp, mv, sd, b_s)
            g.dbg_stores.append(P.dma("pool", S.x1[t0 + i * 128:t0 + (i + 1) * 128, :], xts[i], reads=[b_xts[i]], writes=[g.b_scrD]))
        ln_to_featmajor(g, None, 4, hT, b_hT, [modA[:, 32 + k, 0:1] for k in range(8)], [modA[:, 24 + k, 0:1] for k in range(8)],
                        b_modA, xts, b_xts, xn, b_xn, (P.buf(), P.buf(), bk))
        g.dbg_stores.append(P.dma("pool", S.h2T[:, :, 1 + t0:1 + t0 + 512].rearrange("k p n -> p k n"), hT, reads=[b_hT], writes=[g.b_scrD]))
    g.sb_off = mark
    P.barrier()


def phase_E(g, l, dst, modA, G, b_modA, b_G):
    nc, P, I, S, sb = g.nc, g.P, g.I, g.S, g.sb
    mark = g.sb_off
    W1 = sb([8, 2 * FF], BF16)
    W2 = sb([22, D], BF16)
    cw = sb([44, 4], F32)
    lng, lnb = sb([D], F32), sb([D], F32)
    b_w, b_rows = P.buf(), P.buf()
    for k in range(8):
        P.dma("pool", W1[:, k, :], I.ffn_w_in[l][k * 128:(k + 1) * 128, :], writes=[b_w])
    P.dma("pool", W2, I.ffn_w_out[l].rearrange("(k p) n -> p k n", p=128), writes=[b_w])
    P.dma("sp", cw, I.ffn_cw[l], writes=[b_rows])
    r0 = ROWS.index(f"ln2_g{l}")
    P.dma("sp", lng, I.rows[r0], writes=[b_rows])
    P.dma("sp", lnb, I.rows[r0 + 1], writes=[b_rows])
    g.ln_st = sb([2, 6], F32)
    mv, sd = sb([2], F32), sb([1], F32)
    b_s = P.buf()
    hb = sb([8, 514], BF16)
    b_hb = P.buf()
    hid = sb([22, 512], BF16)
    b_hid = P.buf()
    cb = [sb([512], F32) for _ in range(2)]
    b_cb = [P.buf(), P.buf()]
    ga = sb([512], F32)
    b_ga = P.buf()
    xt = sb([D], F32)
    b_xt = P.buf()
    tmp = sb([D], F32)
    b_tmp = P.buf()
    bk = g.bk
    ps = g.ps
    pY = ps[:, 6:8, :].rearrange("p a b -> p (a b)")
    nblk = 16 if DEBUG not in ("D1", "H1", "H2", "H3") else 1
    for blk in range(nblk):
        t0 = blk * 512
        P.dma("sp", hb, S.h2T[:, :, t0:t0 + 514].rearrange("k p n -> p k n"), reads=[g.b_scrD], writes=[b_hb])
        for i in range(22):
            for part in range(2):
                ch = part * 22 + i
                ub = (2 * i + part) % 2
                hcol = (2 * i + part) % 64
                for k in range(8):
                    P.op("pe", lambda e, k=k, ch=ch, ub=ub: e.matmul(ps[:, ub, :], lhsT=W1[:, k, ch * 128:(ch + 1) * 128],
                                                                 rhs=hb[:, k, 1:513], start=(k == 0), stop=(k == 7)),
                         reads=[b_w, b_hb], writes=[bk[ub]])
                for k in range(8):
                    P.op("pe", lambda e, k=k, ch=ch, hcol=hcol: e.matmul(ps[:, 2, 2 * hcol:2 * hcol + 2], lhsT=W1[:, k, ch * 128:(ch + 1) * 128],
                                                                     rhs=hb[:, k, 0:514:513], start=(k == 0), stop=(k == 7)),
                         reads=[b_w, b_hb], writes=[bk[2]])
                c = cb[part]
                bc = b_cb[part]
                uA = ps[:, ub, :]
                uB = ps[:, 2, 2 * hcol:2 * hcol + 2]
                P.op("act", lambda e, c=c, uA=uA, ch=ch: e.activation(out=c, in_=uA, func=AF.Identity, scale=cw[:, ch, 1:2], bias=cw[:, ch, 3:4]),
                     reads=[bk[ub], b_rows], writes=[bc])
                P.op("dve", lambda e, c=c, uA=uA, ch=ch: e.scalar_tensor_tensor(out=c[:, 1:512], in0=uA[:, 0:511], scalar=cw[:, ch, 0:1], in1=c[:, 1:512],
                                                                          op0=ALU.mult, op1=ALU.add), reads=[bk[ub], b_rows, bc], writes=[bc])
                P.op("dve", lambda e, c=c, uA=uA, ch=ch: e.scalar_tensor_tensor(out=c[:, 0:511], in0=uA[:, 1:512], scalar=cw[:, ch, 2:3], in1=c[:, 0:511],
                                                                          op0=ALU.mult, op1=ALU.add), reads=[bk[ub], b_rows, bc], writes=[bc])
                P.op("dve", lambda e, c=c, uB=uB, ch=ch: e.scalar_tensor_tensor(out=c[:, 0:1], in0=uB[:, 0:1], scalar=cw[:, ch, 0:1], in1=c[:, 0:1],
                                                                          op0=ALU.mult, op1=ALU.add), reads=[bk[2], b_rows, bc], writes=[bc])
                P.op("dve", lambda e, c=c, uB=uB, ch=ch: e.scalar_tensor_tensor(out=c[:, 511:512], in0=uB[:, 1:2], scalar=cw[:, ch, 2:3], in1=c[:, 511:512],
                                                                          op0=ALU.mult, op1=ALU.add), reads=[bk[2], b_rows, bc], writes=[bc])
            P.op("act", lambda e: e.activation(out=ga, in_=cb[0], func=AF.Gelu), reads=[b_cb[0]], writes=[b_ga])
            P.op("pool", lambda e, i=i: e.tensor_tensor(out=hid[:, i, :], in0=ga, in1=cb[1], op=ALU.mult), reads=[b_ga, b_cb[1]], writes=[b_hid])
        for it in range(4):
            P.dma("sp", xt, S.x1[t0 + it * 128:t0 + (it + 1) * 128, :], reads=[g.b_scrD], writes=[b_xt])
            for half in range(2):
                for i in range(22):
                    P.op("pe", lambda e, i=i, it=it, half=half: e.matmul(ps[:, 6 + half, :], lhsT=hid[:, i, it * 128:(it + 1) * 128],
                                                                     rhs=W2[:, i, half * 512:(half + 1) * 512], start=(i == 0), stop=(i == 21)),
                         reads=[b_hid, b_w], writes=[bk[6 + half]])
            resid_ln(g, pY, [bk[6], bk[7]], xt, b_xt, G[:, 1, :], b_G, lng, lnb, b_rows, tmp, b_tmp, mv, sd, b_s)
            g.out_stores.append(P.dma("pool", dst[t0 + it * 128:t0 + (it + 1) * 128, :], xt, reads=[b_xt], writes=[g.b_scrX]))
    g.sb_off = mark
    P.barrier()


def rope_tables():
    pos = np.arange(T)
    pr = (pos // 64).astype(np.float32)
    pc = (pos % 64).astype(np.float32)
    inv = (np.float32(10000.0) ** (-np.arange(0, 32, 2, dtype=np.float32) / np.float32(32))).astype(np.float32)
    C = np.ones((64, NK), np.float32)
    Sg = np.zeros((64, NK), np.float32)
    for d in range(64):
        p = pr if d < 32 else pc
        a = (p * inv[d % 16]).astype(np.float32)
        C[d, NCTX:] = np.cos(a)
        sgn = -1.0 if (d % 32) < 16 else 1.0
        Sg[d, NCTX:] = sgn * np.sin(a)
    return np.concatenate([C, C], 0), np.concatenate([Sg, Sg], 0)


def rope_perm_cols():
    idx = np.arange(512)
    d = idx % 64
    partner = np.where((d % 32) < 16, d + 16, d - 16)
    return (idx // 64) * 64 + partner


def make_inputs(inp):
    f32 = np.float32
    common = {}
    common["mod_w"] = np.ascontiguousarray(inp["mod_w"], f32)
    common["modb_col"] = np.ascontiguousarray(inp["mod_b"].reshape(2, 48, 128).transpose(0, 2, 1), f32)
    rows = {}
    for l in range(2):
        rows[f"ln1_g{l}"] = inp["ln1_g"][l]
        rows[f"ln1_b{l}"] = inp["ln1_b"][l]
        rows[f"ln2_g{l}"] = inp["ln2_g"][l]
        rows[f"ln2_b{l}"] = inp["ln2_b"][l]
        rows[f"modb_g1_{l}"] = inp["mod_b"][l, 2 * D:3 * D]
        rows[f"modb_g2_{l}"] = inp["mod_b"][l, 5 * D:6 * D]
    common["rows"] = np.ascontiguousarray(
        np.stack([np.broadcast_to(rows[r][None, :], (128, D)) for r in ROWS]), f32)
    common["cols"] = np.zeros((NCOLS, 128, 1), f32)
    common["cols"][0, :, 0] = inp["da_subln_g"][0]
    common["ident"] = np.eye(128, dtype=f32)
    w_in = np.asarray(inp["da_w_in"][0], f32)
    common["da_w_in"] = np.ascontiguousarray(w_in)
    pc = rope_perm_cols()
    common["da_w_perm"] = np.ascontiguousarray(np.concatenate([w_in[:, 0:512][:, pc], w_in[:, 512:1024][:, pc]], 1))
    common["da_w_out"] = np.ascontiguousarray(inp["da_w_out"][0], f32)
    C, Sg = rope_tables()
    common["rope_c"] = C
    common["rope_s"] = Sg
    lamv = np.stack([inp["da_lam_q1"][0], inp["da_lam_k1"][0], inp["da_lam_q2"][0], inp["da_lam_k2"][0]])
    common["lam"] = np.ascontiguousarray(np.broadcast_to(lamv[None], (128, 4, 64)), f32)
    a = np.arange(64)[:, None].astype(np.float64)
    k1 = np.arange(64)[None, :].astype(np.float64)
    th = 2 * np.pi * a * k1 / 64
    common["f64"] = np.concatenate([np.cos(th), -np.sin(th)], 1).astype(f32)
    p = np.arange(128)[:, None, None].astype(np.float64)
    kk = (np.arange(64)[None, :, None] + 64 * np.arange(128)[None, None, :]).astype(np.float64)
    th = 2 * np.pi * p * kk / 8192
    mr, mi = np.cos(th), -np.sin(th)
    common["fmt"] = np.stack([np.concatenate([mr, mi], 2), np.concatenate([-mi, mr], 2)], 2).astype(f32)
    c = np.arange(128)[:, None].astype(np.float64)
    th = 2 * np.pi * c * c.T / 128
    sc = 1.0 / np.sqrt(8192.0 * 128.0)
    common["fcs"] = np.stack([np.cos(th) * sc, np.sin(th) * sc], 1).astype(f32)
    common["ffn_w_in"] = np.ascontiguousarray(inp["ffn_w_in"], f32)
    common["ffn_w_out"] = np.ascontiguousarray(inp["ffn_w_out"], f32)
    cwb = np.concatenate([inp["ffn_conv_w"], inp["ffn_conv_b"][:, None, :]], 1)
    common["ffn_cw"] = np.ascontiguousarray(cwb.reshape(2, 4, 44, 128).transpose(0, 3, 2, 1), f32)
    common["hy_w_in"] = np.ascontiguousarray(inp["hy_w_in"][0], f32)
    common["hy_w_out"] = np.ascontiguousarray(inp["hy_w_out"][0], f32)
    hcw = np.concatenate([inp["hy_conv_w"][0], inp["hy_conv_b"][0][None, :]], 0)
    common["hy_cw"] = np.ascontiguousarray(hcw.reshape(4, 24, 128).transpose(2, 1, 0), f32)
    min_decay = math.log(1e-2) / 1.5
    max_decay = math.log(1e-2) / 0.3
    deltas = np.abs(np.linspace(min_decay, max_decay, D, dtype=f32))
    common["hy_cols"] = np.ascontiguousarray(
        np.stack([inp["hy_d"][0].reshape(8, 128).T, -deltas.reshape(8, 128).T], -1), f32)
    L = T
    tt = np.linspace(0.0, 1.0, L, dtype=f32)
    wv = (f32(2.0 * math.pi) * np.arange(L, dtype=f32) / f32(L)).astype(f32)
    fb = np.linspace(1e-4, 15, 16, dtype=f32)
    ang = (fb[None, :] * wv[:, None]).astype(f32)
    z = np.concatenate([tt[:, None], np.cos(ang), -np.sin(ang)], -1).astype(f32)
    idx = (L - np.arange(L)) % L
    idx[0] = 0
    zr = z[idx]
    common["hyf_z"] = np.ascontiguousarray(np.stack([z.T, zr.T]), f32)
    common["hyf_t"] = np.ascontiguousarray(np.stack([np.broadcast_to(tt[None], (128, L)),
                                                     np.broadcast_to(tt[idx][None], (128, L))]), f32)
    common["hyf_w1"] = np.ascontiguousarray(inp["hy_f_w1"][0], f32)
    common["hyf_w23"] = np.ascontiguousarray(np.stack([inp["hy_f_w2"][0], inp["hy_f_w3"][0]]), f32)
    common["hyf_w4"] = np.ascontiguousarray(inp["hy_f_w4"][0], f32)
    common["hyf_cols"] = np.ascontiguousarray(np.stack([inp["hy_f_freq"][0], inp["hy_f_b1"][0], inp["hy_f_b2"][0],
                                                        inp["hy_f_b3"][0]], -1), f32)
    a = np.arange(128)[:, None].astype(np.float64)
    kk1 = np.arange(128)[None, :].astype(np.float64)
    th = 2 * np.pi * a * kk1 / 128
    common["dft_f1"] = np.stack([np.concatenate([np.cos(th), -np.sin(th)], 1),
                                 np.concatenate([np.sin(th), np.cos(th)], 1)], 1).astype(f32)
    pp = np.arange(128)[None, :, None].astype(np.float64)
    kfull = (np.arange(128)[:, None, None] + 128 * np.arange(128)[None, None, :]).astype(np.float64)
    th = 2 * np.pi * pp * kfull / 16384.0
    mr, mi = np.cos(th), -np.sin(th)
    common["dft_mt"] = np.concatenate([mr, mi, -mi, mr], 2).astype(f32)
    maps = []
    for b in range(4):
        m = dict(common)
        m["x"] = np.ascontiguousarray(inp["x"][b], f32)
        m["ctx"] = np.ascontiguousarray(inp["ctx"][b], f32)
        cv = np.stack([inp["c"][b].reshape(8, 128).T, inp["c_ctx"].reshape(8, 128).T], -1)
        m["cvec"] = np.ascontiguousarray(cv, f32)
        maps.append(m)
    return maps


def kernel(**inputs):
    inp = {k: np.asarray(v) for k, v in inputs.items()}
    nc, _ = build_program()
    maps = make_inputs(inp)
    res = run_bass_kernel_spmd(nc, maps, core_ids=[0, 1, 2, 3])
    return np.stack([np.asarray(r["out"], np.float32) for r in res.results], 0)
```

```python
import contextlib
import math
import numpy as np
import ml_dtypes
import concourse.bass as bass
import concourse.mybir as mybir
from concourse.bass_utils import run_bass_kernel_spmd

F32 = mybir.dt.float32
BF16 = mybir.dt.bfloat16
AF = mybir.ActivationFunctionType
ALU = mybir.AluOpType

D = 1024
T = 8192
NCTX = 256
NK = T + NCTX
FF = 2816
EPS = 1e-5
ALPHA = 4 ** 0.25
DEBUG = None
DEBUG_OUT = set()


class Buf:
    __slots__ = ("name", "w", "r")

    def __init__(self, name):
        self.name = name
        self.w = None
        self.r = []


class Prog:
    ENGS = ("pe", "act", "dve", "pool", "sp")
    NDMASEM = 8

    def __init__(self, nc):
        self.nc = nc
        self.ops = []
        self.bar_deps = set()
        self.bar_pending = set()
        self.since_bar = {}

    def barrier(self):
        deps = set(self.bar_deps)
        for k, v in self.since_bar.items():
            if k == "dma":
                deps.update(v)
            else:
                deps.add(v)
        self.bar_deps = deps
        self.bar_pending = set(self.ENGS)
        self.since_bar = {}

    def buf(self, name="b"):
        return Buf(name)

    def op(self, eng, fn, reads=(), writes=(), dma=False):
        oid = len(self.ops)
        deps = set()
        for b in reads:
            if b.w is not None:
                deps.add(b.w)
        for b in writes:
            if b.w is not None:
                deps.add(b.w)
            last = {}
            for r in b.r:
                o = self.ops[r]
                if o["dma"]:
                    deps.add(r)
                else:
                    last[o["eng"]] = r
            deps.update(last.values())
        for b in reads:
            b.r.append(oid)
        for b in writes:
            b.w = oid
            b.r = []
        if eng in self.bar_pending:
            deps.update(self.bar_deps)
            self.bar_pending.discard(eng)
        deps.discard(oid)
        if dma:
            self.since_bar.setdefault("dma", []).append(oid)
        else:
            self.since_bar[eng] = oid
        self.ops.append(dict(id=oid, eng=eng, fn=fn, deps=deps, dma=dma))
        return oid

    def dma(self, q, out, in_, reads=(), writes=(), **kw):
        return self.op(q, lambda e: e.dma_start(out=out, in_=in_, **kw), reads, writes, dma=True)

    def emit(self, final_deps):
        nc = self.nc
        ops = self.ops
        ops.append(dict(id=len(ops), eng="sp", fn=None, deps=set(final_deps), dma=False))
        needed = set()
        for o in ops:
            for d in list(o["deps"]):
                od = ops[d]
                if o["eng"] == "pe" and od["eng"] == "pe" and not od["dma"] and not o["dma"]:
                    o["deps"].discard(d)
            needed |= o["deps"]
        st = contextlib.ExitStack()
        sems = {e: st.enter_context(nc.semaphore("s_" + e)) for e in self.ENGS}
        dsems = {q: [st.enter_context(nc.semaphore(f"d_{q}{i}")) for i in range(self.NDMASEM)]
                 for q in ("sp", "act", "pool")}
        cnt = {e: 0 for e in self.ENGS}
        dcnt = {q: 0 for q in dsems}
        for o in ops:
            o["sig"] = None
            o["pre"] = None
            if o["dma"]:
                q = o["eng"]
                k = dcnt[q]
                dcnt[q] += 1
                s = dsems[q][k % self.NDMASEM]
                v = 16 * (k // self.NDMASEM + 1)
                o["sig"] = (s, v, 16)
                if k >= self.NDMASEM:
                    o["pre"] = (s, v - 16)
            elif o["id"] in needed:
                cnt[o["eng"]] += 1
                o["sig"] = (sems[o["eng"]], cnt[o["eng"]], 1)
        self.stats = dict(n=len(ops), cnt=cnt, dcnt=dcnt)
        per = {e: [o for o in ops if o["eng"] == e] for e in self.ENGS}

        def replay(eng_name):
            def run(e):
                waited = {}

                def w(s, v):
                    if waited.get(id(s), 0) < v:
                        e.wait_ge(s, v)
                        waited[id(s)] = v
                for o in per[eng_name]:
                    if o["pre"] is not None:
                        w(*o["pre"])
                    for d in sorted(o["deps"]):
                        s, v, _ = ops[d]["sig"]
                        w(s, v)
                    if o["fn"] is None:
                        continue
                    ins = o["fn"](e)
                    if o["sig"] is not None:
                        ins.then_inc(o["sig"][0], o["sig"][2])
            return run

        with nc.Block() as block:
            block.tensor(replay("pe"))
            block.scalar(replay("act"))
            block.vector(replay("dve"))
            block.gpsimd(replay("pool"))
            block.sync(replay("sp"))
        st.close()


class Ctx:
    pass


def build_program():
    nc = bass.Bass("TRN2", target_bir_lowering=False)
    g = Ctx()
    g.nc = nc
    P = Prog(nc)
    g.P = P
    g.final = []
    g.dbg_stores = []
    g.b_scrB = P.buf("scrB")
    g.b_scrX = P.buf("scrX")
    g.out_stores = []
    g.bk = [P.buf(f"bank{i}") for i in range(8)]

    def din(name, shape, dt=F32):
        return nc.dram_tensor(name, list(shape), dt, kind="ExternalInput").ap()

    def dscr(name, shape, dt):
        kind = "ExternalOutput" if name in DEBUG_OUT else "Internal"
        return nc.dram_tensor(name, list(shape), dt, kind=kind).ap()

    I = Ctx()
    g.I = I
    I.x = din("x", [T, D])
    I.ctx = din("ctx", [NCTX, D])
    I.cvec = din("cvec", [128, 8, 2])
    I.mod_w = din("mod_w", [2, D, 6 * D])
    I.modb_col = din("modb_col", [2, 128, 48])
    I.rows = din("rows", [NROWS, 128, D])
    I.cols = din("cols", [NCOLS, 128, 1])
    I.ident = din("ident", [128, 128])
    I.da_w_in = din("da_w_in", [D, 2048])
    I.da_w_perm = din("da_w_perm", [D, 1024])
    I.da_w_out = din("da_w_out", [D, D])
    I.rope_c = din("rope_c", [128, NK])
    I.rope_s = din("rope_s", [128, NK])
    I.lam = din("lam", [128, 4, 64])
    I.hy_w_in = din("hy_w_in", [D, 3 * D])
    I.hy_w_out = din("hy_w_out", [D, D])
    I.hy_cw = din("hy_cw", [128, 24, 4])
    I.dft_f1 = din("dft_f1", [128, 2, 256])
    I.dft_mt = din("dft_mt", [128, 128, 512])
    I.hyf_z = din("hyf_z", [2, 33, T])
    I.hyf_t = din("hyf_t", [2, 128, T])
    I.hyf_w1 = din("hyf_w1", [33, 64])
    I.hyf_w23 = din("hyf_w23", [2, 64, 64])
    I.hyf_w4 = din("hyf_w4", [64, 2 * D])
    I.hyf_cols = din("hyf_cols", [64, 4])
    I.hy_cols = din("hy_cols", [128, 8, 2])
    I.ffn_w_in = din("ffn_w_in", [2, D, 2 * FF])
    I.ffn_w_out = din("ffn_w_out", [2, FF, D])
    I.ffn_cw = din("ffn_cw", [2, 128, 44, 4])
    I.f64 = din("f64", [64, 128])
    I.fmt = din("fmt", [128, 64, 2, 256])
    I.fcs = din("fcs", [128, 2, 128])
    g.out = nc.dram_tensor("out", [T, D], F32, kind="ExternalOutput").ap()

    S = Ctx()
    g.S = S
    S.QT = dscr("QT", [4, 128, T], BF16)
    S.KT = dscr("KT", [4, 128, NK], BF16)
    S.V = dscr("V", [NK, 512], BF16)
    S.F = dscr("F", [T, 512], BF16)
    S.mixT = dscr("mixT", [8, 128, T], BF16)
    S.x1 = dscr("x1", [T, D], F32)
    S.x2 = dscr("x2", [T, D], F32)
    S.h2T = dscr("h2T", [8, 128, T + 2], BF16)
    S.x0T = dscr("x0T", [8, 128, T], BF16)
    S.vT = dscr("vT", [8, 128, T], BF16)
    S.vtok = dscr("vtok", [T, D], BF16)
    S.ktok = dscr("ktok", [2 * T, D], BF16)
    S.KS = dscr("KS", [8, 128, 128, 256], BF16)
    S.Ztok = dscr("Ztok", [2, 2 * T, D], BF16)
    S.MTb = dscr("MTb", [128, 128, 512], BF16)
    S.y_dbg = dscr("y_dbg", [128, T], F32) if "y_dbg" in DEBUG_OUT else None
    S.knorm_dbg = dscr("knorm_dbg", [128, 8], F32) if "knorm_dbg" in DEBUG_OUT else None
    S.hT_dbg = dscr("hT_dbg", [128, 8, T], F32) if "hT_dbg" in DEBUG_OUT else None

    arena = nc.alloc_sbuf_tensor("arena", [128, ARENA_BYTES // 4], F32)
    g.arena = arena
    g.ps = nc.alloc_psum_tensor("ps", [128, 8, 512], F32)
    g.sb_off = 0

    def sb(shape, dt, reset_to=None):
        n = int(np.prod(shape))
        nbytes = n * (4 if dt == F32 else 2)
        nbytes = (nbytes + 31) // 32 * 32
        off = g.sb_off
        assert off + nbytes <= ARENA_BYTES, (off, nbytes)
        g.sb_off += nbytes
        v = arena[:, off // 4:(off + nbytes) // 4]
        if dt != F32:
            v = v.bitcast(dt)
        v = v[:, 0:n]
        if len(shape) == 2:
            v = v.rearrange("p (a b) -> p a b", b=shape[1])
        elif len(shape) == 3:
            v = v.rearrange("p (a b c) -> p a b c", b=shape[1], c=shape[2])
        return v
    g.sb = sb

    g.ident = sb([128], BF16)
    g.identf = sb([128], F32)
    g.epsT = sb([1], F32)
    bconst = P.buf("const")
    g.bconst = bconst
    P.dma("sp", g.identf, I.ident, writes=[bconst])
    P.op("dve", lambda e: e.tensor_copy(out=g.ident, in_=g.identf), reads=[bconst], writes=[bconst])
    P.op("pool", lambda e: e.memset(g.epsT, EPS), writes=[bconst])
    g.persist_off = g.sb_off

    layer0(g)
    if DEBUG is None:
        g.final.extend(g.out_stores)
    elif not g.final:
        g.final.extend(g.dbg_stores + g.out_stores)

    P.emit(g.final)
    return nc, P


ROWS = ["ln1_g0", "ln1_b0", "ln2_g0", "ln2_b0", "ln1_g1", "ln1_b1", "ln2_g1", "ln2_b1",
        "modb_g1_0", "modb_g2_0", "modb_g1_1", "modb_g2_1"]
NROWS = len(ROWS)
NCOLS = 64
ARENA_BYTES = 212480


def mod_params(g, l):
    nc, P, I, sb = g.nc, g.P, g.I, g.sb
    modA = sb([48, 2], F32)
    G = sb([2, D], F32)
    mark = g.sb_off
    cv = sb([8, 2], F32)
    sc = sb([8, 2], F32)
    screp = sb([8, 128], F32)
    mbc = sb([48], F32)
    wblk = [sb([8, 512], F32) for _ in range(2)]
    grow = sb([2, D], F32)
    b_modA, b_G, b_cv, b_rep = P.buf(), P.buf(), P.buf(), P.buf()
    b_w = [P.buf(), P.buf()]
    b_pm, b_pg = P.buf(), P.buf()
    pM = g.ps[:, 0, 0:96].rearrange("p (j s) -> p j s", s=2)
    P.dma("sp", cv, I.cvec, writes=[b_cv])
    P.dma("sp", mbc, I.modb_col[l], writes=[b_cv])
    r1 = ROWS.index(f"modb_g1_{l}")
    P.dma("sp", grow[:, 0, :], I.rows[r1], writes=[b_cv])
    P.dma("sp", grow[:, 1, :], I.rows[r1 + 1], writes=[b_cv])
    P.op("act", lambda e: e.activation(out=sc, in_=cv, func=AF.Silu), reads=[b_cv], writes=[b_cv])
    for k in range(8):
        P.op("dve", lambda e, k=k: e.tensor_copy(out=screp[:, k, :], in_=sc[:, k, 0:1].to_broadcast([128, 128])),
             reads=[b_cv], writes=[b_rep])
    wv = I.mod_w[l].rearrange("(k p) n -> p k n", p=128)
    for blk in range(12):
        w = wblk[blk % 2]
        bw = b_w[blk % 2]
        P.dma("sp", w, wv[:, :, blk * 512:(blk + 1) * 512], writes=[bw])
        for jj in range(4):
            j = blk * 4 + jj
            for k in range(8):
                P.op("pe", lambda e, k=k, jj=jj, j=j, w=w: e.matmul(
                    pM[:, j, :], lhsT=w[:, k, jj * 128:(jj + 1) * 128], rhs=sc[:, k, :],
                    start=(k == 0), stop=(k == 7)), reads=[bw, b_cv], writes=[b_pm])
        if blk in (4, 5, 10, 11):
            gi = 0 if blk < 6 else 1
            half = blk % 2
            pG = g.ps[:, 1 + half, :]
            for k in range(8):
                P.op("pe", lambda e, k=k, w=w, pG=pG: e.matmul(
                    pG, lhsT=screp[:, k, :], rhs=w[:, k, :], start=(k == 0), stop=(k == 7)),
                    reads=[bw, b_rep], writes=[b_pg])
            P.op("dve", lambda e, gi=gi, half=half, pG=pG: e.tensor_tensor(
                out=G[:, gi, half * 512:(half + 1) * 512], in0=pG, in1=grow[:, gi, half * 512:(half + 1) * 512],
                op=ALU.add), reads=[b_pg, b_cv], writes=[b_G])
    P.op("dve", lambda e: e.tensor_tensor(out=modA, in0=pM, in1=mbc.unsqueeze(2).to_broadcast([128, 48, 2]),
                                          op=ALU.add), reads=[b_pm, b_cv], writes=[b_modA])
    for lo in (8, 32):
        P.op("dve", lambda e, lo=lo: e.tensor_scalar(out=modA[:, lo:lo + 8, :], in0=modA[:, lo:lo + 8, :],
                                                    scalar1=1.0, scalar2=None, op0=ALU.add),
             reads=[b_modA], writes=[b_modA])
    g.sb_off = mark
    P.barrier()
    return modA, G, b_modA, b_G


def ln_to_featmajor(g, src_tiles, ntile, hT, b_hT, scale_cols, bias_cols, b_mod, xt, b_xt, xn, b_xn, tagbufs):
    nc, P = g.nc, g.P
    b_st, b_mv, b_pT = tagbufs
    st, mv, sd = g.ln_st, g.ln_mv, g.ln_sd
    for i in range(ntile):
        for hh in range(2):
            P.op("dve", lambda e, i=i, hh=hh: e.bn_stats(out=st[:, hh, :], in_=xt[i][:, hh * 512:(hh + 1) * 512]),
                 reads=[b_xt[i]], writes=[b_st])
        P.op("dve", lambda e, i=i: e.bn_aggr(out=mv[:, i, :], in_=st), reads=[b_st], writes=[b_mv])
    P.op("act", lambda e: e.activation(out=sd[:, 0:ntile], in_=mv[:, 0:ntile, 1], func=AF.Sqrt,
                                       bias=g.epsT[:, 0:1], scale=1.0), reads=[b_mv, g.bconst], writes=[b_mv])
    P.op("dve", lambda e: e.reciprocal(out=sd[:, 0:ntile], in_=sd[:, 0:ntile]), reads=[b_mv], writes=[b_mv])
    for i in range(ntile):
        P.op("dve", lambda e, i=i: e.tensor_scalar(out=xn[i], in0=xt[i], scalar1=mv[:, i, 0:1],
                                                   scalar2=sd[:, i:i + 1], op0=ALU.subtract, op1=ALU.mult),
             reads=[b_xt[i], b_mv], writes=[b_xn[i]])
    pT = g.ps[:, 0:4, :].bitcast(BF16).rearrange("p b (h n) -> p (b h) n", h=2)
    n = ntile * 128
    for j in range(4):
        for k in (2 * j, 2 * j + 1):
            for i in range(ntile):
                P.op("pe", lambda e, k=k, i=i: e.transpose(out=pT[:, k, i * 128:(i + 1) * 128],
                                                          in_=xn[i][:, k * 128:(k + 1) * 128], identity=g.ident),
                     reads=[b_xn[i], g.bconst], writes=[b_pT[j]])
        for k in (2 * j, 2 * j + 1):
            P.op("act", lambda e, k=k, n=n: e.activation(out=hT[:, k, 0:n], in_=pT[:, k, 0:n], func=AF.Identity,
                                                        scale=scale_cols[k], bias=bias_cols[k]),
                 reads=[b_pT[j], b_mod], writes=[b_hT])


def layer0(g):
    nc, P, I, S, sb = g.nc, g.P, g.I, g.S, g.sb
    g.sb_off = g.persist_off
    modA, G, b_modA, b_G = mod_params(g, 0)
    g.modA0, g.G0, g.b_modA0, g.b_G0 = modA, G, b_modA, b_G
    if DEBUG == "mod":
        d = nc.dram_tensor("dbg_modA", [128, 96], F32, kind="ExternalOutput").ap()
        d2 = nc.dram_tensor("dbg_G", [128, 2 * D], F32, kind="ExternalOutput").ap()
        g.final.append(P.dma("sp", d, modA.rearrange("p j s -> p (j s)"), reads=[b_modA]))
        g.final.append(P.dma("sp", d2, G.rearrange("p a b -> p (a b)"), reads=[b_G]))
        return
    phase_mark = g.sb_off
    NCOLW = 3072
    W = sb([8, NCOLW], BF16)
    b_W = P.buf()
    if DEBUG != "A0":
        P.dma("pool", W[:, :, 0:2048], I.da_w_in.rearrange("(k p) n -> p k n", p=128), writes=[b_W])
        P.dma("pool", W[:, :, 2048:3072], I.da_w_perm.rearrange("(k p) n -> p k n", p=128), writes=[b_W])
    g.ln_st = sb([2, 6], F32)
    g.ln_mv = sb([4, 2], F32)
    g.ln_sd = sb([4], F32)
    xts = [[sb([D], F32) for _ in range(4)] for _ in range(2)]
    b_xts = [[P.buf() for _ in range(4)] for _ in range(2)]
    xn = [sb([D], BF16) for _ in range(4)]
    b_xn = [P.buf() for _ in range(4)]
    hT = sb([8, 512], BF16)
    b_hT = P.buf()
    tag = (P.buf(), P.buf(), g.bk)
    ctab = [sb([512], F32) for _ in range(2)]
    stab = [sb([512], F32) for _ in range(2)]
    b_tab = [P.buf(), P.buf()]
    t1 = sb([512], F32)
    t2 = sb([512], F32)
    b_t1, b_t2 = P.buf(), P.buf()
    qo = [sb([512], BF16) for _ in range(2)]
    b_qo = [P.buf(), P.buf()]
    vo = [sb([512], BF16) for _ in range(2)]
    b_vo = [P.buf(), P.buf()]
    pA, pB = g.ps[:, 4, :], g.ps[:, 5, :]
    b_pA, b_pB = g.bk[4], g.bk[5]
    pV = [g.ps[:, 6, :], g.ps[:, 7, :]]
    b_pV = [g.bk[6], g.bk[7]]
    b_scr = P.buf("scrA")
    g.b_scrA = b_scr
    nblk = 17
    if DEBUG in ("A1", "A0"):
        nblk = 2
    qcnt = 0
    vcnt = 0
    for blk in range(nblk):
        ctxb = (blk == 0)
        ntile = 2 if ctxb else 4
        n = ntile * 128
        par = blk % 2
        src = I.ctx if ctxb else I.x
        t0 = 0 if ctxb else (blk - 1) * 512
        kpos = 0 if ctxb else NCTX + t0
        for i in range(ntile):
            P.dma("sp", xts[par][i], src[t0 + i * 128:t0 + (i + 1) * 128, :], writes=[b_xts[par][i]])
        P.dma("sp", ctab[par][:, 0:n], I.rope_c[:, kpos:kpos + n], writes=[b_tab[par]])
        P.dma("sp", stab[par][:, 0:n], I.rope_s[:, kpos:kpos + n], writes=[b_tab[par]])
        s = 1 if ctxb else 0
        ln_to_featmajor(g, None, ntile, hT, b_hT,
                        [modA[:, 8 + k, s:s + 1] for k in range(8)], [modA[:, k, s:s + 1] for k in range(8)],
                        b_modA, xts[par], b_xts[par], xn, b_xn, tag)
        if S.hT_dbg is not None and not ctxb:
            if not hasattr(g, "hTf"):
                g.hTf = sb([8, 512], F32)
                g.b_hTf = P.buf()
            P.op("dve", lambda e: e.tensor_copy(out=g.hTf, in_=hT), reads=[b_hT], writes=[g.b_hTf])
            g.final.append(P.dma("sp", S.hT_dbg[:, :, t0:t0 + n], g.hTf[:, :, 0:n], reads=[g.b_hTf]))
        if DEBUG == "A0":
            continue
        for which in ((1,) if ctxb else (0, 1)):
            for h in range(4):
                c0 = which * 512 + h * 128
                c1 = 2048 + which * 512 + h * 128
                for k in range(8):
                    P.op("pe", lambda e, k=k, c0=c0, n=n: e.matmul(pA[:, 0:n], lhsT=W[:, k, c0:c0 + 128], rhs=hT[:, k, 0:n],
                                                               start=(k == 0), stop=(k == 7)),
                         reads=[b_W, b_hT], writes=[b_pA])
                for k in range(8):
                    P.op("pe", lambda e, k=k, c1=c1, n=n: e.matmul(pB[:, 0:n], lhsT=W[:, k, c1:c1 + 128], rhs=hT[:, k, 0:n],
                                                               start=(k == 0), stop=(k == 7)),
                         reads=[b_W, b_hT], writes=[b_pB])
                P.op("dve", lambda e, n=n, par=par: e.tensor_tensor(out=t1[:, 0:n], in0=pA[:, 0:n], in1=ctab[par][:, 0:n], op=ALU.mult),
                     reads=[b_pA, b_tab[par]], writes=[b_t1])
                P.op("dve", lambda e, n=n, par=par: e.tensor_tensor(out=t2[:, 0:n], in0=pB[:, 0:n], in1=stab[par][:, 0:n], op=ALU.mult),
                     reads=[b_pB, b_tab[par]], writes=[b_t2])
                qb = qcnt % 2
                qcnt += 1
                P.op("pool", lambda e, n=n, qb=qb: e.tensor_tensor(out=qo[qb][:, 0:n], in0=t1[:, 0:n], in1=t2[:, 0:n], op=ALU.add),
                     reads=[b_t1, b_t2], writes=[b_qo[qb]])
                dst = S.QT[h][:, t0:t0 + n] if which == 0 else S.KT[h][:, kpos:kpos + n]
                g.dbg_stores.append(P.dma("pool", dst, qo[qb][:, 0:n], reads=[b_qo[qb]], writes=[b_scr]))
        for i in range(ntile):
            for which in ((0,) if ctxb else (0, 1)):
                c0 = 1024 + which * 512
                vb = vcnt % 2
                vcnt += 1
                for k in range(8):
                    P.op("pe", lambda e, k=k, i=i, c0=c0, vb=vb: e.matmul(pV[vb], lhsT=hT[:, k, i * 128:(i + 1) * 128],
                                                                     rhs=W[:, k, c0:c0 + 512], start=(k == 0), stop=(k == 7)),
                         reads=[b_W, b_hT], writes=[b_pV[vb]])
                P.op("act", lambda e, vb=vb: e.activation(out=vo[vb], in_=pV[vb], func=AF.Copy),
                     reads=[b_pV[vb]], writes=[b_vo[vb]])
                if which == 0:
                    dst = S.V[kpos + i * 128:kpos + (i + 1) * 128, :]
                else:
                    dst = S.F[t0 + i * 128:t0 + (i + 1) * 128, :]
                g.dbg_stores.append(P.dma("act", dst, vo[vb], reads=[b_vo[vb]], writes=[b_scr]))
    if DEBUG in ("A1", "A0"):
        g.final.extend(g.dbg_stores)
        return
    g.sb_off = phase_mark
    P.barrier()
    attention(g)


def attention(g):
    nc, P, I, S, sb = g.nc, g.P, g.I, g.S, g.sb
    mark = g.sb_off
    KT = sb([NK], BF16)
    Vh = sb([66, 128], BF16)
    b_KT, b_Vh = P.buf(), P.buf()
    QTb = [sb([512], BF16) for _ in range(2)]
    b_Q = [P.buf(), P.buf()]
    Pm = [[sb([512], BF16) for _ in range(2)] for _ in range(2)]
    b_Pm = [[P.buf(), P.buf()], [P.buf(), P.buf()]]
    ones = sb([128], BF16)
    onesf = sb([128], F32)
    lamt = sb([4, 64], F32)
    lt = sb([2, 64], F32)
    ls = sb([2], F32)
    nlam = sb([1], F32)
    gp = sb([1], F32)
    b_c = P.buf()
    P.op("pool", lambda e: e.memset(ones, 1.0), writes=[b_c])
    P.op("pool", lambda e: e.memset(onesf, 1.0 / 128.0), writes=[b_c])
    P.dma("sp", lamt, I.lam, writes=[b_c])
    P.dma("sp", gp, I.cols[0], writes=[b_c])
    P.op("dve", lambda e: e.tensor_tensor(out=lt, in0=lamt[:, 0:4:2, :], in1=lamt[:, 1:4:2, :], op=ALU.mult),
         reads=[b_c], writes=[b_c])
    P.op("dve", lambda e: e.tensor_reduce(out=ls, in_=lt, axis=mybir.AxisListType.X, op=ALU.add), reads=[b_c], writes=[b_c])
    P.op("act", lambda e: e.activation(out=ls, in_=ls, func=AF.Exp), reads=[b_c], writes=[b_c])
    P.op("dve", lambda e: e.tensor_tensor(out=nlam, in0=ls[:, 1:2], in1=ls[:, 0:1], op=ALU.subtract), reads=[b_c], writes=[b_c])
    P.op("dve", lambda e: e.tensor_scalar(out=nlam, in0=nlam, scalar1=-(0.8 - 0.6), scalar2=None, op0=ALU.add), reads=[b_c], writes=[b_c])
    P.op("dve", lambda e: e.tensor_scalar(out=gp, in0=gp, scalar1=1.0 - (0.8 - 0.6), scalar2=None, op0=ALU.mult), reads=[b_c], writes=[b_c])
    r1 = sb([512], F32)
    o1 = sb([512], F32)
    o2 = sb([512], F32)
    sq = sb([512], F32)
    rs = sb([512], F32)
    ob = [sb([512], BF16) for _ in range(2)]
    b_r1, b_o1, b_o2, b_sq, b_rs = P.buf(), P.buf(), P.buf(), P.buf(), P.buf()
    b_ob = [P.buf(), P.buf()]
    bk = [P.buf() for _ in range(8)]
    ps = g.ps
    heads = range(4)
    qbs = range(16)
    if DEBUG == "B1":
        heads, qbs = [0], [0]
    if DEBUG in ("C1", "D1", "H1", "H2", "H3"):
        heads = []
    cnt = 0
    for h in heads:
        P.dma("sp", KT, S.KT[h], reads=[g.b_scrA], writes=[b_KT])
        P.dma("sp", Vh, S.V[:, h * 128:(h + 1) * 128].rearrange("(t p) e -> p t e", p=128), reads=[g.b_scrA], writes=[b_Vh])
        for qb in qbs:
            Q = QTb[cnt % 2]
            bq = b_Q[cnt % 2]
            obuf = ob[cnt % 2]
            b_obuf = b_ob[cnt % 2]
            cnt += 1
            P.dma("sp", Q, S.QT[h][:, qb * 512:(qb + 1) * 512], reads=[g.b_scrA], writes=[bq])
            def s_ops(kt):
                for c in range(2):
                    sbk = c * 2 + kt % 2
                    P.op("pe", lambda e, c=c, kt=kt, sbk=sbk, Q=Q: e.matmul(
                        ps[:, sbk, :], lhsT=KT[c * 64:(c + 1) * 64, kt * 128:(kt + 1) * 128], rhs=Q[c * 64:(c + 1) * 64, :],
                        start=True, stop=True), reads=[b_KT, bq], writes=[bk[sbk]])

            def e_ops(kt):
                for c in range(2):
                    sbk = c * 2 + kt % 2
                    pm = Pm[c][kt % 2]
                    P.op("act", lambda e, sbk=sbk, pm=pm: e.activation(out=pm, in_=ps[:, sbk, :], func=AF.Exp, scale=0.125),
                         reads=[bk[sbk]], writes=[b_Pm[c][kt % 2]])

            def pv_ops(kt):
                for c in range(2):
                    pm = Pm[c][kt % 2]
                    bpm = b_Pm[c][kt % 2]
                    P.op("pe", lambda e, c=c, kt=kt, pm=pm: e.matmul(ps[:, 4 + 2 * c, :], lhsT=Vh[:, kt, :], rhs=pm,
                                                                  start=(kt == 0), stop=(kt == 65)),
                         reads=[b_Vh, bpm], writes=[bk[4 + 2 * c]])
                    P.op("pe", lambda e, c=c, kt=kt, pm=pm: e.matmul(ps[:, 5 + 2 * c, :], lhsT=ones, rhs=pm,
                                                                  start=(kt == 0), stop=(kt == 65)),
                         reads=[b_c, bpm], writes=[bk[5 + 2 * c]])
            s_ops(0)
            for kt in range(66):
                if kt + 1 < 66:
                    s_ops(kt + 1)
                e_ops(kt)
                pv_ops(kt)
            P.op("dve", lambda e: e.reciprocal(out=r1, in_=ps[:, 5, :]), reads=[bk[5]], writes=[b_r1])
            P.op("dve", lambda e: e.tensor_tensor(out=o1, in0=ps[:, 4, :], in1=r1, op=ALU.mult), reads=[bk[4], b_r1], writes=[b_o1])
            P.op("dve", lambda e: e.reciprocal(out=r1, in_=ps[:, 7, :]), reads=[bk[7], b_r1], writes=[b_r1])
            P.op("dve", lambda e: e.tensor_tensor(out=o2, in0=ps[:, 6, :], in1=r1, op=ALU.mult), reads=[bk[6], b_r1], writes=[b_o2])
            P.op("dve", lambda e: e.scalar_tensor_tensor(out=o1, in0=o2, scalar=nlam[:, 0:1], in1=o1, op0=ALU.mult, op1=ALU.add),
                 reads=[b_o2, b_o1, b_c], writes=[b_o1])
            P.op("pool", lambda e: e.tensor_tensor(out=sq, in0=o1, in1=o1, op=ALU.mult), reads=[b_o1], writes=[b_sq])
            P.op("pe", lambda e: e.matmul(ps[:, 0, :], lhsT=onesf, rhs=sq, start=True, stop=True), reads=[b_sq, b_c], writes=[bk[0]])
            P.op("act", lambda e: e.activation(out=rs, in_=ps[:, 0, :], func=AF.Ln, bias=g.epsT[:, 0:1], scale=1.0),
                 reads=[bk[0], g.bconst], writes=[b_rs])
            P.op("act", lambda e: e.activation(out=rs, in_=rs, func=AF.Exp, scale=-0.5), reads=[b_rs], writes=[b_rs])
            P.op("dve", lambda e, obuf=obuf: e.scalar_tensor_tensor(out=obuf, in0=o1, scalar=gp[:, 0:1], in1=rs, op0=ALU.mult, op1=ALU.mult),
                 reads=[b_o1, b_rs, b_c], writes=[b_obuf])
            g.dbg_stores.append(P.dma("pool", S.mixT[h][:, qb * 512:(qb + 1) * 512], obuf, reads=[b_obuf], writes=[g.b_scrB]))
    if DEBUG == "B1":
        g.final.extend(g.dbg_stores)
        return
    g.sb_off = mark
    P.barrier()
    fourier(g)


def fourier(g):
    nc, P, I, S, sb = g.nc, g.P, g.I, g.S, g.sb
    mark = g.sb_off
    F64 = sb([128], BF16)
    MT = sb([64, 2, 256], BF16)
    CS = sb([2, 128], BF16)
    b_t = P.buf()
    P.dma("pool", F64[0:64, :], I.f64, writes=[b_t])
    P.dma("pool", MT, I.fmt, writes=[b_t])
    P.dma("pool", CS, I.fcs, writes=[b_t])
    G = sb([128, 128], BF16)
    Y = sb([2, 64, 128], BF16)
    X = sb([64, 256], BF16)
    fm = sb([T], BF16)
    b_G, b_Y, b_X, b_fm = P.buf(), P.buf(), P.buf(), P.buf()
    bk = [P.buf() for _ in range(8)]
    ps = g.ps
    groups = range(4)
    if DEBUG == "C1":
        groups = [0]
    if DEBUG in ("D1", "H1", "H2", "H3"):
        groups = []
    for gi in groups:
        P.dma("sp", G[0:64], S.F[:, gi * 128:(gi + 1) * 128].rearrange("(a p) c -> a p c", p=128),
              reads=[g.b_scrA], writes=[b_G])
        for c0 in range(0, 128, 4):
            b = (c0 // 4) % 2
            for cc in range(4):
                P.op("pe", lambda e, c=c0 + cc, cc=cc, b=b: e.matmul(ps[:, b, cc * 128:(cc + 1) * 128], lhsT=G[0:64, :, c],
                                                                 rhs=F64[0:64, :], start=True, stop=True),
                     reads=[b_G, b_t], writes=[bk[b]])
            P.op("act", lambda e, c0=c0, b=b: e.activation(
                out=Y[:, :, :, c0:c0 + 4], in_=ps[:, b, :].rearrange("p (cc r k) -> p r k cc", cc=4, r=2),
                func=AF.Copy), reads=[bk[b]], writes=[b_Y])
        for k1 in range(64):
            b = 2 + (k1 // 2) % 2
            o = (k1 % 2) * 256
            P.op("pe", lambda e, k1=k1, b=b, o=o: e.matmul(ps[:, b, o:o + 256], lhsT=Y[:, 0, k1, :], rhs=MT[:, k1, 0, :],
                                                       start=True, stop=False), reads=[b_Y, b_t], writes=[bk[b]])
            P.op("pe", lambda e, k1=k1, b=b, o=o: e.matmul(ps[:, b, o:o + 256], lhsT=Y[:, 1, k1, :], rhs=MT[:, k1, 1, :],
                                                       start=False, stop=True), reads=[b_Y, b_t], writes=[bk[b]])
            if k1 % 2 == 1:
                P.op("dve", lambda e, k1=k1, b=b: e.tensor_copy(out=X[:, k1 - 1:k1 + 1, :],
                                                              in_=ps[:, b, :].rearrange("p (a x) -> p a x", a=2)),
                     reads=[bk[b]], writes=[b_X])
        fmv = fm.rearrange("p (k2 k1) -> p k1 k2", k1=64)
        for q in range(16):
            b = 4 + q % 2
            P.op("pe", lambda e, q=q, b=b: e.matmul(ps[:, b, :], lhsT=CS[:, 0, :], rhs=X[:, 4 * q:4 * q + 4, 0:128],
                                                start=True, stop=False), reads=[b_X, b_t], writes=[bk[b]])
            P.op("pe", lambda e, q=q, b=b: e.matmul(ps[:, b, :], lhsT=CS[:, 1, :], rhs=X[:, 4 * q:4 * q + 4, 128:256],
                                                start=False, stop=True), reads=[b_X, b_t], writes=[bk[b]])
            P.op("act", lambda e, q=q, b=b: e.activation(out=fmv[:, 4 * q:4 * q + 4, :],
                                                     in_=ps[:, b, :].rearrange("p (a x) -> p a x", a=4), func=AF.Copy),
                 reads=[bk[b]], writes=[b_fm])
        g.dbg_stores.append(P.dma("act", S.mixT[4 + gi], fm, reads=[b_fm], writes=[g.b_scrB]))
    if DEBUG == "C1":
        g.final.extend(g.dbg_stores)
        return
    g.sb_off = mark
    P.barrier()
    phase_D(g, 0, I.x, I.da_w_out, g.modA0, g.G0, g.b_modA0, g.b_G0)
    phase_E(g, 0, S.x2, g.modA0, g.G0, g.b_modA0, g.b_G0)
    if DEBUG in ("D1",):
        return
    layer1(g)


def fm_to_tok(g, src, b_src, dst_rows, tb, b_tb, bank, b_scr):
    P = g.P
    pT = g.ps[:, bank, :].bitcast(BF16)[:, 0:512].rearrange("p (j c) -> p j c", c=128)
    for j in range(4):
        P.op("pe", lambda e, j=j: e.transpose(out=pT[:, j, :], in_=src[:, j * 128:(j + 1) * 128], identity=g.ident),
             reads=[b_src, g.bconst], writes=[g.bk[bank]])
    P.op("act", lambda e: e.activation(out=tb, in_=pT, func=AF.Copy), reads=[g.bk[bank]], writes=[b_tb])
    g.dbg_stores.append(P.dma("act", dst_rows, tb, reads=[b_tb], writes=[b_scr]))


def layer1(g):
    nc, P, I, S, sb = g.nc, g.P, g.I, g.S, g.sb
    g.sb_off = g.persist_off
    P.barrier()
    modA, G, b_modA, b_G = mod_params(g, 1)
    g.knorm = sb([8], F32)
    g.b_knorm = P.buf()
    mark = g.sb_off
    g.ln_st = sb([2, 6], F32)
    g.ln_mv = sb([4, 2], F32)
    g.ln_sd = sb([4], F32)
    xts = [sb([D], F32) for _ in range(4)]
    b_xts = [P.buf() for _ in range(4)]
    xn = [sb([D], BF16) for _ in range(4)]
    b_xn = [P.buf() for _ in range(4)]
    hT = sb([8, 512], BF16)
    b_hT = P.buf()
    nblk = 16 if DEBUG not in ("H1", "H2", "H3") else 1
    g.b_scrH = P.buf()
    for blk in range(nblk):
        t0 = blk * 512
        for i in range(4):
            P.dma("sp", xts[i], S.x2[t0 + i * 128:t0 + (i + 1) * 128, :], reads=[g.b_scrX], writes=[b_xts[i]])
        ln_to_featmajor(g, None, 4, hT, b_hT, [modA[:, 8 + k, 0:1] for k in range(8)], [modA[:, k, 0:1] for k in range(8)],
                        b_modA, xts, b_xts, xn, b_xn, (P.buf(), P.buf(), g.bk))
        P.dma("act", S.h2T[:, :, 1 + t0:1 + t0 + 512].rearrange("k p n -> p k n"), hT, reads=[b_hT], writes=[g.b_scrH])
    g.sb_off = mark
    P.barrier()
    W = sb([8, 3 * D], BF16)
    cw = sb([24, 4], F32)
    b_w, b_rows = P.buf(), P.buf()
    for k in range(8):
        P.dma("pool", W[:, k, :], I.hy_w_in[k * 128:(k + 1) * 128, :], writes=[b_w])
    P.dma("sp", cw, I.hy_cw, writes=[b_rows])
    hb = sb([8, 514], BF16)
    b_hb = P.buf()
    cb = [sb([512], F32) for _ in range(3)]
    b_cb = [P.buf() for _ in range(3)]
    x0b = sb([512], BF16)
    vvb = sb([512], BF16)
    b_x0b, b_vvb = P.buf(), P.buf()
    tb = sb([4, 128], BF16)
    b_tb = P.buf()
    ps, bk = g.ps, g.bk
    for blk in range(nblk):
        t0 = blk * 512
        P.dma("sp", hb, S.h2T[:, :, t0:t0 + 514].rearrange("k p n -> p k n"), reads=[g.b_scrH], writes=[b_hb])
        for i in range(8):
            for part in range(3):
                ch = part * 8 + i
                ub = (3 * i + part) % 2
                hcol = (3 * i + part) % 64
                hbk = 2 if ub == 0 else 4
                for k in range(8):
                    P.op("pe", lambda e, k=k, ch=ch, ub=ub: e.matmul(ps[:, ub, :], lhsT=W[:, k, ch * 128:(ch + 1) * 128],
                                                                 rhs=hb[:, k, 1:513], start=(k == 0), stop=(k == 7)),
                         reads=[b_w, b_hb], writes=[bk[ub]])
                for k in range(8):
                    P.op("pe", lambda e, k=k, ch=ch, hcol=hcol, hbk=hbk: e.matmul(ps[:, hbk, 2 * hcol:2 * hcol + 2], lhsT=W[:, k, ch * 128:(ch + 1) * 128],
                                                                     rhs=hb[:, k, 0:514:513], start=(k == 0), stop=(k == 7)),
                         reads=[b_w, b_hb], writes=[bk[hbk]])
                c = cb[part]
                bc = b_cb[part]
                uA = ps[:, ub, :]
                uB = ps[:, hbk, 2 * hcol:2 * hcol + 2]
                P.op("act", lambda e, c=c, uA=uA, ch=ch: e.activation(out=c, in_=uA, func=AF.Identity, scale=cw[:, ch, 1:2], bias=cw[:, ch, 3:4]),
                     reads=[bk[ub], b_rows], writes=[bc])
                P.op("dve", lambda e, c=c, uA=uA, ch=ch: e.scalar_tensor_tensor(out=c[:, 1:512], in0=uA[:, 0:511], scalar=cw[:, ch, 0:1], in1=c[:, 1:512],
                                                                          op0=ALU.mult, op1=ALU.add), reads=[bk[ub], b_rows, bc], writes=[bc])
                P.op("dve", lambda e, c=c, uA=uA, ch=ch: e.scalar_tensor_tensor(out=c[:, 0:511], in0=uA[:, 1:512], scalar=cw[:, ch, 2:3], in1=c[:, 0:511],
                                                                          op0=ALU.mult, op1=ALU.add), reads=[bk[ub], b_rows, bc], writes=[bc])
                P.op("dve", lambda e, c=c, uB=uB, ch=ch: e.scalar_tensor_tensor(out=c[:, 0:1], in0=uB[:, 0:1], scalar=cw[:, ch, 0:1], in1=c[:, 0:1],
                                                                          op0=ALU.mult, op1=ALU.add), reads=[bk[hbk], b_rows, bc], writes=[bc])
                P.op("dve", lambda e, c=c, uB=uB, ch=ch: e.scalar_tensor_tensor(out=c[:, 511:512], in0=uB[:, 1:2], scalar=cw[:, ch, 2:3], in1=c[:, 511:512],
                                                                          op0=ALU.mult, op1=ALU.add), reads=[bk[hbk], b_rows, bc], writes=[bc])
            P.op("act", lambda e: e.activation(out=x0b, in_=cb[0], func=AF.Copy), reads=[b_cb[0]], writes=[b_x0b])
            P.op("pool", lambda e: e.tensor_tensor(out=vvb, in0=cb[2], in1=cb[1], op=ALU.mult), reads=[b_cb[2], b_cb[1]], writes=[b_vvb])
            g.dbg_stores.append(P.dma("act", S.x0T[i][:, t0:t0 + 512], x0b, reads=[b_x0b], writes=[g.b_scrH]))
            g.dbg_stores.append(P.dma("pool", S.vT[i][:, t0:t0 + 512], vvb, reads=[b_vvb], writes=[g.b_scrH]))
            fm_to_tok(g, vvb, b_vvb, S.vtok[t0:t0 + 512, i * 128:(i + 1) * 128].rearrange("(j p) c -> p j c", p=128),
                      tb, b_tb, 3, g.b_scrH)
    if DEBUG == "H3":
        zt_ = sb([T - 512], BF16)
        b_z_ = P.buf()
        P.op("pool", lambda e: e.memset(zt_, 0.0), writes=[b_z_])
        for r in range(4, 64):
            P.dma("pool", S.vtok[r * 128:(r + 1) * 128, :], zt_[:, 0:1024], reads=[b_z_], writes=[g.b_scrH])
        P.dma("pool", S.x0T[0][:, 512:T], zt_, reads=[b_z_], writes=[g.b_scrH])
        P.dma("pool", S.vT[0][:, 512:T], zt_, reads=[b_z_], writes=[g.b_scrH])
    g.sb_off = mark
    P.barrier()
    if DEBUG == "H1":
        return
    hyena_filter(g)
    if DEBUG == "H2":
        return
    hyena_conv(g)
    if DEBUG == "H3":
        return
    phase_D(g, 1, S.x2, I.hy_w_out, modA, G, b_modA, b_G)
    phase_E(g, 1, g.out, modA, G, b_modA, b_G)


def dft16k(g, planes, A, consumer, groups, b_src, preload=None):
    nc, P, I, S, sb = g.nc, g.P, g.I, g.S, g.sb
    npl = len(planes)
    G = [sb([128, 128], BF16) for _ in range(npl)]
    Y = sb([2, 128, 128], BF16)
    mt = [sb([2, 256], BF16) for _ in range(4)]
    b_G, b_Y = P.buf(), P.buf()
    b_mt = [P.buf() for _ in range(4)]
    ps, bk = g.ps, g.bk
    for gi in groups:
        for pl in range(npl):
            v = planes[pl][:, gi * 128:(gi + 1) * 128].rearrange("(a p) c -> a p c", p=128)
            for q in range(4):
                P.dma("sp", G[pl][0:A, q * 32:(q + 1) * 32, :], v[:, q * 32:(q + 1) * 32, :], reads=[b_src], writes=[b_G])
        for c0 in range(0, 128, 2):
            b = (c0 // 2) % 2
            for cc in range(2):
                for pl in range(npl):
                    P.op("pe", lambda e, c=c0 + cc, cc=cc, b=b, pl=pl: e.matmul(
                        ps[:, b, cc * 256:(cc + 1) * 256], lhsT=G[pl][0:A, :, c], rhs=g.F1[0:A, pl, :],
                        start=(pl == 0), stop=(pl == npl - 1)), reads=[b_G, g.b_dft], writes=[bk[b]])
            P.op("act", lambda e, c0=c0, b=b: e.activation(
                out=Y[:, :, :, c0:c0 + 2], in_=ps[:, b, :].rearrange("p (cc r k) -> p r k cc", cc=2, r=2),
                func=AF.Copy), reads=[bk[b]], writes=[b_Y])
        def ld(k1):
            P.dma("sp", mt[k1 % 4], S.MTb[k1].rearrange("p (v n) -> p v n", v=2), reads=[g.b_dft], writes=[b_mt[k1 % 4]])
            if preload is not None:
                preload(gi, k1)
        for k1 in range(3):
            ld(k1)
        for k1 in range(128):
            par = k1 % 4
            b = 2 + k1 % 2
            if k1 + 3 < 128:
                ld(k1 + 3)
            P.op("pe", lambda e, k1=k1, b=b, par=par: e.matmul(ps[:, b, 0:256], lhsT=Y[:, 0, k1, :], rhs=mt[par][:, 0, :],
                                                           start=True, stop=False), reads=[b_Y, b_mt[par]], writes=[bk[b]])
            P.op("pe", lambda e, k1=k1, b=b, par=par: e.matmul(ps[:, b, 0:256], lhsT=Y[:, 1, k1, :], rhs=mt[par][:, 1, :],
                                                           start=False, stop=True), reads=[b_Y, b_mt[par]], writes=[bk[b]])
            consumer(gi, k1, ps[:, b, 0:256], b)


def hyena_conv(g):
    nc, P, I, S, sb = g.nc, g.P, g.I, g.S, g.sb
    ps, bk = g.ps, g.bk
    base = g.sb_off
    groups = range(8) if DEBUG != "H3" else [0]
    g.F1 = sb([2, 256], BF16)
    g.b_dft = P.buf()
    P.dma("pool", g.F1, I.dft_f1, writes=[g.b_dft])
    mark = g.sb_off
    mf = [sb([512], F32) for _ in range(2)]
    mb = [sb([512], BF16) for _ in range(2)]
    b_mf, b_mb = [P.buf(), P.buf()], [P.buf(), P.buf()]
    for k1 in range(128):
        par = k1 % 2
        P.dma("sp", mf[par], I.dft_mt[k1], writes=[b_mf[par]])
        P.op("act", lambda e, par=par: e.activation(out=mb[par], in_=mf[par], func=AF.Copy), reads=[b_mf[par]], writes=[b_mb[par]])
        P.dma("act", S.MTb[k1], mb[par], reads=[b_mb[par]], writes=[g.b_dft])
    g.sb_off = mark
    P.barrier()
    kt = [sb([256], BF16) for _ in range(2)]
    b_kt = [P.buf(), P.buf()]
    g.b_scrKS = P.buf()

    def cons_filter(gi, k1, pz, bank):
        par = k1 % 2
        P.op("act", lambda e: e.activation(out=kt[par], in_=pz, func=AF.Copy), reads=[bk[bank]], writes=[b_kt[par]])
        P.dma("act", S.KS[gi][:, k1, :], kt[par], reads=[b_kt[par]], writes=[g.b_scrKS])
    dft16k(g, [S.ktok], 128, cons_filter, groups, g.b_scrK)
    g.sb_off = mark
    P.barrier()
    ksb = [sb([256], BF16) for _ in range(4)]
    b_ksb = [P.buf() for _ in range(4)]
    ta, tb_, tc, td = sb([128], F32), sb([128], F32), sb([128], F32), sb([128], F32)
    b_ta, b_tb2, b_tc, b_td = P.buf(), P.buf(), P.buf(), P.buf()
    zz = [sb([2, 128], BF16) for _ in range(2)]
    b_zz = [P.buf(), P.buf()]
    zt = [sb([2, 128], BF16) for _ in range(2)]
    b_zt = [P.buf(), P.buf()]
    g.b_scrZ = P.buf()

    def cons_signal(gi, k1, pz, bank):
        par = k1 % 2
        K = ksb[k1 % 4]
        b_K = b_ksb[k1 % 4]
        xr, xi = pz[:, 0:128], pz[:, 128:256]
        kr, ki = K[:, 0:128], K[:, 128:256]
        P.op("dve", lambda e: e.tensor_tensor(out=ta, in0=xr, in1=kr, op=ALU.mult), reads=[bk[bank], b_K], writes=[b_ta])
        P.op("dve", lambda e: e.tensor_tensor(out=tb_, in0=xi, in1=ki, op=ALU.mult), reads=[bk[bank], b_K], writes=[b_tb2])
        P.op("dve", lambda e: e.tensor_tensor(out=tc, in0=xr, in1=ki, op=ALU.mult), reads=[bk[bank], b_K], writes=[b_tc])
        P.op("dve", lambda e: e.tensor_tensor(out=td, in0=xi, in1=kr, op=ALU.mult), reads=[bk[bank], b_K], writes=[b_td])
        Z = zz[par]
        P.op("pool", lambda e: e.tensor_tensor(out=Z[:, 0, :], in0=ta, in1=tb_, op=ALU.subtract), reads=[b_ta, b_tb2], writes=[b_zz[par]])
        P.op("dve", lambda e: e.scalar_tensor_tensor(out=Z[:, 1, :], in0=tc, scalar=-1.0, in1=td, op0=ALU.mult, op1=ALU.subtract),
             reads=[b_tc, b_td], writes=[b_zz[par]])
        tbank = 4 + par
        pT = ps[:, tbank, :].bitcast(BF16)[:, 0:256].rearrange("p (v c) -> p v c", v=2)
        for v in range(2):
            P.op("pe", lambda e, v=v: e.transpose(out=pT[:, v, :], in_=Z[:, v, :], identity=g.ident),
                 reads=[b_zz[par], g.bconst], writes=[bk[tbank]])
        P.op("act", lambda e: e.activation(out=zt[par], in_=pT, func=AF.Copy), reads=[bk[tbank]], writes=[b_zt[par]])
        for v in range(2):
            dst = S.Ztok[v][:, gi * 128:(gi + 1) * 128].rearrange("(k2 k1) c -> k1 k2 c", k1=128)[k1]
            P.dma("act", dst, zt[par][:, v, :], reads=[b_zt[par]], writes=[g.b_scrZ])
    def pre_signal(gi, k1):
        P.dma("sp", ksb[k1 % 4], S.KS[gi][:, k1, :], reads=[g.b_scrKS], writes=[b_ksb[k1 % 4]])
    dft16k(g, [S.vtok], 64, cons_signal, groups, g.b_scrH, preload=pre_signal)
    g.sb_off = mark
    P.barrier()
    yt = sb([T], F32)
    vt = sb([T], BF16)
    x0t = sb([T], BF16)
    ot = vt
    hc = sb([8, 2], F32)
    rk = sb([8], F32)
    b_yt, b_vt, b_ot, b_c = P.buf(), P.buf(), P.buf(), P.buf()
    P.dma("sp", hc, I.hy_cols, writes=[b_c])
    P.op("dve", lambda e: e.reciprocal(out=rk, in_=g.knorm), reads=[g.b_knorm], writes=[b_c])
    P.op("dve", lambda e: e.tensor_scalar(out=rk, in0=rk, scalar1=1.0 / 16384.0, scalar2=None, op0=ALU.mult), reads=[b_c], writes=[b_c])
    ytv = yt.rearrange("p (n2 n1) -> p n1 n2", n1=128)

    def cons_inv(gi, k1, pz, bank):
        P.op("act", lambda e: e.activation(out=ytv[:, k1, :], in_=pz[:, 0:64], func=AF.Copy), reads=[bk[bank]], writes=[b_yt])
        if k1 == 127:
            P.dma("sp", vt, S.vT[gi], reads=[g.b_scrH], writes=[b_vt])
            P.dma("sp", x0t, S.x0T[gi], reads=[g.b_scrH], writes=[b_vt])
            if S.y_dbg is not None and gi == 0:
                g.dbg_stores.append(P.dma("sp", S.y_dbg, yt, reads=[b_yt]))
            P.op("dve", lambda e: e.tensor_scalar(out=yt, in0=yt, scalar1=rk[:, gi:gi + 1], scalar2=None, op0=ALU.mult),
                 reads=[b_yt, b_c], writes=[b_yt])
            P.op("dve", lambda e: e.scalar_tensor_tensor(out=yt, in0=vt, scalar=hc[:, gi, 0:1], in1=yt, op0=ALU.mult, op1=ALU.add),
                 reads=[b_vt, b_yt, b_c], writes=[b_yt])
            P.op("dve", lambda e: e.tensor_tensor(out=ot, in0=yt, in1=x0t, op=ALU.mult), reads=[b_yt, b_vt], writes=[b_vt])
            g.dbg_stores.append(P.dma("pool", S.mixT[gi], ot, reads=[b_vt], writes=[g.b_scrB]))
    dft16k(g, [S.Ztok[0], S.Ztok[1]], 128, cons_inv, groups, g.b_scrZ)
    g.sb_off = base
    P.barrier()


def hyena_filter(g):
    nc, P, I, S, sb = g.nc, g.P, g.I, g.S, g.sb
    mark = g.sb_off
    w1 = sb([64], F32)
    w23 = sb([2, 64], F32)
    w4 = sb([2 * D], F32)
    fc = sb([4], F32)
    sc = sb([4], F32)
    hc = sb([8, 2], F32)
    b_w = P.buf()
    P.dma("sp", w1[0:33, :], I.hyf_w1, writes=[b_w])
    P.dma("sp", w23[0:64], I.hyf_w23.rearrange("l k n -> k l n"), writes=[b_w])
    P.dma("sp", w4[0:64, :], I.hyf_w4, writes=[b_w])
    P.dma("sp", fc[0:64, :], I.hyf_cols, writes=[b_w])
    P.dma("sp", hc, I.hy_cols, writes=[b_w])
    P.op("dve", lambda e: e.tensor_scalar(out=sc[0:64, 0:1], in0=fc[0:64, 0:1], scalar1=1.0 / 3.0, scalar2=None, op0=ALU.mult),
         reads=[b_w], writes=[b_w])
    P.op("dve", lambda e: e.tensor_scalar(out=sc[0:64, 1:4], in0=fc[0:64, 1:4], scalar1=sc[0:64, 0:1], scalar2=None, op0=ALU.mult),
         reads=[b_w], writes=[b_w])
    P.op("pool", lambda e: e.memset(g.knorm, 0.0), writes=[g.b_knorm])
    zt = [sb([512], F32) for _ in range(2)]
    tv = [sb([512], F32) for _ in range(2)]
    b_zt = [P.buf(), P.buf()]
    hs = sb([512], F32)
    ht = sb([512], F32)
    hx = sb([512], F32)
    b_hs, b_ht, b_hx = P.buf(), P.buf(), P.buf()
    dec = [sb([512], F32) for _ in range(2)]
    kr = [sb([512], F32) for _ in range(2)]
    krb = [sb([512], BF16) for _ in range(2)]
    part = [sb([1], F32) for _ in range(2)]
    b_dec, b_kr, b_krb, b_part = [[P.buf(), P.buf()] for _ in range(4)]
    tb = [sb([4, 128], BF16) for _ in range(2)]
    b_tb = [P.buf(), P.buf()]
    ps, bk = g.ps, g.bk
    g.b_scrK = P.buf()
    it = 0
    for dr in range(2):
        for blk in range(16):
            n0 = blk * 512
            par = it % 2
            it += 1
            P.dma("sp", zt[par][0:33, :], I.hyf_z[dr][:, n0:n0 + 512], writes=[b_zt[par]])
            P.dma("sp", tv[par], I.hyf_t[dr][:, n0:n0 + 512], writes=[b_zt[par]])
            src, b_src = zt[par], b_zt[par]
            kdim = 33
            for layer in range(3):
                lw = w1[0:33, :] if layer == 0 else w23[0:64, layer - 1, :]
                P.op("pe", lambda e, lw=lw, src=src, kdim=kdim: e.matmul(ps[0:64, 0, :], lhsT=lw, rhs=src[0:kdim, :], start=True, stop=True),
                     reads=[b_w, b_src], writes=[bk[0]])
                P.op("act", lambda e, layer=layer: e.activation(out=hs[0:64, :], in_=ps[0:64, 0, :], func=AF.Sin,
                                                              scale=sc[0:64, 0:1], bias=sc[0:64, 1 + layer:2 + layer]),
                     reads=[bk[0], b_w], writes=[b_hs])
                P.op("dve", lambda e: e.tensor_tensor(out=ht[0:64, :], in0=hs[0:64, :], in1=hs[0:64, :], op=ALU.mult), reads=[b_hs], writes=[b_ht])
                P.op("dve", lambda e: e.tensor_scalar(out=ht[0:64, :], in0=ht[0:64, :], scalar1=-4.0, scalar2=3.0, op0=ALU.mult, op1=ALU.add),
                     reads=[b_ht], writes=[b_ht])
                P.op("dve", lambda e: e.tensor_tensor(out=hx[0:64, :], in0=hs[0:64, :], in1=ht[0:64, :], op=ALU.mult), reads=[b_hs, b_ht], writes=[b_hx])
                src, b_src, kdim = hx, b_hx, 64
            for ci in range(8):
                bank = 1 + ci % 2
                c0 = dr * D + ci * 128
                P.op("pe", lambda e, c0=c0, bank=bank: e.matmul(ps[:, bank, :], lhsT=w4[0:64, c0:c0 + 128], rhs=hx[0:64, :], start=True, stop=True),
                     reads=[b_w, b_hx], writes=[bk[bank]])
                q = ci % 2
                P.op("act", lambda e, ci=ci, par=par, q=q: e.activation(out=dec[q], in_=tv[par], func=AF.Exp, scale=hc[:, ci, 1:2]),
                     reads=[b_zt[par], b_w], writes=[b_dec[q]])
                P.op("dve", lambda e, bank=bank, q=q: e.tensor_tensor(out=kr[q], in0=ps[:, bank, :], in1=dec[q], op=ALU.mult),
                     reads=[bk[bank], b_dec[q]], writes=[b_kr[q]])
                if dr == 1 and blk == 0:
                    P.op("dve", lambda e, q=q: e.memset(kr[q][:, 0:1], 0.0), reads=[b_kr[q]], writes=[b_kr[q]])
                P.op("dve", lambda e, q=q: e.tensor_reduce(out=part[q], in_=kr[q], axis=mybir.AxisListType.X, op=ALU.add, apply_absolute_value=True),
                     reads=[b_kr[q]], writes=[b_part[q]])
                P.op("dve", lambda e, ci=ci, q=q: e.tensor_tensor(out=g.knorm[:, ci:ci + 1], in0=g.knorm[:, ci:ci + 1], in1=part[q], op=ALU.add),
                     reads=[b_part[q], g.b_knorm], writes=[g.b_knorm])
                P.op("act", lambda e, q=q: e.activation(out=krb[q], in_=kr[q], func=AF.Copy), reads=[b_kr[q]], writes=[b_krb[q]])
                r0 = dr * T + n0
                fm_to_tok(g, krb[ci % 2], b_krb[ci % 2], S.ktok[r0:r0 + 512, ci * 128:(ci + 1) * 128].rearrange("(j p) c -> p j c", p=128),
                          tb[ci % 2], b_tb[ci % 2], 3 + (ci % 2), g.b_scrK)
    if S.knorm_dbg is not None:
        g.dbg_stores.append(P.dma("sp", S.knorm_dbg, g.knorm, reads=[g.b_knorm]))
    g.sb_off = mark
    P.barrier()


def ln_stats(g, tile, b_tile, mv, sd, b_s):
    P = g.P
    st = g.ln_st
    for hh in range(2):
        P.op("dve", lambda e, hh=hh: e.bn_stats(out=st[:, hh, :], in_=tile[:, hh * 512:(hh + 1) * 512]),
             reads=[b_tile], writes=[b_s])
    P.op("dve", lambda e: e.bn_aggr(out=mv, in_=st), reads=[b_s], writes=[b_s])
    P.op("act", lambda e: e.activation(out=sd, in_=mv[:, 1:2], func=AF.Sqrt, bias=g.epsT[:, 0:1], scale=1.0),
         reads=[b_s, g.bconst], writes=[b_s])
    P.op("dve", lambda e: e.reciprocal(out=sd, in_=sd), reads=[b_s], writes=[b_s])


def resid_ln(g, pY, b_pY, xt, b_xt, Grow, b_G, lng, lnb, b_rows, tmp, b_tmp, mv, sd, b_s):
    P = g.P
    P.op("dve", lambda e: e.tensor_tensor(out=tmp, in0=pY, in1=Grow, op=ALU.mult), reads=b_pY + [b_G], writes=[b_tmp])
    P.op("dve", lambda e: e.scalar_tensor_tensor(out=xt, in0=xt, scalar=ALPHA, in1=tmp, op0=ALU.mult, op1=ALU.add),
         reads=[b_xt, b_tmp], writes=[b_xt])
    ln_stats(g, xt, b_xt, mv, sd, b_s)
    P.op("dve", lambda e: e.tensor_scalar(out=tmp, in0=xt, scalar1=mv[:, 0:1], scalar2=sd[:, 0:1],
                                          op0=ALU.subtract, op1=ALU.mult), reads=[b_xt, b_s], writes=[b_tmp])
    P.op("pool", lambda e: e.tensor_tensor(out=tmp, in0=tmp, in1=lng, op=ALU.mult), reads=[b_tmp, b_rows], writes=[b_tmp])
    P.op("pool", lambda e: e.tensor_tensor(out=xt, in0=tmp, in1=lnb, op=ALU.add), reads=[b_tmp, b_rows], writes=[b_xt])


def phase_D(g, l, x_src, wout_ap, modA, G, b_modA, b_G):
    nc, P, I, S, sb = g.nc, g.P, g.I, g.S, g.sb
    mark = g.sb_off
    Wo = sb([8, D], BF16)
    lng, lnb = sb([D], F32), sb([D], F32)
    b_w, b_rows = P.buf(), P.buf()
    P.dma("pool", Wo, wout_ap.rearrange("(k p) n -> p k n", p=128), writes=[b_w])
    r0 = ROWS.index(f"ln1_g{l}")
    P.dma("sp", lng, I.rows[r0], writes=[b_rows])
    P.dma("sp", lnb, I.rows[r0 + 1], writes=[b_rows])
    g.ln_st = sb([2, 6], F32)
    g.ln_mv = sb([4, 2], F32)
    g.ln_sd = sb([4], F32)
    mv, sd = sb([2], F32), sb([1], F32)
    b_s = P.buf()
    mixb = sb([8, 512], BF16)
    b_mix = P.buf()
    xts = [sb([D], F32) for _ in range(4)]
    b_xts = [P.buf() for _ in range(4)]
    xn = [sb([D], BF16) for _ in range(4)]
    b_xn = [P.buf() for _ in range(4)]
    tmp = sb([D], F32)
    b_tmp = P.buf()
    hT = sb([8, 512], BF16)
    b_hT = P.buf()
    zt = sb([8, 2], BF16)
    b_z = P.buf()
    P.op("pool", lambda e: e.memset(zt, 0.0), writes=[b_z])
    g.b_scrD = P.buf()
    P.dma("pool", S.h2T[:, :, 0:1].rearrange("k p o -> p k o"), zt[:, :, 0:1], reads=[b_z], writes=[g.b_scrD], allow_slow_non_contiguous=True)
    P.dma("pool", S.h2T[:, :, T + 1:T + 2].rearrange("k p o -> p k o"), zt[:, :, 1:2], reads=[b_z], writes=[g.b_scrD], allow_slow_non_contiguous=True)
    bk = g.bk
    pY = g.ps[:, 6:8, :].rearrange("p a b -> p (a b)")
    nblk = 16 if DEBUG not in ("D1", "H1", "H2", "H3") else 1
    if DEBUG in ("D1", "H1", "H2", "H3") and l == 0:
        zz = sb([8, 512], BF16)
        P.op("pool", lambda e: e.memset(zz, 0.0), writes=[b_z])
        P.dma("pool", S.mixT[:, :, 0:512].rearrange("k p n -> p k n"), zz, reads=[b_z], writes=[g.b_scrB])
        P.dma("pool", S.h2T[:, :, 513:514].rearrange("k p o -> p k o"), zt[:, :, 0:1], reads=[b_z], writes=[g.b_scrD], allow_slow_non_contiguous=True)
    for blk in range(nblk):
        t0 = blk * 512
        P.dma("sp", mixb, S.mixT[:, :, t0:t0 + 512].rearrange("k p n -> p k n"), reads=[g.b_scrB], writes=[b_mix])
        for i in range(4):
            P.dma("sp", xts[i], x_src[t0 + i * 128:t0 + (i + 1) * 128, :], reads=[g.b_scrX], writes=[b_xts[i]])
            for half in range(2):
                for k in range(8):
                    P.op("pe", lambda e, k=k, i=i, half=half: e.matmul(
                        g.ps[:, 6 + half, :], lhsT=mixb[:, k, i * 128:(i + 1) * 128], rhs=Wo[:, k, half * 512:(half + 1) * 512],
                        start=(k == 0), stop=(k == 7)), reads=[b_mix, b_w], writes=[bk[6 + half]])
            resid_ln(g, pY, [bk[6], bk[7]], xts[i], b_xts[i], G[:, 0, :], b_G, lng, lnb, b_rows, tmp, b_tmp, mv, sd, b_s)
            g.dbg_stores.append(P.dma("pool", S.x1[t0 + i * 128:t0 + (i + 1) * 128, :], xts[i], reads=[b_xts[i]], writes=[g.b_scrD]))
        ln_to_featmajor(g, None, 4, hT, b_hT, [modA[:, 32 + k, 0:1] for k in range(8)], [modA[:, 24 + k, 0:1] for k in range(8)],
                        b_modA, xts, b_xts, xn, b_xn, (P.buf(), P.buf(), bk))
        g.dbg_stores.append(P.dma("act", S.h2T[:, :, 1 + t0:1 + t0 + 512].rearrange("k p n -> p k n"), hT, reads=[b_hT], writes=[g.b_scrD]))
    g.sb_off = mark
    P.barrier()


def phase_E(g, l, dst, modA, G, b_modA, b_G):
    nc, P, I, S, sb = g.nc, g.P, g.I, g.S, g.sb
    mark = g.sb_off
    W1 = sb([8, 2 * FF], BF16)
    W2 = sb([22, D], BF16)
    cw = sb([44, 4], F32)
    lng, lnb = sb([D], F32), sb([D], F32)
    b_w, b_rows = P.buf(), P.buf()
    for k in range(8):
        P.dma("pool", W1[:, k, :], I.ffn_w_in[l][k * 128:(k + 1) * 128, :], writes=[b_w])
    P.dma("pool", W2, I.ffn_w_out[l].rearrange("(k p) n -> p k n", p=128), writes=[b_w])
    P.dma("sp", cw, I.ffn_cw[l], writes=[b_rows])
    r0 = ROWS.index(f"ln2_g{l}")
    P.dma("sp", lng, I.rows[r0], writes=[b_rows])
    P.dma("sp", lnb, I.rows[r0 + 1], writes=[b_rows])
    g.ln_st = sb([2, 6], F32)
    mv, sd = sb([2], F32), sb([1], F32)
    b_s = P.buf()
    hb = sb([8, 514], BF16)
    b_hb = P.buf()
    hid = sb([22, 512], BF16)
    b_hid = P.buf()
    cb = [sb([512], F32) for _ in range(2)]
    b_cb = [P.buf(), P.buf()]
    ga = sb([512], F32)
    b_ga = P.buf()
    xt = sb([D], F32)
    b_xt = P.buf()
    tmp = sb([D], F32)
    b_tmp = P.buf()
    bk = g.bk
    ps = g.ps
    pY = ps[:, 6:8, :].rearrange("p a b -> p (a b)")
    nblk = 16 if DEBUG not in ("D1", "H1", "H2", "H3") else 1
    for blk in range(nblk):
        t0 = blk * 512
        P.dma("sp", hb, S.h2T[:, :, t0:t0 + 514].rearrange("k p n -> p k n"), reads=[g.b_scrD], writes=[b_hb])
        for i in range(22):
            for part in range(2):
                ch = part * 22 + i
                ub = (2 * i + part) % 2
                hcol = (2 * i + part) % 64
                hbk = 2 + ub
                for k in range(8):
                    P.op("pe", lambda e, k=k, ch=ch, ub=ub: e.matmul(ps[:, ub, :], lhsT=W1[:, k, ch * 128:(ch + 1) * 128],
                                                                 rhs=hb[:, k, 1:513], start=(k == 0), stop=(k == 7)),
                         reads=[b_w, b_hb], writes=[bk[ub]])
                for k in range(8):
                    P.op("pe", lambda e, k=k, ch=ch, hcol=hcol, hbk=hbk: e.matmul(ps[:, hbk, 2 * hcol:2 * hcol + 2], lhsT=W1[:, k, ch * 128:(ch + 1) * 128],
                                                                     rhs=hb[:, k, 0:514:513], start=(k == 0), stop=(k == 7)),
                         reads=[b_w, b_hb], writes=[bk[hbk]])
                c = cb[part]
                bc = b_cb[part]
                uA = ps[:, ub, :]
                uB = ps[:, hbk, 2 * hcol:2 * hcol + 2]
                P.op("act", lambda e, c=c, uA=uA, ch=ch: e.activation(out=c, in_=uA, func=AF.Identity, scale=cw[:, ch, 1:2], bias=cw[:, ch, 3:4]),
                     reads=[bk[ub], b_rows], writes=[bc])
                P.op("dve", lambda e, c=c, uA=uA, ch=ch: e.scalar_tensor_tensor(out=c[:, 1:512], in0=uA[:, 0:511], scalar=cw[:, ch, 0:1], in1=c[:, 1:512],
                                                                          op0=ALU.mult, op1=ALU.add), reads=[bk[ub], b_rows, bc], writes=[bc])
                P.op("dve", lambda e, c=c, uA=uA, ch=ch: e.scalar_tensor_tensor(out=c[:, 0:511], in0=uA[:, 1:512], scalar=cw[:, ch, 2:3], in1=c[:, 0:511],
                                                                          op0=ALU.mult, op1=ALU.add), reads=[bk[ub], b_rows, bc], writes=[bc])
                P.op("dve", lambda e, c=c, uB=uB, ch=ch: e.scalar_tensor_tensor(out=c[:, 0:1], in0=uB[:, 0:1], scalar=cw[:, ch, 0:1], in1=c[:, 0:1],
                                                                          op0=ALU.mult, op1=ALU.add), reads=[bk[hbk], b_rows, bc], writes=[bc])
                P.op("dve", lambda e, c=c, uB=uB, ch=ch: e.scalar_tensor_tensor(out=c[:, 511:512], in0=uB[:, 1:2], scalar=cw[:, ch, 2:3], in1=c[:, 511:512],
                                                                          op0=ALU.mult, op1=ALU.add), reads=[bk[hbk], b_rows, bc], writes=[bc])
            P.op("act", lambda e: e.activation(out=ga, in_=cb[0], func=AF.Gelu), reads=[b_cb[0]], writes=[b_ga])
            P.op("pool", lambda e, i=i: e.tensor_tensor(out=hid[:, i, :], in0=ga, in1=cb[1], op=ALU.mult), reads=[b_ga, b_cb[1]], writes=[b_hid])
        for it in range(4):
            P.dma("sp", xt, S.x1[t0 + it * 128:t0 + (it + 1) * 128, :], reads=[g.b_scrD], writes=[b_xt])
            for half in range(2):
                for i in range(22):
                    P.op("pe", lambda e, i=i, it=it, half=half: e.matmul(ps[:, 6 + half, :], lhsT=hid[:, i, it * 128:(it + 1) * 128],
                                                                     rhs=W2[:, i, half * 512:(half + 1) * 512], start=(i == 0), stop=(i == 21)),
                         reads=[b_hid, b_w], writes=[bk[6 + half]])
            resid_ln(g, pY, [bk[6], bk[7]], xt, b_xt, G[:, 1, :], b_G, lng, lnb, b_rows, tmp, b_tmp, mv, sd, b_s)
            g.out_stores.append(P.dma("pool", dst[t0 + it * 128:t0 + (it + 1) * 128, :], xt, reads=[b_xt], writes=[g.b_scrX]))
    g.sb_off = mark
    P.barrier()


def rope_tables():
    pos = np.arange(T)
    pr = (pos // 64).astype(np.float32)
    pc = (pos % 64).astype(np.float32)
    inv = (np.float32(10000.0) ** (-np.arange(0, 32, 2, dtype=np.float32) / np.float32(32))).astype(np.float32)
    C = np.ones((64, NK), np.float32)
    Sg = np.zeros((64, NK), np.float32)
    for d in range(64):
        p = pr if d < 32 else pc
        a = (p * inv[d % 16]).astype(np.float32)
        C[d, NCTX:] = np.cos(a)
        sgn = -1.0 if (d % 32) < 16 else 1.0
        Sg[d, NCTX:] = sgn * np.sin(a)
    return np.concatenate([C, C], 0), np.concatenate([Sg, Sg], 0)


def rope_perm_cols():
    idx = np.arange(512)
    d = idx % 64
    partner = np.where((d % 32) < 16, d + 16, d - 16)
    return (idx // 64) * 64 + partner


def make_inputs(inp):
    f32 = np.float32
    common = {}
    common["mod_w"] = np.ascontiguousarray(inp["mod_w"], f32)
    common["modb_col"] = np.ascontiguousarray(inp["mod_b"].reshape(2, 48, 128).transpose(0, 2, 1), f32)
    rows = {}
    for l in range(2):
        rows[f"ln1_g{l}"] = inp["ln1_g"][l]
        rows[f"ln1_b{l}"] = inp["ln1_b"][l]
        rows[f"ln2_g{l}"] = inp["ln2_g"][l]
        rows[f"ln2_b{l}"] = inp["ln2_b"][l]
        rows[f"modb_g1_{l}"] = inp["mod_b"][l, 2 * D:3 * D]
        rows[f"modb_g2_{l}"] = inp["mod_b"][l, 5 * D:6 * D]
    common["rows"] = np.ascontiguousarray(
        np.stack([np.broadcast_to(rows[r][None, :], (128, D)) for r in ROWS]), f32)
    common["cols"] = np.zeros((NCOLS, 128, 1), f32)
    common["cols"][0, :, 0] = inp["da_subln_g"][0]
    common["ident"] = np.eye(128, dtype=f32)
    w_in = np.asarray(inp["da_w_in"][0], f32)
    common["da_w_in"] = np.ascontiguousarray(w_in)
    pc = rope_perm_cols()
    common["da_w_perm"] = np.ascontiguousarray(np.concatenate([w_in[:, 0:512][:, pc], w_in[:, 512:1024][:, pc]], 1))
    common["da_w_out"] = np.ascontiguousarray(inp["da_w_out"][0], f32)
    C, Sg = rope_tables()
    common["rope_c"] = C
    common["rope_s"] = Sg
    lamv = np.stack([inp["da_lam_q1"][0], inp["da_lam_k1"][0], inp["da_lam_q2"][0], inp["da_lam_k2"][0]])
    common["lam"] = np.ascontiguousarray(np.broadcast_to(lamv[None], (128, 4, 64)), f32)
    a = np.arange(64)[:, None].astype(np.float64)
    k1 = np.arange(64)[None, :].astype(np.float64)
    th = 2 * np.pi * a * k1 / 64
    common["f64"] = np.concatenate([np.cos(th), -np.sin(th)], 1).astype(f32)
    p = np.arange(128)[:, None, None].astype(np.float64)
    kk = (np.arange(64)[None, :, None] + 64 * np.arange(128)[None, None, :]).astype(np.float64)
    th = 2 * np.pi * p * kk / 8192
    mr, mi = np.cos(th), -np.sin(th)
    common["fmt"] = np.stack([np.concatenate([mr, mi], 2), np.concatenate([-mi, mr], 2)], 2).astype(f32)
    c = np.arange(128)[:, None].astype(np.float64)
    th = 2 * np.pi * c * c.T / 128
    sc = 1.0 / np.sqrt(8192.0 * 128.0)
    common["fcs"] = np.stack([np.cos(th) * sc, np.sin(th) * sc], 1).astype(f32)
    common["ffn_w_in"] = np.ascontiguousarray(inp["ffn_w_in"], f32)
    common["ffn_w_out"] = np.ascontiguousarray(inp["ffn_w_out"], f32)
    cwb = np.concatenate([inp["ffn_conv_w"], inp["ffn_conv_b"][:, None, :]], 1)
    common["ffn_cw"] = np.ascontiguousarray(cwb.reshape(2, 4, 44, 128).transpose(0, 3, 2, 1), f32)
    common["hy_w_in"] = np.ascontiguousarray(inp["hy_w_in"][0], f32)
    common["hy_w_out"] = np.ascontiguousarray(inp["hy_w_out"][0], f32)
    hcw = np.concatenate([inp["hy_conv_w"][0], inp["hy_conv_b"][0][None, :]], 0)
    common["hy_cw"] = np.ascontiguousarray(hcw.reshape(4, 24, 128).transpose(2, 1, 0), f32)
    min_decay = math.log(1e-2) / 1.5
    max_decay = math.log(1e-2) / 0.3
    deltas = np.abs(np.linspace(min_decay, max_decay, D, dtype=f32))
    common["hy_cols"] = np.ascontiguousarray(
        np.stack([inp["hy_d"][0].reshape(8, 128).T, -deltas.reshape(8, 128).T], -1), f32)
    L = T
    tt = np.linspace(0.0, 1.0, L, dtype=f32)
    wv = (f32(2.0 * math.pi) * np.arange(L, dtype=f32) / f32(L)).astype(f32)
    fb = np.linspace(1e-4, 15, 16, dtype=f32)
    ang = (fb[None, :] * wv[:, None]).astype(f32)
    z = np.concatenate([tt[:, None], np.cos(ang), -np.sin(ang)], -1).astype(f32)
    idx = (L - np.arange(L)) % L
    idx[0] = 0
    zr = z[idx]
    common["hyf_z"] = np.ascontiguousarray(np.stack([z.T, zr.T]), f32)
    common["hyf_t"] = np.ascontiguousarray(np.stack([np.broadcast_to(tt[None], (128, L)),
                                                     np.broadcast_to(tt[idx][None], (128, L))]), f32)
    common["hyf_w1"] = np.ascontiguousarray(inp["hy_f_w1"][0], f32)
    common["hyf_w23"] = np.ascontiguousarray(np.stack([inp["hy_f_w2"][0], inp["hy_f_w3"][0]]), f32)
    common["hyf_w4"] = np.ascontiguousarray(inp["hy_f_w4"][0], f32)
    common["hyf_cols"] = np.ascontiguousarray(np.stack([inp["hy_f_freq"][0], inp["hy_f_b1"][0], inp["hy_f_b2"][0],
                                                        inp["hy_f_b3"][0]], -1), f32)
    a = np.arange(128)[:, None].astype(np.float64)
    kk1 = np.arange(128)[None, :].astype(np.float64)
    th = 2 * np.pi * a * kk1 / 128
    common["dft_f1"] = np.stack([np.concatenate([np.cos(th), -np.sin(th)], 1),
                                 np.concatenate([np.sin(th), np.cos(th)], 1)], 1).astype(f32)
    pp = np.arange(128)[None, :, None].astype(np.float64)
    kfull = (np.arange(128)[:, None, None] + 128 * np.arange(128)[None, None, :]).astype(np.float64)
    th = 2 * np.pi * pp * kfull / 16384.0
    mr, mi = np.cos(th), -np.sin(th)
    common["dft_mt"] = np.concatenate([mr, mi, -mi, mr], 2).astype(f32)
    maps = []
    for b in range(4):
        m = dict(common)
        m["x"] = np.ascontiguousarray(inp["x"][b], f32)
        m["ctx"] = np.ascontiguousarray(inp["ctx"][b], f32)
        cv = np.stack([inp["c"][b].reshape(8, 128).T, inp["c_ctx"].reshape(8, 128).T], -1)
        m["cvec"] = np.ascontiguousarray(cv, f32)
        maps.append(m)
    return maps


def kernel(**inputs):
    inp = {k: np.asarray(v) for k, v in inputs.items()}
    nc, _ = build_program()
    maps = make_inputs(inp)
    res = run_bass_kernel_spmd(nc, maps, core_ids=[0, 1, 2, 3])
    return np.stack([np.asarray(r["out"], np.float32) for r in res.results], 0)
```

```python
import contextlib
import math
import numpy as np
import ml_dtypes
import concourse.bass as bass
import concourse.mybir as mybir
from concourse.bass_utils import run_bass_kernel_spmd

F32 = mybir.dt.float32
BF16 = mybir.dt.bfloat16
AF = mybir.ActivationFunctionType
ALU = mybir.AluOpType

D = 1024
T = 8192
NCTX = 256
NK = T + NCTX
FF = 2816
EPS = 1e-5
ALPHA = 4 ** 0.25
DEBUG = None
DEBUG_OUT = set()


class Buf:
    __slots__ = ("name", "w", "r")

    def __init__(self, name):
        self.name = name
        self.w = None
        self.r = []


class Prog:
    ENGS = ("pe", "act", "dve", "pool", "sp")
    NDMASEM = 8

    def __init__(self, nc):
        self.nc = nc
        self.ops = []
        self.bar_deps = set()
        self.bar_pending = set()
        self.since_bar = {}

    def barrier(self):
        deps = set(self.bar_deps)
        for k, v in self.since_bar.items():
            if k == "dma":
                deps.update(v)
            else:
                deps.add(v)
        self.bar_deps = deps
        self.bar_pending = set(self.ENGS)
        self.since_bar = {}

    def buf(self, name="b"):
        return Buf(name)

    def op(self, eng, fn, reads=(), writes=(), dma=False):
        oid = len(self.ops)
        deps = set()
        for b in reads:
            if b.w is not None:
                deps.add(b.w)
        for b in writes:
            if b.w is not None:
                deps.add(b.w)
            last = {}
            for r in b.r:
                o = self.ops[r]
                if o["dma"]:
                    deps.add(r)
                else:
                    last[o["eng"]] = r
            deps.update(last.values())
        for b in reads:
            b.r.append(oid)
        for b in writes:
            b.w = oid
            b.r = []
        if eng in self.bar_pending:
            deps.update(self.bar_deps)
            self.bar_pending.discard(eng)
        deps.discard(oid)
        if dma:
            self.since_bar.setdefault("dma", []).append(oid)
        else:
            self.since_bar[eng] = oid
        self.ops.append(dict(id=oid, eng=eng, fn=fn, deps=deps, dma=dma))
        return oid

    def dma(self, q, out, in_, reads=(), writes=(), **kw):
        return self.op(q, lambda e: e.dma_start(out=out, in_=in_, **kw), reads, writes, dma=True)

    def emit(self, final_deps):
        nc = self.nc
        ops = self.ops
        ops.append(dict(id=len(ops), eng="sp", fn=None, deps=set(final_deps), dma=False))
        needed = set()
        for o in ops:
            for d in list(o["deps"]):
                od = ops[d]
                if o["eng"] == "pe" and od["eng"] == "pe" and not od["dma"] and not o["dma"]:
                    o["deps"].discard(d)
            needed |= o["deps"]
        st = contextlib.ExitStack()
        sems = {e: st.enter_context(nc.semaphore("s_" + e)) for e in self.ENGS}
        dsems = {q: [st.enter_context(nc.semaphore(f"d_{q}{i}")) for i in range(self.NDMASEM)]
                 for q in ("sp", "act", "pool")}
        cnt = {e: 0 for e in self.ENGS}
        dcnt = {q: 0 for q in dsems}
        for o in ops:
            o["sig"] = None
            o["pre"] = None
            if o["dma"]:
                q = o["eng"]
                k = dcnt[q]
                dcnt[q] += 1
                s = dsems[q][k % self.NDMASEM]
                v = 16 * (k // self.NDMASEM + 1)
                o["sig"] = (s, v, 16)
                if k >= self.NDMASEM:
                    o["pre"] = (s, v - 16)
            elif o["id"] in needed:
                cnt[o["eng"]] += 1
                o["sig"] = (sems[o["eng"]], cnt[o["eng"]], 1)
        self.stats = dict(n=len(ops), cnt=cnt, dcnt=dcnt)
        per = {e: [o for o in ops if o["eng"] == e] for e in self.ENGS}

        def replay(eng_name):
            def run(e):
                waited = {}

                def w(s, v):
                    if waited.get(id(s), 0) < v:
                        e.wait_ge(s, v)
                        waited[id(s)] = v
                for o in per[eng_name]:
                    if o["pre"] is not None:
                        w(*o["pre"])
                    for d in sorted(o["deps"]):
                        s, v, _ = ops[d]["sig"]
                        w(s, v)
                    if o["fn"] is None:
                        continue
                    ins = o["fn"](e)
                    if o["sig"] is not None:
                        ins.then_inc(o["sig"][0], o["sig"][2])
            return run

        with nc.Block() as block:
            block.tensor(replay("pe"))
            block.scalar(replay("act"))
            block.vector(replay("dve"))
            block.gpsimd(replay("pool"))
            block.sync(replay("sp"))
        st.close()


class Ctx:
    pass


def build_program():
    nc = bass.Bass("TRN2", target_bir_lowering=False)
    g = Ctx()
    g.nc = nc
    P = Prog(nc)
    g.P = P
    g.final = []
    g.dbg_stores = []
    g.b_scrB = P.buf("scrB")
    g.b_scrX = P.buf("scrX")
    g.out_stores = []
    g.bk = [P.buf(f"bank{i}") for i in range(8)]

    def din(name, shape, dt=F32):
        return nc.dram_tensor(name, list(shape), dt, kind="ExternalInput").ap()

    def dscr(name, shape, dt):
        kind = "ExternalOutput" if name in DEBUG_OUT else "Internal"
        return nc.dram_tensor(name, list(shape), dt, kind=kind).ap()

    I = Ctx()
    g.I = I
    I.x = din("x", [T, D])
    I.ctx = din("ctx", [NCTX, D])
    I.cvec = din("cvec", [128, 8, 2])
    I.mod_w = din("mod_w", [2, D, 6 * D])
    I.modb_col = din("modb_col", [2, 128, 48])
    I.rows = din("rows", [NROWS, 128, D])
    I.cols = din("cols", [NCOLS, 128, 1])
    I.ident = din("ident", [128, 128])
    I.da_w_in = din("da_w_in", [D, 2048])
    I.da_w_perm = din("da_w_perm", [D, 1024])
    I.da_w_out = din("da_w_out", [D, D])
    I.rope_c = din("rope_c", [128, NK])
    I.rope_s = din("rope_s", [128, NK])
    I.lam = din("lam", [128, 4, 64])
    I.hy_w_in = din("hy_w_in", [D, 3 * D])
    I.hy_w_out = din("hy_w_out", [D, D])
    I.hy_cw = din("hy_cw", [128, 24, 4])
    I.dft_f1 = din("dft_f1", [128, 2, 256])
    I.dft_mt = din("dft_mt", [128, 128, 512])
    I.hyf_z = din("hyf_z", [2, 33, T])
    I.hyf_t = din("hyf_t", [2, 128, T])
    I.hyf_w1 = din("hyf_w1", [33, 64])
    I.hyf_w23 = din("hyf_w23", [2, 64, 64])
    I.hyf_w4 = din("hyf_w4", [64, 2 * D])
    I.hyf_cols = din("hyf_cols", [64, 4])
    I.hy_cols = din("hy_cols", [128, 8, 2])
    I.ffn_w_in = din("ffn_w_in", [2, D, 2 * FF])
    I.ffn_w_out = din("ffn_w_out", [2, FF, D])
    I.ffn_cw = din("ffn_cw", [2, 128, 44, 4])
    I.f64 = din("f64", [64, 128])
    I.fmt = din("fmt", [128, 64, 2, 256])
    I.fcs = din("fcs", [128, 2, 128])
    g.out = nc.dram_tensor("out", [T, D], F32, kind="ExternalOutput").ap()

    S = Ctx()
    g.S = S
    S.QT = dscr("QT", [4, 128, T], BF16)
    S.KT = dscr("KT", [4, 128, NK], BF16)
    S.V = dscr("V", [NK, 512], BF16)
    S.F = dscr("F", [T, 512], BF16)
    S.mixT = dscr("mixT", [8, 128, T], BF16)
    S.x1 = dscr("x1", [T, D], F32)
    S.x2 = dscr("x2", [T, D], F32)
    S.h2T = dscr("h2T", [8, 128, T + 2], BF16)
    S.x0T = dscr("x0T", [8, 128, T], BF16)
    S.vT = dscr("vT", [8, 128, T], BF16)
    S.vtok = dscr("vtok", [T, D], BF16)
    S.ktok = dscr("ktok", [2 * T, D], BF16)
    S.KS = dscr("KS", [8, 128, 128, 256], BF16)
    S.Ztok = dscr("Ztok", [2, 2 * T, D], BF16)
    S.MTb = dscr("MTb", [128, 128, 512], BF16)
    S.y_dbg = dscr("y_dbg", [128, T], F32) if "y_dbg" in DEBUG_OUT else None
    S.knorm_dbg = dscr("knorm_dbg", [128, 8], F32) if "knorm_dbg" in DEBUG_OUT else None
    S.hT_dbg = dscr("hT_dbg", [128, 8, T], F32) if "hT_dbg" in DEBUG_OUT else None

    arena = nc.alloc_sbuf_tensor("arena", [128, ARENA_BYTES // 4], F32)
    g.arena = arena
    g.ps = nc.alloc_psum_tensor("ps", [128, 8, 512], F32)
    g.sb_off = 0

    def sb(shape, dt, reset_to=None):
        n = int(np.prod(shape))
        nbytes = n * (4 if dt == F32 else 2)
        nbytes = (nbytes + 31) // 32 * 32
        off = g.sb_off
        assert off + nbytes <= ARENA_BYTES, (off, nbytes)
        g.sb_off += nbytes
        v = arena[:, off // 4:(off + nbytes) // 4]
        if dt != F32:
            v = v.bitcast(dt)
        v = v[:, 0:n]
        if len(shape) == 2:
            v = v.rearrange("p (a b) -> p a b", b=shape[1])
        elif len(shape) == 3:
            v = v.rearrange("p (a b c) -> p a b c", b=shape[1], c=shape[2])
        return v
    g.sb = sb

    g.ident = sb([128], BF16)
    g.identf = sb([128], F32)
    g.epsT = sb([1], F32)
    bconst = P.buf("const")
    g.bconst = bconst
    P.dma("sp", g.identf, I.ident, writes=[bconst])
    P.op("dve", lambda e: e.tensor_copy(out=g.ident, in_=g.identf), reads=[bconst], writes=[bconst])
    P.op("pool", lambda e: e.memset(g.epsT, EPS), writes=[bconst])
    g.persist_off = g.sb_off

    layer0(g)
    if DEBUG is None:
        g.final.extend(g.out_stores)
    elif not g.final:
        g.final.extend(g.dbg_stores + g.out_stores)

    P.emit(g.final)
    return nc, P


ROWS = ["ln1_g0", "ln1_b0", "ln2_g0", "ln2_b0", "ln1_g1", "ln1_b1", "ln2_g1", "ln2_b1",
        "modb_g1_0", "modb_g2_0", "modb_g1_1", "modb_g2_1"]
NROWS = len(ROWS)
NCOLS = 64
ARENA_BYTES = 212480


def mod_params(g, l):
    nc, P, I, sb = g.nc, g.P, g.I, g.sb
    modA = sb([48, 2], F32)
    G = sb([2, D], F32)
    mark = g.sb_off
    cv = sb([8, 2], F32)
    sc = sb([8, 2], F32)
    screp = sb([8, 128], F32)
    mbc = sb([48], F32)
    wblk = [sb([8, 512], F32) for _ in range(2)]
    grow = sb([2, D], F32)
    b_modA, b_G, b_cv, b_rep = P.buf(), P.buf(), P.buf(), P.buf()
    b_w = [P.buf(), P.buf()]
    b_pm, b_pg = P.buf(), P.buf()
    pM = g.ps[:, 0, 0:96].rearrange("p (j s) -> p j s", s=2)
    P.dma("sp", cv, I.cvec, writes=[b_cv])
    P.dma("sp", mbc, I.modb_col[l], writes=[b_cv])
    r1 = ROWS.index(f"modb_g1_{l}")
    P.dma("sp", grow[:, 0, :], I.rows[r1], writes=[b_cv])
    P.dma("sp", grow[:, 1, :], I.rows[r1 + 1], writes=[b_cv])
    P.op("act", lambda e: e.activation(out=sc, in_=cv, func=AF.Silu), reads=[b_cv], writes=[b_cv])
    for k in range(8):
        P.op("dve", lambda e, k=k: e.tensor_copy(out=screp[:, k, :], in_=sc[:, k, 0:1].to_broadcast([128, 128])),
             reads=[b_cv], writes=[b_rep])
    wv = I.mod_w[l].rearrange("(k p) n -> p k n", p=128)
    for blk in range(12):
        w = wblk[blk % 2]
        bw = b_w[blk % 2]
        P.dma("sp", w, wv[:, :, blk * 512:(blk + 1) * 512], writes=[bw])
        for jj in range(4):
            j = blk * 4 + jj
            for k in range(8):
                P.op("pe", lambda e, k=k, jj=jj, j=j, w=w: e.matmul(
                    pM[:, j, :], lhsT=w[:, k, jj * 128:(jj + 1) * 128], rhs=sc[:, k, :],
                    start=(k == 0), stop=(k == 7)), reads=[bw, b_cv], writes=[b_pm])
        if blk in (4, 5, 10, 11):
            gi = 0 if blk < 6 else 1
            half = blk % 2
            pG = g.ps[:, 1 + half, :]
            for k in range(8):
                P.op("pe", lambda e, k=k, w=w, pG=pG: e.matmul(
                    pG, lhsT=screp[:, k, :], rhs=w[:, k, :], start=(k == 0), stop=(k == 7)),
                    reads=[bw, b_rep], writes=[b_pg])
            P.op("dve", lambda e, gi=gi, half=half, pG=pG: e.tensor_tensor(
                out=G[:, gi, half * 512:(half + 1) * 512], in0=pG, in1=grow[:, gi, half * 512:(half + 1) * 512],
                op=ALU.add), reads=[b_pg, b_cv], writes=[b_G])
    P.op("dve", lambda e: e.tensor_tensor(out=modA, in0=pM, in1=mbc.unsqueeze(2).to_broadcast([128, 48, 2]),
                                          op=ALU.add), reads=[b_pm, b_cv], writes=[b_modA])
    for lo in (8, 32):
        P.op("dve", lambda e, lo=lo: e.tensor_scalar(out=modA[:, lo:lo + 8, :], in0=modA[:, lo:lo + 8, :],
                                                    scalar1=1.0, scalar2=None, op0=ALU.add),
             reads=[b_modA], writes=[b_modA])
    g.sb_off = mark
    P.barrier()
    return modA, G, b_modA, b_G


def ln_to_featmajor(g, src_tiles, ntile, hT, b_hT, scale_cols, bias_cols, b_mod, xt, b_xt, xn, b_xn, tagbufs):
    nc, P = g.nc, g.P
    b_st, b_mv, b_pT = tagbufs
    st, mv, sd = g.ln_st, g.ln_mv, g.ln_sd
    for i in range(ntile):
        for hh in range(2):
            P.op("dve", lambda e, i=i, hh=hh: e.bn_stats(out=st[:, hh, :], in_=xt[i][:, hh * 512:(hh + 1) * 512]),
                 reads=[b_xt[i]], writes=[b_st])
        P.op("dve", lambda e, i=i: e.bn_aggr(out=mv[:, i, :], in_=st), reads=[b_st], writes=[b_mv])
    P.op("act", lambda e: e.activation(out=sd[:, 0:ntile], in_=mv[:, 0:ntile, 1], func=AF.Sqrt,
                                       bias=g.epsT[:, 0:1], scale=1.0), reads=[b_mv, g.bconst], writes=[b_mv])
    P.op("dve", lambda e: e.reciprocal(out=sd[:, 0:ntile], in_=sd[:, 0:ntile]), reads=[b_mv], writes=[b_mv])
    for i in range(ntile):
        P.op("dve", lambda e, i=i: e.tensor_scalar(out=xn[i], in0=xt[i], scalar1=mv[:, i, 0:1],
                                                   scalar2=sd[:, i:i + 1], op0=ALU.subtract, op1=ALU.mult),
             reads=[b_xt[i], b_mv], writes=[b_xn[i]])
    pT = g.ps[:, 0:4, :].bitcast(BF16).rearrange("p b (h n) -> p (b h) n", h=2)
    n = ntile * 128
    for j in range(4):
        for k in (2 * j, 2 * j + 1):
            for i in range(ntile):
                P.op("pe", lambda e, k=k, i=i: e.transpose(out=pT[:, k, i * 128:(i + 1) * 128],
                                                          in_=xn[i][:, k * 128:(k + 1) * 128], identity=g.ident),
                     reads=[b_xn[i], g.bconst], writes=[b_pT[j]])
        for k in (2 * j, 2 * j + 1):
            P.op("act", lambda e, k=k, n=n: e.activation(out=hT[:, k, 0:n], in_=pT[:, k, 0:n], func=AF.Identity,
                                                        scale=scale_cols[k], bias=bias_cols[k]),
                 reads=[b_pT[j], b_mod], writes=[b_hT])


def layer0(g):
    nc, P, I, S, sb = g.nc, g.P, g.I, g.S, g.sb
    g.sb_off = g.persist_off
    modA, G, b_modA, b_G = mod_params(g, 0)
    g.modA0, g.G0, g.b_modA0, g.b_G0 = modA, G, b_modA, b_G
    if DEBUG == "mod":
        d = nc.dram_tensor("dbg_modA", [128, 96], F32, kind="ExternalOutput").ap()
        d2 = nc.dram_tensor("dbg_G", [128, 2 * D], F32, kind="ExternalOutput").ap()
        g.final.append(P.dma("sp", d, modA.rearrange("p j s -> p (j s)"), reads=[b_modA]))
        g.final.append(P.dma("sp", d2, G.rearrange("p a b -> p (a b)"), reads=[b_G]))
        return
    phase_mark = g.sb_off
    NCOLW = 3072
    W = sb([8, NCOLW], BF16)
    b_W = P.buf()
    if DEBUG != "A0":
        P.dma("pool", W[:, :, 0:2048], I.da_w_in.rearrange("(k p) n -> p k n", p=128), writes=[b_W])
        P.dma("pool", W[:, :, 2048:3072], I.da_w_perm.rearrange("(k p) n -> p k n", p=128), writes=[b_W])
    g.ln_st = sb([2, 6], F32)
    g.ln_mv = sb([4, 2], F32)
    g.ln_sd = sb([4], F32)
    xts = [[sb([D], F32) for _ in range(4)] for _ in range(2)]
    b_xts = [[P.buf() for _ in range(4)] for _ in range(2)]
    xn = [sb([D], BF16) for _ in range(4)]
    b_xn = [P.buf() for _ in range(4)]
    hT = sb([8, 512], BF16)
    b_hT = P.buf()
    tag = (P.buf(), P.buf(), g.bk)
    ctab = [sb([512], F32) for _ in range(2)]
    stab = [sb([512], F32) for _ in range(2)]
    b_tab = [P.buf(), P.buf()]
    t1 = sb([512], F32)
    t2 = sb([512], F32)
    b_t1, b_t2 = P.buf(), P.buf()
    qo = [sb([512], BF16) for _ in range(2)]
    b_qo = [P.buf(), P.buf()]
    vo = [sb([512], BF16) for _ in range(2)]
    b_vo = [P.buf(), P.buf()]
    pA, pB = g.ps[:, 4, :], g.ps[:, 5, :]
    b_pA, b_pB = g.bk[4], g.bk[5]
    pV = [g.ps[:, 6, :], g.ps[:, 7, :]]
    b_pV = [g.bk[6], g.bk[7]]
    b_scr = P.buf("scrA")
    g.b_scrA = b_scr
    nblk = 17
    if DEBUG in ("A1", "A0"):
        nblk = 2
    qcnt = 0
    vcnt = 0
    for blk in range(nblk):
        ctxb = (blk == 0)
        ntile = 2 if ctxb else 4
        n = ntile * 128
        par = blk % 2
        src = I.ctx if ctxb else I.x
        t0 = 0 if ctxb else (blk - 1) * 512
        kpos = 0 if ctxb else NCTX + t0
        for i in range(ntile):
            P.dma("sp", xts[par][i], src[t0 + i * 128:t0 + (i + 1) * 128, :], writes=[b_xts[par][i]])
        P.dma("sp", ctab[par][:, 0:n], I.rope_c[:, kpos:kpos + n], writes=[b_tab[par]])
        P.dma("sp", stab[par][:, 0:n], I.rope_s[:, kpos:kpos + n], writes=[b_tab[par]])
        s = 1 if ctxb else 0
        ln_to_featmajor(g, None, ntile, hT, b_hT,
                        [modA[:, 8 + k, s:s + 1] for k in range(8)], [modA[:, k, s:s + 1] for k in range(8)],
                        b_modA, xts[par], b_xts[par], xn, b_xn, tag)
        if S.hT_dbg is not None and not ctxb:
            if not hasattr(g, "hTf"):
                g.hTf = sb([8, 512], F32)
                g.b_hTf = P.buf()
            P.op("dve", lambda e: e.tensor_copy(out=g.hTf, in_=hT), reads=[b_hT], writes=[g.b_hTf])
            g.final.append(P.dma("sp", S.hT_dbg[:, :, t0:t0 + n], g.hTf[:, :, 0:n], reads=[g.b_hTf]))
        if DEBUG == "A0":
            continue
        for which in ((1,) if ctxb else (0, 1)):
            for h in range(4):
                c0 = which * 512 + h * 128
                c1 = 2048 + which * 512 + h * 128
                for k in range(8):
                    P.op("pe", lambda e, k=k, c0=c0, n=n: e.matmul(pA[:, 0:n], lhsT=W[:, k, c0:c0 + 128], rhs=hT[:, k, 0:n],
                                                               start=(k == 0), stop=(k == 7)),
                         reads=[b_W, b_hT], writes=[b_pA])
                for k in range(8):
                    P.op("pe", lambda e, k=k, c1=c1, n=n: e.matmul(pB[:, 0:n], lhsT=W[:, k, c1:c1 + 128], rhs=hT[:, k, 0:n],
                                                               start=(k == 0), stop=(k == 7)),
                         reads=[b_W, b_hT], writes=[b_pB])
                P.op("dve", lambda e, n=n, par=par: e.tensor_tensor(out=t1[:, 0:n], in0=pA[:, 0:n], in1=ctab[par][:, 0:n], op=ALU.mult),
                     reads=[b_pA, b_tab[par]], writes=[b_t1])
                P.op("dve", lambda e, n=n, par=par: e.tensor_tensor(out=t2[:, 0:n], in0=pB[:, 0:n], in1=stab[par][:, 0:n], op=ALU.mult),
                     reads=[b_pB, b_tab[par]], writes=[b_t2])
                qb = qcnt % 2
                qcnt += 1
                P.op("pool", lambda e, n=n, qb=qb: e.tensor_tensor(out=qo[qb][:, 0:n], in0=t1[:, 0:n], in1=t2[:, 0:n], op=ALU.add),
                     reads=[b_t1, b_t2], writes=[b_qo[qb]])
                dst = S.QT[h][:, t0:t0 + n] if which == 0 else S.KT[h][:, kpos:kpos + n]
                g.dbg_stores.append(P.dma("pool", dst, qo[qb][:, 0:n], reads=[b_qo[qb]], writes=[b_scr]))
        for i in range(ntile):
            for which in ((0,) if ctxb else (0, 1)):
                c0 = 1024 + which * 512
                vb = vcnt % 2
                vcnt += 1
                for k in range(8):
                    P.op("pe", lambda e, k=k, i=i, c0=c0, vb=vb: e.matmul(pV[vb], lhsT=hT[:, k, i * 128:(i + 1) * 128],
                                                                     rhs=W[:, k, c0:c0 + 512], start=(k == 0), stop=(k == 7)),
                         reads=[b_W, b_hT], writes=[b_pV[vb]])
                P.op("act", lambda e, vb=vb: e.activation(out=vo[vb], in_=pV[vb], func=AF.Copy),
                     reads=[b_pV[vb]], writes=[b_vo[vb]])
                if which == 0:
                    dst = S.V[kpos + i * 128:kpos + (i + 1) * 128, :]
                else:
                    dst = S.F[t0 + i * 128:t0 + (i + 1) * 128, :]
                g.dbg_stores.append(P.dma("act", dst, vo[vb], reads=[b_vo[vb]], writes=[b_scr]))
    if DEBUG in ("A1", "A0"):
        g.final.extend(g.dbg_stores)
        return
    g.sb_off = phase_mark
    P.barrier()
    attention(g)


def attention(g):
    nc, P, I, S, sb = g.nc, g.P, g.I, g.S, g.sb
    mark = g.sb_off
    KT = sb([NK], BF16)
    Vh = sb([66, 128], BF16)
    b_KT, b_Vh = P.buf(), P.buf()
    QTb = [sb([512], BF16) for _ in range(2)]
    b_Q = [P.buf(), P.buf()]
    Pm = [[sb([512], BF16) for _ in range(2)] for _ in range(2)]
    b_Pm = [[P.buf(), P.buf()], [P.buf(), P.buf()]]
    ones = sb([128], BF16)
    onesf = sb([128], F32)
    ones32 = sb([128], F32)
    zacc = sb([512], F32)
    b_zacc = P.buf()
    lamt = sb([4, 64], F32)
    lt = sb([2, 64], F32)
    ls = sb([2], F32)
    nlam = sb([1], F32)
    gp = sb([1], F32)
    b_c = P.buf()
    P.op("pool", lambda e: e.memset(ones, 1.0), writes=[b_c])
    P.op("pool", lambda e: e.memset(onesf, 1.0 / 128.0), writes=[b_c])
    P.op("pool", lambda e: e.memset(ones32, 1.0), writes=[b_c])
    P.dma("sp", lamt, I.lam, writes=[b_c])
    P.dma("sp", gp, I.cols[0], writes=[b_c])
    P.op("dve", lambda e: e.tensor_tensor(out=lt, in0=lamt[:, 0:4:2, :], in1=lamt[:, 1:4:2, :], op=ALU.mult),
         reads=[b_c], writes=[b_c])
    P.op("dve", lambda e: e.tensor_reduce(out=ls, in_=lt, axis=mybir.AxisListType.X, op=ALU.add), reads=[b_c], writes=[b_c])
    P.op("act", lambda e: e.activation(out=ls, in_=ls, func=AF.Exp), reads=[b_c], writes=[b_c])
    P.op("dve", lambda e: e.tensor_tensor(out=nlam, in0=ls[:, 1:2], in1=ls[:, 0:1], op=ALU.subtract), reads=[b_c], writes=[b_c])
    P.op("dve", lambda e: e.tensor_scalar(out=nlam, in0=nlam, scalar1=-(0.8 - 0.6), scalar2=None, op0=ALU.add), reads=[b_c], writes=[b_c])
    P.op("dve", lambda e: e.tensor_scalar(out=gp, in0=gp, scalar1=1.0 - (0.8 - 0.6), scalar2=None, op0=ALU.mult), reads=[b_c], writes=[b_c])
    r1 = sb([512], F32)
    o1 = sb([512], F32)
    o2 = sb([512], F32)
    sq = sb([512], F32)
    rs = sb([512], F32)
    ob = [sb([512], BF16) for _ in range(2)]
    b_r1, b_o1, b_o2, b_sq, b_rs = P.buf(), P.buf(), P.buf(), P.buf(), P.buf()
    b_ob = [P.buf(), P.buf()]
    bk = [P.buf() for _ in range(8)]
    ps = g.ps
    heads = range(4)
    qbs = range(16)
    if DEBUG == "B1":
        heads, qbs = [0], [0]
    if DEBUG in ("C1", "D1", "H1", "H2", "H3"):
        heads = []
    cnt = 0
    for h in heads:
        P.dma("sp", KT, S.KT[h], reads=[g.b_scrA], writes=[b_KT])
        P.dma("sp", Vh, S.V[:, h * 128:(h + 1) * 128].rearrange("(t p) e -> p t e", p=128), reads=[g.b_scrA], writes=[b_Vh])
        for qb in qbs:
            Q = QTb[cnt % 2]
            bq = b_Q[cnt % 2]
            obuf = ob[cnt % 2]
            b_obuf = b_ob[cnt % 2]
            cnt += 1
            P.dma("sp", Q, S.QT[h][:, qb * 512:(qb + 1) * 512], reads=[g.b_scrA], writes=[bq])
            def s_ops(kt):
                for c in range(2):
                    sbk = c * 2 + kt % 2
                    P.op("pe", lambda e, c=c, kt=kt, sbk=sbk, Q=Q: e.matmul(
                        ps[:, sbk, :], lhsT=KT[c * 64:(c + 1) * 64, kt * 128:(kt + 1) * 128], rhs=Q[c * 64:(c + 1) * 64, :],
                        start=True, stop=True), reads=[b_KT, bq], writes=[bk[sbk]])

            def e_ops(kt):
                for c in range(2):
                    sbk = c * 2 + kt % 2
                    pm = Pm[c][kt % 2]
                    P.op("act", lambda e, sbk=sbk, pm=pm: e.activation(out=pm, in_=ps[:, sbk, :], func=AF.Exp, scale=0.125),
                         reads=[bk[sbk]], writes=[b_Pm[c][kt % 2]])

            def pv_ops(kt):
                for c in range(2):
                    pm = Pm[c][kt % 2]
                    bpm = b_Pm[c][kt % 2]
                    P.op("pe", lambda e, c=c, kt=kt, pm=pm: e.matmul(ps[:, 4 + 2 * c, :], lhsT=Vh[:, kt, :], rhs=pm,
                                                                  start=(kt == 0), stop=(kt == 65)),
                         reads=[b_Vh, bpm], writes=[bk[4 + 2 * c]])
                    if c == 0:
                        P.op("pe", lambda e, c=c, kt=kt, pm=pm: e.matmul(ps[:, 5 + 2 * c, :], lhsT=ones, rhs=pm,
                                                                      start=(kt == 0), stop=(kt == 65)),
                             reads=[b_c, bpm], writes=[bk[5 + 2 * c]])
                    elif kt == 0:
                        P.op("dve", lambda e, pm=pm: e.tensor_copy(out=zacc, in_=pm), reads=[bpm], writes=[b_zacc])
                    else:
                        P.op("dve", lambda e, pm=pm: e.tensor_tensor(out=zacc, in0=zacc, in1=pm, op=ALU.add),
                             reads=[bpm, b_zacc], writes=[b_zacc])
            s_ops(0)
            for kt in range(66):
                if kt + 1 < 66:
                    s_ops(kt + 1)
                e_ops(kt)
                pv_ops(kt)
            P.op("pe", lambda e: e.matmul(ps[:, 7, :], lhsT=ones32, rhs=zacc, start=True, stop=True),
                 reads=[b_c, b_zacc], writes=[bk[7]])
            P.op("dve", lambda e: e.reciprocal(out=r1, in_=ps[:, 5, :]), reads=[bk[5]], writes=[b_r1])
            P.op("dve", lambda e: e.tensor_tensor(out=o1, in0=ps[:, 4, :], in1=r1, op=ALU.mult), reads=[bk[4], b_r1], writes=[b_o1])
            P.op("dve", lambda e: e.reciprocal(out=r1, in_=ps[:, 7, :]), reads=[bk[7], b_r1], writes=[b_r1])
            P.op("dve", lambda e: e.tensor_tensor(out=o2, in0=ps[:, 6, :], in1=r1, op=ALU.mult), reads=[bk[6], b_r1], writes=[b_o2])
            P.op("dve", lambda e: e.scalar_tensor_tensor(out=o1, in0=o2, scalar=nlam[:, 0:1], in1=o1, op0=ALU.mult, op1=ALU.add),
                 reads=[b_o2, b_o1, b_c], writes=[b_o1])
            P.op("pool", lambda e: e.tensor_tensor(out=sq, in0=o1, in1=o1, op=ALU.mult), reads=[b_o1], writes=[b_sq])
            P.op("pe", lambda e: e.matmul(ps[:, 0, :], lhsT=onesf, rhs=sq, start=True, stop=True), reads=[b_sq, b_c], writes=[bk[0]])
            P.op("act", lambda e: e.activation(out=rs, in_=ps[:, 0, :], func=AF.Ln, bias=g.epsT[:, 0:1], scale=1.0),
                 reads=[bk[0], g.bconst], writes=[b_rs])
            P.op("act", lambda e: e.activation(out=rs, in_=rs, func=AF.Exp, scale=-0.5), reads=[b_rs], writes=[b_rs])
            P.op("dve", lambda e, obuf=obuf: e.scalar_tensor_tensor(out=obuf, in0=o1, scalar=gp[:, 0:1], in1=rs, op0=ALU.mult, op1=ALU.mult),
                 reads=[b_o1, b_rs, b_c], writes=[b_obuf])
            g.dbg_stores.append(P.dma("pool", S.mixT[h][:, qb * 512:(qb + 1) * 512], obuf, reads=[b_obuf], writes=[g.b_scrB]))
    if DEBUG == "B1":
        g.final.extend(g.dbg_stores)
        return
    g.sb_off = mark
    P.barrier()
    fourier(g)


def fourier(g):
    nc, P, I, S, sb = g.nc, g.P, g.I, g.S, g.sb
    mark = g.sb_off
    F64 = sb([128], BF16)
    MT = sb([64, 2, 256], BF16)
    CS = sb([2, 128], BF16)
    b_t = P.buf()
    P.dma("pool", F64[0:64, :], I.f64, writes=[b_t])
    P.dma("pool", MT, I.fmt, writes=[b_t])
    P.dma("pool", CS, I.fcs, writes=[b_t])
    G = sb([128, 128], BF16)
    Y = sb([2, 64, 128], BF16)
    X = sb([64, 256], BF16)
    fm = sb([T], BF16)
    b_G, b_Y, b_X, b_fm = P.buf(), P.buf(), P.buf(), P.buf()
    bk = [P.buf() for _ in range(8)]
    ps = g.ps
    groups = range(4)
    if DEBUG == "C1":
        groups = [0]
    if DEBUG in ("D1", "H1", "H2", "H3"):
        groups = []
    for gi in groups:
        P.dma("sp", G[0:64], S.F[:, gi * 128:(gi + 1) * 128].rearrange("(a p) c -> a p c", p=128),
              reads=[g.b_scrA], writes=[b_G])
        for c0 in range(0, 128, 4):
            b = (c0 // 4) % 2
            for cc in range(4):
                P.op("pe", lambda e, c=c0 + cc, cc=cc, b=b: e.matmul(ps[:, b, cc * 128:(cc + 1) * 128], lhsT=G[0:64, :, c],
                                                                 rhs=F64[0:64, :], start=True, stop=True),
                     reads=[b_G, b_t], writes=[bk[b]])
            P.op("act", lambda e, c0=c0, b=b: e.activation(
                out=Y[:, :, :, c0:c0 + 4], in_=ps[:, b, :].rearrange("p (cc r k) -> p r k cc", cc=4, r=2),
                func=AF.Copy), reads=[bk[b]], writes=[b_Y])
        for k1 in range(64):
            b = 2 + (k1 // 2) % 2
            o = (k1 % 2) * 256
            P.op("pe", lambda e, k1=k1, b=b, o=o: e.matmul(ps[:, b, o:o + 256], lhsT=Y[:, 0, k1, :], rhs=MT[:, k1, 0, :],
                                                       start=True, stop=False), reads=[b_Y, b_t], writes=[bk[b]])
            P.op("pe", lambda e, k1=k1, b=b, o=o: e.matmul(ps[:, b, o:o + 256], lhsT=Y[:, 1, k1, :], rhs=MT[:, k1, 1, :],
                                                       start=False, stop=True), reads=[b_Y, b_t], writes=[bk[b]])
            if k1 % 2 == 1:
                P.op("dve", lambda e, k1=k1, b=b: e.tensor_copy(out=X[:, k1 - 1:k1 + 1, :],
                                                              in_=ps[:, b, :].rearrange("p (a x) -> p a x", a=2)),
                     reads=[bk[b]], writes=[b_X])
        fmv = fm.rearrange("p (k2 k1) -> p k1 k2", k1=64)
        for q in range(16):
            b = 4 + q % 2
            P.op("pe", lambda e, q=q, b=b: e.matmul(ps[:, b, :], lhsT=CS[:, 0, :], rhs=X[:, 4 * q:4 * q + 4, 0:128],
                                                start=True, stop=False), reads=[b_X, b_t], writes=[bk[b]])
            P.op("pe", lambda e, q=q, b=b: e.matmul(ps[:, b, :], lhsT=CS[:, 1, :], rhs=X[:, 4 * q:4 * q + 4, 128:256],
                                                start=False, stop=True), reads=[b_X, b_t], writes=[bk[b]])
            P.op("act", lambda e, q=q, b=b: e.activation(out=fmv[:, 4 * q:4 * q + 4, :],
                                                     in_=ps[:, b, :].rearrange("p (a x) -> p a x", a=4), func=AF.Copy),
                 reads=[bk[b]], writes=[b_fm])
        g.dbg_stores.append(P.dma("act", S.mixT[4 + gi], fm, reads=[b_fm], writes=[g.b_scrB]))
    if DEBUG == "C1":
        g.final.extend(g.dbg_stores)
        return
    g.sb_off = mark
    P.barrier()
    phase_D(g, 0, I.x, I.da_w_out, g.modA0, g.G0, g.b_modA0, g.b_G0)
    phase_E(g, 0, S.x2, g.modA0, g.G0, g.b_modA0, g.b_G0)
    if DEBUG in ("D1",):
        return
    layer1(g)


def fm_to_tok(g, src, b_src, dst_rows, tb, b_tb, bank, b_scr):
    P = g.P
    pT = g.ps[:, bank, :].bitcast(BF16)[:, 0:512].rearrange("p (j c) -> p j c", c=128)
    for j in range(4):
        P.op("pe", lambda e, j=j: e.transpose(out=pT[:, j, :], in_=src[:, j * 128:(j + 1) * 128], identity=g.ident),
             reads=[b_src, g.bconst], writes=[g.bk[bank]])
    P.op("act", lambda e: e.activation(out=tb, in_=pT, func=AF.Copy), reads=[g.bk[bank]], writes=[b_tb])
    g.dbg_stores.append(P.dma("act", dst_rows, tb, reads=[b_tb], writes=[b_scr]))


def layer1(g):
    nc, P, I, S, sb = g.nc, g.P, g.I, g.S, g.sb
    g.sb_off = g.persist_off
    P.barrier()
    modA, G, b_modA, b_G = mod_params(g, 1)
    g.knorm = sb([8], F32)
    g.b_knorm = P.buf()
    mark = g.sb_off
    g.ln_st = sb([2, 6], F32)
    g.ln_mv = sb([4, 2], F32)
    g.ln_sd = sb([4], F32)
    xts = [sb([D], F32) for _ in range(4)]
    b_xts = [P.buf() for _ in range(4)]
    xn = [sb([D], BF16) for _ in range(4)]
    b_xn = [P.buf() for _ in range(4)]
    hT = sb([8, 512], BF16)
    b_hT = P.buf()
    nblk = 16 if DEBUG not in ("H1", "H2", "H3") else 1
    g.b_scrH = P.buf()
    for blk in range(nblk):
        t0 = blk * 512
        for i in range(4):
            P.dma("sp", xts[i], S.x2[t0 + i * 128:t0 + (i + 1) * 128, :], reads=[g.b_scrX], writes=[b_xts[i]])
        ln_to_featmajor(g, None, 4, hT, b_hT, [modA[:, 8 + k, 0:1] for k in range(8)], [modA[:, k, 0:1] for k in range(8)],
                        b_modA, xts, b_xts, xn, b_xn, (P.buf(), P.buf(), g.bk))
        P.dma("act", S.h2T[:, :, 1 + t0:1 + t0 + 512].rearrange("k p n -> p k n"), hT, reads=[b_hT], writes=[g.b_scrH])
    g.sb_off = mark
    P.barrier()
    W = sb([8, 3 * D], BF16)
    cw = sb([24, 4], F32)
    b_w, b_rows = P.buf(), P.buf()
    for k in range(8):
        P.dma("pool", W[:, k, :], I.hy_w_in[k * 128:(k + 1) * 128, :], writes=[b_w])
    P.dma("sp", cw, I.hy_cw, writes=[b_rows])
    hb = sb([8, 514], BF16)
    b_hb = P.buf()
    cb = [sb([512], F32) for _ in range(3)]
    b_cb = [P.buf() for _ in range(3)]
    x0b = sb([512], BF16)
    vvb = sb([512], BF16)
    b_x0b, b_vvb = P.buf(), P.buf()
    tb = sb([4, 128], BF16)
    b_tb = P.buf()
    ps, bk = g.ps, g.bk
    for blk in range(nblk):
        t0 = blk * 512
        P.dma("sp", hb, S.h2T[:, :, t0:t0 + 514].rearrange("k p n -> p k n"), reads=[g.b_scrH], writes=[b_hb])
        for i in range(8):
            for part in range(3):
                ch = part * 8 + i
                ub = (3 * i + part) % 2
                hcol = (3 * i + part) % 64
                hbk = 2 if ub == 0 else 4
                for k in range(8):
                    P.op("pe", lambda e, k=k, ch=ch, ub=ub: e.matmul(ps[:, ub, :], lhsT=W[:, k, ch * 128:(ch + 1) * 128],
                                                                 rhs=hb[:, k, 1:513], start=(k == 0), stop=(k == 7)),
                         reads=[b_w, b_hb], writes=[bk[ub]])
                for k in range(8):
                    P.op("pe", lambda e, k=k, ch=ch, hcol=hcol, hbk=hbk: e.matmul(ps[:, hbk, 2 * hcol:2 * hcol + 2], lhsT=W[:, k, ch * 128:(ch + 1) * 128],
                                                                     rhs=hb[:, k, 0:514:513], start=(k == 0), stop=(k == 7)),
                         reads=[b_w, b_hb], writes=[bk[hbk]])
                c = cb[part]
                bc = b_cb[part]
                uA = ps[:, ub, :]
                uB = ps[:, hbk, 2 * hcol:2 * hcol + 2]
                P.op("act", lambda e, c=c, uA=uA, ch=ch: e.activation(out=c, in_=uA, func=AF.Identity, scale=cw[:, ch, 1:2], bias=cw[:, ch, 3:4]),
                     reads=[bk[ub], b_rows], writes=[bc])
                P.op("dve", lambda e, c=c, uA=uA, ch=ch: e.scalar_tensor_tensor(out=c[:, 1:512], in0=uA[:, 0:511], scalar=cw[:, ch, 0:1], in1=c[:, 1:512],
                                                                          op0=ALU.mult, op1=ALU.add), reads=[bk[ub], b_rows, bc], writes=[bc])
                P.op("dve", lambda e, c=c, uA=uA, ch=ch: e.scalar_tensor_tensor(out=c[:, 0:511], in0=uA[:, 1:512], scalar=cw[:, ch, 2:3], in1=c[:, 0:511],
                                                                          op0=ALU.mult, op1=ALU.add), reads=[bk[ub], b_rows, bc], writes=[bc])
                P.op("dve", lambda e, c=c, uB=uB, ch=ch: e.scalar_tensor_tensor(out=c[:, 0:1], in0=uB[:, 0:1], scalar=cw[:, ch, 0:1], in1=c[:, 0:1],
                                                                          op0=ALU.mult, op1=ALU.add), reads=[bk[hbk], b_rows, bc], writes=[bc])
                P.op("dve", lambda e, c=c, uB=uB, ch=ch: e.scalar_tensor_tensor(out=c[:, 511:512], in0=uB[:, 1:2], scalar=cw[:, ch, 2:3], in1=c[:, 511:512],
                                                                          op0=ALU.mult, op1=ALU.add), reads=[bk[hbk], b_rows, bc], writes=[bc])
            P.op("act", lambda e: e.activation(out=x0b, in_=cb[0], func=AF.Copy), reads=[b_cb[0]], writes=[b_x0b])
            P.op("pool", lambda e: e.tensor_tensor(out=vvb, in0=cb[2], in1=cb[1], op=ALU.mult), reads=[b_cb[2], b_cb[1]], writes=[b_vvb])
            g.dbg_stores.append(P.dma("act", S.x0T[i][:, t0:t0 + 512], x0b, reads=[b_x0b], writes=[g.b_scrH]))
            g.dbg_stores.append(P.dma("pool", S.vT[i][:, t0:t0 + 512], vvb, reads=[b_vvb], writes=[g.b_scrH]))
            fm_to_tok(g, vvb, b_vvb, S.vtok[t0:t0 + 512, i * 128:(i + 1) * 128].rearrange("(j p) c -> p j c", p=128),
                      tb, b_tb, 3, g.b_scrH)
    if DEBUG == "H3":
        zt_ = sb([T - 512], BF16)
        b_z_ = P.buf()
        P.op("pool", lambda e: e.memset(zt_, 0.0), writes=[b_z_])
        for r in range(4, 64):
            P.dma("pool", S.vtok[r * 128:(r + 1) * 128, :], zt_[:, 0:1024], reads=[b_z_], writes=[g.b_scrH])
        P.dma("pool", S.x0T[0][:, 512:T], zt_, reads=[b_z_], writes=[g.b_scrH])
        P.dma("pool", S.vT[0][:, 512:T], zt_, reads=[b_z_], writes=[g.b_scrH])
    g.sb_off = mark
    P.barrier()
    if DEBUG == "H1":
        return
    hyena_filter(g)
    if DEBUG == "H2":
        return
    hyena_conv(g)
    if DEBUG == "H3":
        return
    phase_D(g, 1, S.x2, I.hy_w_out, modA, G, b_modA, b_G)
    phase_E(g, 1, g.out, modA, G, b_modA, b_G)


def dft16k(g, planes, A, consumer, groups, b_src, preload=None):
    nc, P, I, S, sb = g.nc, g.P, g.I, g.S, g.sb
    npl = len(planes)
    G = [sb([128, 128], BF16) for _ in range(npl)]
    Y = sb([2, 128, 128], BF16)
    mt = [sb([2, 256], BF16) for _ in range(4)]
    b_G, b_Y = P.buf(), P.buf()
    b_mt = [P.buf() for _ in range(4)]
    ps, bk = g.ps, g.bk
    for gi in groups:
        for pl in range(npl):
            v = planes[pl][:, gi * 128:(gi + 1) * 128].rearrange("(a p) c -> a p c", p=128)
            for q in range(4):
                P.dma("sp", G[pl][0:A, q * 32:(q + 1) * 32, :], v[:, q * 32:(q + 1) * 32, :], reads=[b_src], writes=[b_G])
        for c0 in range(0, 128, 2):
            b = (c0 // 2) % 2
            for cc in range(2):
                for pl in range(npl):
                    P.op("pe", lambda e, c=c0 + cc, cc=cc, b=b, pl=pl: e.matmul(
                        ps[:, b, cc * 256:(cc + 1) * 256], lhsT=G[pl][0:A, :, c], rhs=g.F1[0:A, pl, :],
                        start=(pl == 0), stop=(pl == npl - 1)), reads=[b_G, g.b_dft], writes=[bk[b]])
            P.op("act", lambda e, c0=c0, b=b: e.activation(
                out=Y[:, :, :, c0:c0 + 2], in_=ps[:, b, :].rearrange("p (cc r k) -> p r k cc", cc=2, r=2),
                func=AF.Copy), reads=[bk[b]], writes=[b_Y])
        def ld(k1):
            P.dma("sp", mt[k1 % 4], S.MTb[k1].rearrange("p (v n) -> p v n", v=2), reads=[g.b_dft], writes=[b_mt[k1 % 4]])
            if preload is not None:
                preload(gi, k1)
        for k1 in range(3):
            ld(k1)
        for k1 in range(128):
            par = k1 % 4
            b = 2 + k1 % 2
            if k1 + 3 < 128:
                ld(k1 + 3)
            P.op("pe", lambda e, k1=k1, b=b, par=par: e.matmul(ps[:, b, 0:256], lhsT=Y[:, 0, k1, :], rhs=mt[par][:, 0, :],
                                                           start=True, stop=False), reads=[b_Y, b_mt[par]], writes=[bk[b]])
            P.op("pe", lambda e, k1=k1, b=b, par=par: e.matmul(ps[:, b, 0:256], lhsT=Y[:, 1, k1, :], rhs=mt[par][:, 1, :],
                                                           start=False, stop=True), reads=[b_Y, b_mt[par]], writes=[bk[b]])
            consumer(gi, k1, ps[:, b, 0:256], b)


def hyena_conv(g):
    nc, P, I, S, sb = g.nc, g.P, g.I, g.S, g.sb
    ps, bk = g.ps, g.bk
    base = g.sb_off
    groups = range(8) if DEBUG != "H3" else [0]
    g.F1 = sb([2, 256], BF16)
    g.b_dft = P.buf()
    P.dma("pool", g.F1, I.dft_f1, writes=[g.b_dft])
    mark = g.sb_off
    mf = [sb([512], F32) for _ in range(2)]
    mb = [sb([512], BF16) for _ in range(2)]
    b_mf, b_mb = [P.buf(), P.buf()], [P.buf(), P.buf()]
    for k1 in range(128):
        par = k1 % 2
        P.dma("sp", mf[par], I.dft_mt[k1], writes=[b_mf[par]])
        P.op("act", lambda e, par=par: e.activation(out=mb[par], in_=mf[par], func=AF.Copy), reads=[b_mf[par]], writes=[b_mb[par]])
        P.dma("act", S.MTb[k1], mb[par], reads=[b_mb[par]], writes=[g.b_dft])
    g.sb_off = mark
    P.barrier()
    kt = [sb([256], BF16) for _ in range(2)]
    b_kt = [P.buf(), P.buf()]
    g.b_scrKS = P.buf()

    def cons_filter(gi, k1, pz, bank):
        par = k1 % 2
        P.op("act", lambda e: e.activation(out=kt[par], in_=pz, func=AF.Copy), reads=[bk[bank]], writes=[b_kt[par]])
        P.dma("act", S.KS[gi][:, k1, :], kt[par], reads=[b_kt[par]], writes=[g.b_scrKS])
    dft16k(g, [S.ktok], 128, cons_filter, groups, g.b_scrK)
    g.sb_off = mark
    P.barrier()
    ksb = [sb([256], BF16) for _ in range(4)]
    b_ksb = [P.buf() for _ in range(4)]
    ta, tb_, tc, td = sb([128], F32), sb([128], F32), sb([128], F32), sb([128], F32)
    b_ta, b_tb2, b_tc, b_td = P.buf(), P.buf(), P.buf(), P.buf()
    zz = [sb([2, 128], BF16) for _ in range(2)]
    b_zz = [P.buf(), P.buf()]
    zt = [sb([2, 128], BF16) for _ in range(2)]
    b_zt = [P.buf(), P.buf()]
    g.b_scrZ = P.buf()

    def cons_signal(gi, k1, pz, bank):
        par = k1 % 2
        K = ksb[k1 % 4]
        b_K = b_ksb[k1 % 4]
        xr, xi = pz[:, 0:128], pz[:, 128:256]
        kr, ki = K[:, 0:128], K[:, 128:256]
        P.op("dve", lambda e: e.tensor_tensor(out=ta, in0=xr, in1=kr, op=ALU.mult), reads=[bk[bank], b_K], writes=[b_ta])
        P.op("dve", lambda e: e.tensor_tensor(out=tb_, in0=xi, in1=ki, op=ALU.mult), reads=[bk[bank], b_K], writes=[b_tb2])
        P.op("dve", lambda e: e.tensor_tensor(out=tc, in0=xr, in1=ki, op=ALU.mult), reads=[bk[bank], b_K], writes=[b_tc])
        P.op("dve", lambda e: e.tensor_tensor(out=td, in0=xi, in1=kr, op=ALU.mult), reads=[bk[bank], b_K], writes=[b_td])
        Z = zz[par]
        P.op("pool", lambda e: e.tensor_tensor(out=Z[:, 0, :], in0=ta, in1=tb_, op=ALU.subtract), reads=[b_ta, b_tb2], writes=[b_zz[par]])
        P.op("dve", lambda e: e.scalar_tensor_tensor(out=Z[:, 1, :], in0=tc, scalar=-1.0, in1=td, op0=ALU.mult, op1=ALU.subtract),
             reads=[b_tc, b_td], writes=[b_zz[par]])
        tbank = 4 + par
        pT = ps[:, tbank, :].bitcast(BF16)[:, 0:256].rearrange("p (v c) -> p v c", v=2)
        for v in range(2):
            P.op("pe", lambda e, v=v: e.transpose(out=pT[:, v, :], in_=Z[:, v, :], identity=g.ident),
                 reads=[b_zz[par], g.bconst], writes=[bk[tbank]])
        P.op("act", lambda e: e.activation(out=zt[par], in_=pT, func=AF.Copy), reads=[bk[tbank]], writes=[b_zt[par]])
        for v in range(2):
            dst = S.Ztok[v][:, gi * 128:(gi + 1) * 128].rearrange("(k2 k1) c -> k1 k2 c", k1=128)[k1]
            P.dma("act", dst, zt[par][:, v, :], reads=[b_zt[par]], writes=[g.b_scrZ])
    def pre_signal(gi, k1):
        P.dma("sp", ksb[k1 % 4], S.KS[gi][:, k1, :], reads=[g.b_scrKS], writes=[b_ksb[k1 % 4]])
    dft16k(g, [S.vtok], 64, cons_signal, groups, g.b_scrH, preload=pre_signal)
    g.sb_off = mark
    P.barrier()
    yt = sb([T], F32)
    vt = sb([T], BF16)
    x0t = sb([T], BF16)
    ot = vt
    hc = sb([8, 2], F32)
    rk = sb([8], F32)
    b_yt, b_vt, b_ot, b_c = P.buf(), P.buf(), P.buf(), P.buf()
    P.dma("sp", hc, I.hy_cols, writes=[b_c])
    P.op("dve", lambda e: e.reciprocal(out=rk, in_=g.knorm), reads=[g.b_knorm], writes=[b_c])
    P.op("dve", lambda e: e.tensor_scalar(out=rk, in0=rk, scalar1=1.0 / 16384.0, scalar2=None, op0=ALU.mult), reads=[b_c], writes=[b_c])
    ytv = yt.rearrange("p (n2 n1) -> p n1 n2", n1=128)

    def cons_inv(gi, k1, pz, bank):
        P.op("act", lambda e: e.activation(out=ytv[:, k1, :], in_=pz[:, 0:64], func=AF.Copy), reads=[bk[bank]], writes=[b_yt])
        if k1 == 127:
            P.dma("sp", vt, S.vT[gi], reads=[g.b_scrH], writes=[b_vt])
            P.dma("sp", x0t, S.x0T[gi], reads=[g.b_scrH], writes=[b_vt])
            if S.y_dbg is not None and gi == 0:
                g.dbg_stores.append(P.dma("sp", S.y_dbg, yt, reads=[b_yt]))
            P.op("dve", lambda e: e.tensor_scalar(out=yt, in0=yt, scalar1=rk[:, gi:gi + 1], scalar2=None, op0=ALU.mult),
                 reads=[b_yt, b_c], writes=[b_yt])
            P.op("dve", lambda e: e.scalar_tensor_tensor(out=yt, in0=vt, scalar=hc[:, gi, 0:1], in1=yt, op0=ALU.mult, op1=ALU.add),
                 reads=[b_vt, b_yt, b_c], writes=[b_yt])
            P.op("dve", lambda e: e.tensor_tensor(out=ot, in0=yt, in1=x0t, op=ALU.mult), reads=[b_yt, b_vt], writes=[b_vt])
            g.dbg_stores.append(P.dma("pool", S.mixT[gi], ot, reads=[b_vt], writes=[g.b_scrB]))
    dft16k(g, [S.Ztok[0], S.Ztok[1]], 128, cons_inv, groups, g.b_scrZ)
    g.sb_off = base
    P.barrier()


def hyena_filter(g):
    nc, P, I, S, sb = g.nc, g.P, g.I, g.S, g.sb
    mark = g.sb_off
    w1 = sb([64], F32)
    w23 = sb([2, 64], F32)
    w4 = sb([2 * D], F32)
    fc = sb([4], F32)
    sc = sb([4], F32)
    hc = sb([8, 2], F32)
    b_w = P.buf()
    P.dma("sp", w1[0:33, :], I.hyf_w1, writes=[b_w])
    P.dma("sp", w23[0:64], I.hyf_w23.rearrange("l k n -> k l n"), writes=[b_w])
    P.dma("sp", w4[0:64, :], I.hyf_w4, writes=[b_w])
    P.dma("sp", fc[0:64, :], I.hyf_cols, writes=[b_w])
    P.dma("sp", hc, I.hy_cols, writes=[b_w])
    P.op("dve", lambda e: e.tensor_scalar(out=sc[0:64, 0:1], in0=fc[0:64, 0:1], scalar1=1.0 / 3.0, scalar2=None, op0=ALU.mult),
         reads=[b_w], writes=[b_w])
    P.op("dve", lambda e: e.tensor_scalar(out=sc[0:64, 1:4], in0=fc[0:64, 1:4], scalar1=sc[0:64, 0:1], scalar2=None, op0=ALU.mult),
         reads=[b_w], writes=[b_w])
    P.op("pool", lambda e: e.memset(g.knorm, 0.0), writes=[g.b_knorm])
    zt = [sb([512], F32) for _ in range(2)]
    tv = [sb([512], F32) for _ in range(2)]
    b_zt = [P.buf(), P.buf()]
    hs = sb([512], F32)
    ht = sb([512], F32)
    hx = sb([512], F32)
    b_hs, b_ht, b_hx = P.buf(), P.buf(), P.buf()
    dec = [sb([512], F32) for _ in range(2)]
    kr = [sb([512], F32) for _ in range(2)]
    krb = [sb([512], BF16) for _ in range(2)]
    part = [sb([1], F32) for _ in range(2)]
    b_dec, b_kr, b_krb, b_part = [[P.buf(), P.buf()] for _ in range(4)]
    tb = [sb([4, 128], BF16) for _ in range(2)]
    b_tb = [P.buf(), P.buf()]
    ps, bk = g.ps, g.bk
    g.b_scrK = P.buf()
    it = 0
    for dr in range(2):
        for blk in range(16):
            n0 = blk * 512
            par = it % 2
            it += 1
            P.dma("sp", zt[par][0:33, :], I.hyf_z[dr][:, n0:n0 + 512], writes=[b_zt[par]])
            P.dma("sp", tv[par], I.hyf_t[dr][:, n0:n0 + 512], writes=[b_zt[par]])
            src, b_src = zt[par], b_zt[par]
            kdim = 33
            for layer in range(3):
                lw = w1[0:33, :] if layer == 0 else w23[0:64, layer - 1, :]
                P.op("pe", lambda e, lw=lw, src=src, kdim=kdim: e.matmul(ps[0:64, 0, :], lhsT=lw, rhs=src[0:kdim, :], start=True, stop=True),
                     reads=[b_w, b_src], writes=[bk[0]])
                P.op("act", lambda e, layer=layer: e.activation(out=hs[0:64, :], in_=ps[0:64, 0, :], func=AF.Sin,
                                                              scale=sc[0:64, 0:1], bias=sc[0:64, 1 + layer:2 + layer]),
                     reads=[bk[0], b_w], writes=[b_hs])
                P.op("dve", lambda e: e.tensor_tensor(out=ht[0:64, :], in0=hs[0:64, :], in1=hs[0:64, :], op=ALU.mult), reads=[b_hs], writes=[b_ht])
                P.op("dve", lambda e: e.tensor_scalar(out=ht[0:64, :], in0=ht[0:64, :], scalar1=-4.0, scalar2=3.0, op0=ALU.mult, op1=ALU.add),
                     reads=[b_ht], writes=[b_ht])
                P.op("dve", lambda e: e.tensor_tensor(out=hx[0:64, :], in0=hs[0:64, :], in1=ht[0:64, :], op=ALU.mult), reads=[b_hs, b_ht], writes=[b_hx])
                src, b_src, kdim = hx, b_hx, 64
            for ci in range(8):
                bank = 1 + ci % 2
                c0 = dr * D + ci * 128
                P.op("pe", lambda e, c0=c0, bank=bank: e.matmul(ps[:, bank, :], lhsT=w4[0:64, c0:c0 + 128], rhs=hx[0:64, :], start=True, stop=True),
                     reads=[b_w, b_hx], writes=[bk[bank]])
                q = ci % 2
                P.op("act", lambda e, ci=ci, par=par, q=q: e.activation(out=dec[q], in_=tv[par], func=AF.Exp, scale=hc[:, ci, 1:2]),
                     reads=[b_zt[par], b_w], writes=[b_dec[q]])
                P.op("dve", lambda e, bank=bank, q=q: e.tensor_tensor(out=kr[q], in0=ps[:, bank, :], in1=dec[q], op=ALU.mult),
                     reads=[bk[bank], b_dec[q]], writes=[b_kr[q]])
                if dr == 1 and blk == 0:
                    P.op("dve", lambda e, q=q: e.memset(kr[q][:, 0:1], 0.0), reads=[b_kr[q]], writes=[b_kr[q]])
                P.op("dve", lambda e, q=q: e.tensor_reduce(out=part[q], in_=kr[q], axis=mybir.AxisListType.X, op=ALU.add, apply_absolute_value=True),
                     reads=[b_kr[q]], writes=[b_part[q]])
                P.op("dve", lambda e, ci=ci, q=q: e.tensor_tensor(out=g.knorm[:, ci:ci + 1], in0=g.knorm[:, ci:ci + 1], in1=part[q], op=ALU.add),
                     reads=[b_part[q], g.b_knorm], writes=[g.b_knorm])
                P.op("act", lambda e, q=q: e.activation(out=krb[q], in_=kr[q], func=AF.Copy), reads=[b_kr[q]], writes=[b_krb[q]])
                r0 = dr * T + n0
                fm_to_tok(g, krb[ci % 2], b_krb[ci % 2], S.ktok[r0:r0 + 512, ci * 128:(ci + 1) * 128].rearrange("(j p) c -> p j c", p=128),
                          tb[ci % 2], b_tb[ci % 2], 3 + (ci % 2), g.b_scrK)
    if S.knorm_dbg is not None:
        g.dbg_stores.append(P.dma("sp", S.knorm_dbg, g.knorm, reads=[g.b_knorm]))
    g.sb_off = mark
    P.barrier()


def ln_stats(g, tile, b_tile, mv, sd, b_s):
    P = g.P
    st = g.ln_st
    for hh in range(2):
        P.op("dve", lambda e, hh=hh: e.bn_stats(out=st[:, hh, :], in_=tile[:, hh * 512:(hh + 1) * 512]),
             reads=[b_tile], writes=[b_s])
    P.op("dve", lambda e: e.bn_aggr(out=mv, in_=st), reads=[b_s], writes=[b_s])
    P.op("act", lambda e: e.activation(out=sd, in_=mv[:, 1:2], func=AF.Sqrt, bias=g.epsT[:, 0:1], scale=1.0),
         reads=[b_s, g.bconst], writes=[b_s])
    P.op("dve", lambda e: e.reciprocal(out=sd, in_=sd), reads=[b_s], writes=[b_s])


def resid_ln(g, pY, b_pY, xt, b_xt, Grow, b_G, lng, lnb, b_rows, tmp, b_tmp, mv, sd, b_s):
    P = g.P
    P.op("dve", lambda e: e.tensor_tensor(out=tmp, in0=pY, in1=Grow, op=ALU.mult), reads=b_pY + [b_G], writes=[b_tmp])
    P.op("dve", lambda e: e.scalar_tensor_tensor(out=xt, in0=xt, scalar=ALPHA, in1=tmp, op0=ALU.mult, op1=ALU.add),
         reads=[b_xt, b_tmp], writes=[b_xt])
    ln_stats(g, xt, b_xt, mv, sd, b_s)
    P.op("dve", lambda e: e.tensor_scalar(out=tmp, in0=xt, scalar1=mv[:, 0:1], scalar2=sd[:, 0:1],
                                          op0=ALU.subtract, op1=ALU.mult), reads=[b_xt, b_s], writes=[b_tmp])
    P.op("pool", lambda e: e.tensor_tensor(out=tmp, in0=tmp, in1=lng, op=ALU.mult), reads=[b_tmp, b_rows], writes=[b_tmp])
    P.op("pool", lambda e: e.tensor_tensor(out=xt, in0=tmp, in1=lnb, op=ALU.add), reads=[b_tmp, b_rows], writes=[b_xt])


def phase_D(g, l, x_src, wout_ap, modA, G, b_modA, b_G):
    nc, P, I, S, sb = g.nc, g.P, g.I, g.S, g.sb
    mark = g.sb_off
    Wo = sb([8, D], BF16)
    lng, lnb = sb([D], F32), sb([D], F32)
    b_w, b_rows = P.buf(), P.buf()
    P.dma("pool", Wo, wout_ap.rearrange("(k p) n -> p k n", p=128), writes=[b_w])
    r0 = ROWS.index(f"ln1_g{l}")
    P.dma("sp", lng, I.rows[r0], writes=[b_rows])
    P.dma("sp", lnb, I.rows[r0 + 1], writes=[b_rows])
    g.ln_st = sb([2, 6], F32)
    g.ln_mv = sb([4, 2], F32)
    g.ln_sd = sb([4], F32)
    mv, sd = sb([2], F32), sb([1], F32)
    b_s = P.buf()
    mixb = sb([8, 512], BF16)
    b_mix = P.buf()
    xts = [sb([D], F32) for _ in range(4)]
    b_xts = [P.buf() for _ in range(4)]
    xn = [sb([D], BF16) for _ in range(4)]
    b_xn = [P.buf() for _ in range(4)]
    tmp = sb([D], F32)
    b_tmp = P.buf()
    hT = sb([8, 512], BF16)
    b_hT = P.buf()
    zt = sb([8, 2], BF16)
    b_z = P.buf()
    P.op("pool", lambda e: e.memset(zt, 0.0), writes=[b_z])
    g.b_scrD = P.buf()
    P.dma("pool", S.h2T[:, :, 0:1].rearrange("k p o -> p k o"), zt[:, :, 0:1], reads=[b_z], writes=[g.b_scrD], allow_slow_non_contiguous=True)
    P.dma("pool", S.h2T[:, :, T + 1:T + 2].rearrange("k p o -> p k o"), zt[:, :, 1:2], reads=[b_z], writes=[g.b_scrD], allow_slow_non_contiguous=True)
    bk = g.bk
    pY = g.ps[:, 6:8, :].rearrange("p a b -> p (a b)")
    nblk = 16 if DEBUG not in ("D1", "H1", "H2", "H3") else 1
    if DEBUG in ("D1", "H1", "H2", "H3") and l == 0:
        zz = sb([8, 512], BF16)
        P.op("pool", lambda e: e.memset(zz, 0.0), writes=[b_z])
        P.dma("pool", S.mixT[:, :, 0:512].rearrange("k p n -> p k n"), zz, reads=[b_z], writes=[g.b_scrB])
        P.dma("pool", S.h2T[:, :, 513:514].rearrange("k p o -> p k o"), zt[:, :, 0:1], reads=[b_z], writes=[g.b_scrD], allow_slow_non_contiguous=True)
    for blk in range(nblk):
        t0 = blk * 512
        P.dma("sp", mixb, S.mixT[:, :, t0:t0 + 512].rearrange("k p n -> p k n"), reads=[g.b_scrB], writes=[b_mix])
        for i in range(4):
            P.dma("sp", xts[i], x_src[t0 + i * 128:t0 + (i + 1) * 128, :], reads=[g.b_scrX], writes=[b_xts[i]])
            for half in range(2):
                for k in range(8):
                    P.op("pe", lambda e, k=k, i=i, half=half: e.matmul(
                        g.ps[:, 6 + half, :], lhsT=mixb[:, k, i * 128:(i + 1) * 128], rhs=Wo[:, k, half * 512:(half + 1) * 512],
                        start=(k == 0), stop=(k == 7)), reads=[b_mix, b_w], writes=[bk[6 + half]])
            resid_ln(g, pY, [bk[6], bk[7]], xts[i], b_xts[i], G[:, 0, :], b_G, lng, lnb, b_rows, tmp, b_tmp, mv, sd, b_s)
            g.dbg_stores.append(P.dma("pool", S.x1[t0 + i * 128:t0 + (i + 1) * 128, :], xts[i], reads=[b_xts[i]], writes=[g.b_scrD]))
        ln_to_featmajor(g, None, 4, hT, b_hT, [modA[:, 32 + k, 0:1] for k in range(8)], [modA[:, 24 + k, 0:1] for k in range(8)],
                        b_modA, xts, b_xts, xn, b_xn, (P.buf(), P.buf(), bk))
        g.dbg_stores.append(P.dma("act", S.h2T[:, :, 1 + t0:1 + t0 + 512].rearrange("k p n -> p k n"), hT, reads=[b_hT], writes=[g.b_scrD]))
    g.sb_off = mark
    P.barrier()


def phase_E(g, l, dst, modA, G, b_modA, b_G):
    nc, P, I, S, sb = g.nc, g.P, g.I, g.S, g.sb
    mark = g.sb_off
    W1 = sb([8, 2 * FF], BF16)
    W2 = sb([22, D], BF16)
    cw = sb([44, 4], F32)
    lng, lnb = sb([D], F32), sb([D], F32)
    b_w, b_rows = P.buf(), P.buf()
    for k in range(8):
        P.dma("pool", W1[:, k, :], I.ffn_w_in[l][k * 128:(k + 1) * 128, :], writes=[b_w])
    P.dma("pool", W2, I.ffn_w_out[l].rearrange("(k p) n -> p k n", p=128), writes=[b_w])
    P.dma("sp", cw, I.ffn_cw[l], writes=[b_rows])
    r0 = ROWS.index(f"ln2_g{l}")
    P.dma("sp", lng, I.rows[r0], writes=[b_rows])
    P.dma("sp", lnb, I.rows[r0 + 1], writes=[b_rows])
    g.ln_st = sb([2, 6], F32)
    mv, sd = sb([2], F32), sb([1], F32)
    b_s = P.buf()
    hb = sb([8, 514], BF16)
    b_hb = P.buf()
    hid = sb([22, 512], BF16)
    b_hid = P.buf()
    cb4 = [[sb([512], F32) for _ in range(2)] for _ in range(2)]
    b_cb4 = [[P.buf(), P.buf()], [P.buf(), P.buf()]]
    ga2 = [sb([512], F32) for _ in range(2)]
    b_ga2 = [P.buf(), P.buf()]
    xt = sb([D], F32)
    b_xt = P.buf()
    tmp = sb([D], F32)
    b_tmp = P.buf()
    bk = g.bk
    ps = g.ps
    pY = ps[:, 6:8, :].rearrange("p a b -> p (a b)")
    nblk = 16 if DEBUG not in ("D1", "H1", "H2", "H3") else 1
    for blk in range(nblk):
        t0 = blk * 512
        P.dma("sp", hb, S.h2T[:, :, t0:t0 + 514].rearrange("k p n -> p k n"), reads=[g.b_scrD], writes=[b_hb])
        for i in range(22):
            for part in range(2):
                ch = part * 22 + i
                ub = (2 * i + part) % 2
                hcol = (2 * i + part) % 64
                hbk = 2 + ub
                for k in range(8):
                    P.op("pe", lambda e, k=k, ch=ch, ub=ub: e.matmul(ps[:, ub, :], lhsT=W1[:, k, ch * 128:(ch + 1) * 128],
                                                                 rhs=hb[:, k, 1:513], start=(k == 0), stop=(k == 7)),
                         reads=[b_w, b_hb], writes=[bk[ub]])
                for k in range(8):
                    P.op("pe", lambda e, k=k, ch=ch, hcol=hcol, hbk=hbk: e.matmul(ps[:, hbk, 2 * hcol:2 * hcol + 2], lhsT=W1[:, k, ch * 128:(ch + 1) * 128],
                                                                     rhs=hb[:, k, 0:514:513], start=(k == 0), stop=(k == 7)),
                         reads=[b_w, b_hb], writes=[bk[hbk]])
                c = cb4[i % 2][part]
                bc = b_cb4[i % 2][part]
                uA = ps[:, ub, :]
                uB = ps[:, hbk, 2 * hcol:2 * hcol + 2]
                P.op("act", lambda e, c=c, uA=uA, ch=ch: e.activation(out=c, in_=uA, func=AF.Identity, scale=cw[:, ch, 1:2], bias=cw[:, ch, 3:4]),
                     reads=[bk[ub], b_rows], writes=[bc])
                P.op("dve", lambda e, c=c, uA=uA, ch=ch: e.scalar_tensor_tensor(out=c[:, 1:512], in0=uA[:, 0:511], scalar=cw[:, ch, 0:1], in1=c[:, 1:512],
                                                                          op0=ALU.mult, op1=ALU.add), reads=[bk[ub], b_rows, bc], writes=[bc])
                P.op("dve", lambda e, c=c, uA=uA, ch=ch: e.scalar_tensor_tensor(out=c[:, 0:511], in0=uA[:, 1:512], scalar=cw[:, ch, 2:3], in1=c[:, 0:511],
                                                                          op0=ALU.mult, op1=ALU.add), reads=[bk[ub], b_rows, bc], writes=[bc])
                P.op("dve", lambda e, c=c, uB=uB, ch=ch: e.scalar_tensor_tensor(out=c[:, 0:1], in0=uB[:, 0:1], scalar=cw[:, ch, 0:1], in1=c[:, 0:1],
                                                                          op0=ALU.mult, op1=ALU.add), reads=[bk[hbk], b_rows, bc], writes=[bc])
                P.op("dve", lambda e, c=c, uB=uB, ch=ch: e.scalar_tensor_tensor(out=c[:, 511:512], in0=uB[:, 1:2], scalar=cw[:, ch, 2:3], in1=c[:, 511:512],
                                                                          op0=ALU.mult, op1=ALU.add), reads=[bk[hbk], b_rows, bc], writes=[bc])
            P.op("act", lambda e, i=i: e.activation(out=ga2[i % 2], in_=cb4[i % 2][0], func=AF.Gelu), reads=[b_cb4[i % 2][0]], writes=[b_ga2[i % 2]])
            P.op("pool", lambda e, i=i: e.tensor_tensor(out=hid[:, i, :], in0=ga2[i % 2], in1=cb4[i % 2][1], op=ALU.mult),
                 reads=[b_ga2[i % 2], b_cb4[i % 2][1]], writes=[b_hid])
        for it in range(4):
            P.dma("sp", xt, S.x1[t0 + it * 128:t0 + (it + 1) * 128, :], reads=[g.b_scrD], writes=[b_xt])
            for half in range(2):
                for i in range(22):
                    P.op("pe", lambda e, i=i, it=it, half=half: e.matmul(ps[:, 6 + half, :], lhsT=hid[:, i, it * 128:(it + 1) * 128],
                                                                     rhs=W2[:, i, half * 512:(half + 1) * 512], start=(i == 0), stop=(i == 21)),
                         reads=[b_hid, b_w], writes=[bk[6 + half]])
            resid_ln(g, pY, [bk[6], bk[7]], xt, b_xt, G[:, 1, :], b_G, lng, lnb, b_rows, tmp, b_tmp, mv, sd, b_s)
            g.out_stores.append(P.dma("pool", dst[t0 + it * 128:t0 + (it + 1) * 128, :], xt, reads=[b_xt], writes=[g.b_scrX]))
    g.sb_off = mark
    P.barrier()


def rope_tables():
    pos = np.arange(T)
    pr = (pos // 64).astype(np.float32)
    pc = (pos % 64).astype(np.float32)
    inv = (np.float32(10000.0) ** (-np.arange(0, 32, 2, dtype=np.float32) / np.float32(32))).astype(np.float32)
    C = np.ones((64, NK), np.float32)
    Sg = np.zeros((64, NK), np.float32)
    for d in range(64):
        p = pr if d < 32 else pc
        a = (p * inv[d % 16]).astype(np.float32)
        C[d, NCTX:] = np.cos(a)
        sgn = -1.0 if (d % 32) < 16 else 1.0
        Sg[d, NCTX:] = sgn * np.sin(a)
    return np.concatenate([C, C], 0), np.concatenate([Sg, Sg], 0)


def rope_perm_cols():
    idx = np.arange(512)
    d = idx % 64
    partner = np.where((d % 32) < 16, d + 16, d - 16)
    return (idx // 64) * 64 + partner


def make_inputs(inp):
    f32 = np.float32
    common = {}
    common["mod_w"] = np.ascontiguousarray(inp["mod_w"], f32)
    common["modb_col"] = np.ascontiguousarray(inp["mod_b"].reshape(2, 48, 128).transpose(0, 2, 1), f32)
    rows = {}
    for l in range(2):
        rows[f"ln1_g{l}"] = inp["ln1_g"][l]
        rows[f"ln1_b{l}"] = inp["ln1_b"][l]
        rows[f"ln2_g{l}"] = inp["ln2_g"][l]
        rows[f"ln2_b{l}"] = inp["ln2_b"][l]
        rows[f"modb_g1_{l}"] = inp["mod_b"][l, 2 * D:3 * D]
        rows[f"modb_g2_{l}"] = inp["mod_b"][l, 5 * D:6 * D]
    common["rows"] = np.ascontiguousarray(
        np.stack([np.broadcast_to(rows[r][None, :], (128, D)) for r in ROWS]), f32)
    common["cols"] = np.zeros((NCOLS, 128, 1), f32)
    common["cols"][0, :, 0] = inp["da_subln_g"][0]
    common["ident"] = np.eye(128, dtype=f32)
    w_in = np.asarray(inp["da_w_in"][0], f32)
    common["da_w_in"] = np.ascontiguousarray(w_in)
    pc = rope_perm_cols()
    common["da_w_perm"] = np.ascontiguousarray(np.concatenate([w_in[:, 0:512][:, pc], w_in[:, 512:1024][:, pc]], 1))
    common["da_w_out"] = np.ascontiguousarray(inp["da_w_out"][0], f32)
    C, Sg = rope_tables()
    common["rope_c"] = C
    common["rope_s"] = Sg
    lamv = np.stack([inp["da_lam_q1"][0], inp["da_lam_k1"][0], inp["da_lam_q2"][0], inp["da_lam_k2"][0]])
    common["lam"] = np.ascontiguousarray(np.broadcast_to(lamv[None], (128, 4, 64)), f32)
    a = np.arange(64)[:, None].astype(np.float64)
    k1 = np.arange(64)[None, :].astype(np.float64)
    th = 2 * np.pi * a * k1 / 64
    common["f64"] = np.concatenate([np.cos(th), -np.sin(th)], 1).astype(f32)
    p = np.arange(128)[:, None, None].astype(np.float64)
    kk = (np.arange(64)[None, :, None] + 64 * np.arange(128)[None, None, :]).astype(np.float64)
    th = 2 * np.pi * p * kk / 8192
    mr, mi = np.cos(th), -np.sin(th)
    common["fmt"] = np.stack([np.concatenate([mr, mi], 2), np.concatenate([-mi, mr], 2)], 2).astype(f32)
    c = np.arange(128)[:, None].astype(np.float64)
    th = 2 * np.pi * c * c.T / 128
    sc = 1.0 / np.sqrt(8192.0 * 128.0)
    common["fcs"] = np.stack([np.cos(th) * sc, np.sin(th) * sc], 1).astype(f32)
    common["ffn_w_in"] = np.ascontiguousarray(inp["ffn_w_in"], f32)
    common["ffn_w_out"] = np.ascontiguousarray(inp["ffn_w_out"], f32)
    cwb = np.concatenate([inp["ffn_conv_w"], inp["ffn_conv_b"][:, None, :]], 1)
    common["ffn_cw"] = np.ascontiguousarray(cwb.reshape(2, 4, 44, 128).transpose(0, 3, 2, 1), f32)
    common["hy_w_in"] = np.ascontiguousarray(inp["hy_w_in"][0], f32)
    common["hy_w_out"] = np.ascontiguousarray(inp["hy_w_out"][0], f32)
    hcw = np.concatenate([inp["hy_conv_w"][0], inp["hy_conv_b"][0][None, :]], 0)
    common["hy_cw"] = np.ascontiguousarray(hcw.reshape(4, 24, 128).transpose(2, 1, 0), f32)
    min_decay = math.log(1e-2) / 1.5
    max_decay = math.log(1e-2) / 0.3
    deltas = np.abs(np.linspace(min_decay, max_decay, D, dtype=f32))
    common["hy_cols"] = np.ascontiguousarray(
        np.stack([inp["hy_d"][0].reshape(8, 128).T, -deltas.reshape(8, 128).T], -1), f32)
    L = T
    tt = np.linspace(0.0, 1.0, L, dtype=f32)
    wv = (f32(2.0 * math.pi) * np.arange(L, dtype=f32) / f32(L)).astype(f32)
    fb = np.linspace(1e-4, 15, 16, dtype=f32)
    ang = (fb[None, :] * wv[:, None]).astype(f32)
    z = np.concatenate([tt[:, None], np.cos(ang), -np.sin(ang)], -1).astype(f32)
    idx = (L - np.arange(L)) % L
    idx[0] = 0
    zr = z[idx]
    common["hyf_z"] = np.ascontiguousarray(np.stack([z.T, zr.T]), f32)
    common["hyf_t"] = np.ascontiguousarray(np.stack([np.broadcast_to(tt[None], (128, L)),
                                                     np.broadcast_to(tt[idx][None], (128, L))]), f32)
    common["hyf_w1"] = np.ascontiguousarray(inp["hy_f_w1"][0], f32)
    common["hyf_w23"] = np.ascontiguousarray(np.stack([inp["hy_f_w2"][0], inp["hy_f_w3"][0]]), f32)
    common["hyf_w4"] = np.ascontiguousarray(inp["hy_f_w4"][0], f32)
    common["hyf_cols"] = np.ascontiguousarray(np.stack([inp["hy_f_freq"][0], inp["hy_f_b1"][0], inp["hy_f_b2"][0],
                                                        inp["hy_f_b3"][0]], -1), f32)
    a = np.arange(128)[:, None].astype(np.float64)
    kk1 = np.arange(128)[None, :].astype(np.float64)
    th = 2 * np.pi * a * kk1 / 128
    common["dft_f1"] = np.stack([np.concatenate([np.cos(th), -np.sin(th)], 1),
                                 np.concatenate([np.sin(th), np.cos(th)], 1)], 1).astype(f32)
    pp = np.arange(128)[None, :, None].astype(np.float64)
    kfull = (np.arange(128)[:, None, None] + 128 * np.arange(128)[None, None, :]).astype(np.float64)
    th = 2 * np.pi * pp * kfull / 16384.0
    mr, mi = np.cos(th), -np.sin(th)
    common["dft_mt"] = np.concatenate([mr, mi, -mi, mr], 2).astype(f32)
    maps = []
    for b in range(4):
        m = dict(common)
        m["x"] = np.ascontiguousarray(inp["x"][b], f32)
        m["ctx"] = np.ascontiguousarray(inp["ctx"][b], f32)
        cv = np.stack([inp["c"][b].reshape(8, 128).T, inp["c_ctx"].reshape(8, 128).T], -1)
        m["cvec"] = np.ascontiguousarray(cv, f32)
        maps.append(m)
    return maps


def kernel(**inputs):
    inp = {k: np.asarray(v) for k, v in inputs.items()}
    nc, _ = build_program()
    maps = make_inputs(inp)
    res = run_bass_kernel_spmd(nc, maps, core_ids=[0, 1, 2, 3])
    return np.stack([np.asarray(r["out"], np.float32) for r in res.results], 0)
```

```python
import contextlib
import math
import numpy as np
import ml_dtypes
import concourse.bass as bass
import concourse.mybir as mybir
from concourse.bass_utils import run_bass_kernel_spmd

F32 = mybir.dt.float32
BF16 = mybir.dt.bfloat16
AF = mybir.ActivationFunctionType
ALU = mybir.AluOpType

D = 1024
T = 8192
NCTX = 256
NK = T + NCTX
FF = 2816
EPS = 1e-5
ALPHA = 4 ** 0.25
DEBUG = None
DEBUG_OUT = set()


class Buf:
    __slots__ = ("name", "w", "r")

    def __init__(self, name):
        self.name = name
        self.w = None
        self.r = []


class Prog:
    ENGS = ("pe", "act", "dve", "pool", "sp")
    NDMASEM = 8

    def __init__(self, nc):
        self.nc = nc
        self.ops = []
        self.bar_deps = set()
        self.bar_pending = set()
        self.since_bar = {}

    def barrier(self):
        deps = set(self.bar_deps)
        for k, v in self.since_bar.items():
            if k == "dma":
                deps.update(v)
            else:
                deps.add(v)
        self.bar_deps = deps
        self.bar_pending = set(self.ENGS)
        self.since_bar = {}

    def buf(self, name="b"):
        return Buf(name)

    def op(self, eng, fn, reads=(), writes=(), dma=False):
        oid = len(self.ops)
        deps = set()
        for b in reads:
            if b.w is not None:
                deps.add(b.w)
        for b in writes:
            if b.w is not None:
                deps.add(b.w)
            last = {}
            for r in b.r:
                o = self.ops[r]
                if o["dma"]:
                    deps.add(r)
                else:
                    last[o["eng"]] = r
            deps.update(last.values())
        for b in reads:
            b.r.append(oid)
        for b in writes:
            b.w = oid
            b.r = []
        if eng in self.bar_pending:
            deps.update(self.bar_deps)
            self.bar_pending.discard(eng)
        deps.discard(oid)
        if dma:
            self.since_bar.setdefault("dma", []).append(oid)
        else:
            self.since_bar[eng] = oid
        self.ops.append(dict(id=oid, eng=eng, fn=fn, deps=deps, dma=dma))
        return oid

    def dma(self, q, out, in_, reads=(), writes=(), **kw):
        return self.op(q, lambda e: e.dma_start(out=out, in_=in_, **kw), reads, writes, dma=True)

    def emit(self, final_deps):
        nc = self.nc
        ops = self.ops
        ops.append(dict(id=len(ops), eng="sp", fn=None, deps=set(final_deps), dma=False))
        needed = set()
        for o in ops:
            for d in list(o["deps"]):
                od = ops[d]
                if o["eng"] == "pe" and od["eng"] == "pe" and not od["dma"] and not o["dma"]:
                    o["deps"].discard(d)
            needed |= o["deps"]
        st = contextlib.ExitStack()
        sems = {e: st.enter_context(nc.semaphore("s_" + e)) for e in self.ENGS}
        dsems = {q: [st.enter_context(nc.semaphore(f"d_{q}{i}")) for i in range(self.NDMASEM)]
                 for q in ("sp", "act", "pool")}
        cnt = {e: 0 for e in self.ENGS}
        dcnt = {q: 0 for q in dsems}
        for o in ops:
            o["sig"] = None
            o["pre"] = None
            if o["dma"]:
                q = o["eng"]
                k = dcnt[q]
                dcnt[q] += 1
                s = dsems[q][k % self.NDMASEM]
                v = 16 * (k // self.NDMASEM + 1)
                o["sig"] = (s, v, 16)
                if k >= self.NDMASEM:
                    o["pre"] = (s, v - 16)
            elif o["id"] in needed:
                cnt[o["eng"]] += 1
                o["sig"] = (sems[o["eng"]], cnt[o["eng"]], 1)
        self.stats = dict(n=len(ops), cnt=cnt, dcnt=dcnt)
        per = {e: [o for o in ops if o["eng"] == e] for e in self.ENGS}

        def replay(eng_name):
            def run(e):
                waited = {}

                def w(s, v):
                    if waited.get(id(s), 0) < v:
                        e.wait_ge(s, v)
                        waited[id(s)] = v
                for o in per[eng_name]:
                    if o["pre"] is not None:
                        w(*o["pre"])
                    for d in sorted(o["deps"]):
                        s, v, _ = ops[d]["sig"]
                        w(s, v)
                    if o["fn"] is None:
                        continue
                    ins = o["fn"](e)
                    if o["sig"] is not None:
                        ins.then_inc(o["sig"][0], o["sig"][2])
            return run

        with nc.Block() as block:
            block.tensor(replay("pe"))
            block.scalar(replay("act"))
            block.vector(replay("dve"))
            block.gpsimd(replay("pool"))
            block.sync(replay("sp"))
        st.close()


class Ctx:
    pass


def build_program():
    nc = bass.Bass("TRN2", target_bir_lowering=False)
    g = Ctx()
    g.nc = nc
    P = Prog(nc)
    g.P = P
    g.final = []
    g.dbg_stores = []
    g.b_scrB = P.buf("scrB")
    g.b_scrX = P.buf("scrX")
    g.out_stores = []
    g.bk = [P.buf(f"bank{i}") for i in range(8)]

    def din(name, shape, dt=F32):
        return nc.dram_tensor(name, list(shape), dt, kind="ExternalInput").ap()

    def dscr(name, shape, dt):
        kind = "ExternalOutput" if name in DEBUG_OUT else "Internal"
        return nc.dram_tensor(name, list(shape), dt, kind=kind).ap()

    I = Ctx()
    g.I = I
    I.x = din("x", [T, D])
    I.ctx = din("ctx", [NCTX, D])
    I.cvec = din("cvec", [128, 8, 2])
    I.mod_w = din("mod_w", [2, D, 6 * D])
    I.modb_col = din("modb_col", [2, 128, 48])
    I.rows = din("rows", [NROWS, 128, D])
    I.cols = din("cols", [NCOLS, 128, 1])
    I.ident = din("ident", [128, 128])
    I.da_w_in = din("da_w_in", [D, 2048])
    I.da_w_perm = din("da_w_perm", [D, 1024])
    I.da_w_out = din("da_w_out", [D, D])
    I.rope_c = din("rope_c", [128, NK])
    I.rope_s = din("rope_s", [128, NK])
    I.lam = din("lam", [128, 4, 64])
    I.hy_w_in = din("hy_w_in", [D, 3 * D])
    I.hy_w_out = din("hy_w_out", [D, D])
    I.hy_cw = din("hy_cw", [128, 24, 4])
    I.dft_f1 = din("dft_f1", [128, 2, 256])
    I.dft_mt = din("dft_mt", [128, 128, 512])
    I.hyf_z = din("hyf_z", [2, 33, T])
    I.hyf_t = din("hyf_t", [2, 128, T])
    I.hyf_w1 = din("hyf_w1", [33, 64])
    I.hyf_w23 = din("hyf_w23", [2, 64, 64])
    I.hyf_w4 = din("hyf_w4", [64, 2 * D])
    I.hyf_cols = din("hyf_cols", [64, 4])
    I.hy_cols = din("hy_cols", [128, 8, 2])
    I.ffn_w_in = din("ffn_w_in", [2, D, 2 * FF])
    I.ffn_w_out = din("ffn_w_out", [2, FF, D])
    I.ffn_cw = din("ffn_cw", [2, 128, 44, 4])
    I.f64 = din("f64", [64, 128])
    I.fmt = din("fmt", [128, 64, 2, 256])
    I.fcs = din("fcs", [128, 2, 128])
    g.out = nc.dram_tensor("out", [T, D], F32, kind="ExternalOutput").ap()

    S = Ctx()
    g.S = S
    S.QT = dscr("QT", [4, 128, T], BF16)
    S.KT = dscr("KT", [4, 128, NK], BF16)
    S.V = dscr("V", [NK, 512], BF16)
    S.F = dscr("F", [T, 512], BF16)
    S.mixT = dscr("mixT", [8, 128, T], BF16)
    S.x1 = dscr("x1", [T, D], F32)
    S.x2 = dscr("x2", [T, D], F32)
    S.h2T = dscr("h2T", [8, 128, T + 2], BF16)
    S.x0T = dscr("x0T", [8, 128, T], BF16)
    S.vT = dscr("vT", [8, 128, T], BF16)
    S.vtok = dscr("vtok", [T, D], BF16)
    S.ktok = dscr("ktok", [2 * T, D], BF16)
    S.KS = dscr("KS", [8, 128, 128, 256], BF16)
    S.Ztok = dscr("Ztok", [2, 2 * T, D], BF16)
    S.MTb = dscr("MTb", [128, 128, 512], BF16)
    S.y_dbg = dscr("y_dbg", [128, T], F32) if "y_dbg" in DEBUG_OUT else None
    S.knorm_dbg = dscr("knorm_dbg", [128, 8], F32) if "knorm_dbg" in DEBUG_OUT else None
    S.hT_dbg = dscr("hT_dbg", [128, 8, T], F32) if "hT_dbg" in DEBUG_OUT else None

    arena = nc.alloc_sbuf_tensor("arena", [128, ARENA_BYTES // 4], F32)
    g.arena = arena
    g.ps = nc.alloc_psum_tensor("ps", [128, 8, 512], F32)
    g.sb_off = 0

    def sb(shape, dt, reset_to=None):
        n = int(np.prod(shape))
        nbytes = n * (4 if dt == F32 else 2)
        nbytes = (nbytes + 31) // 32 * 32
        off = g.sb_off
        assert off + nbytes <= ARENA_BYTES, (off, nbytes)
        g.sb_off += nbytes
        v = arena[:, off // 4:(off + nbytes) // 4]
        if dt != F32:
            v = v.bitcast(dt)
        v = v[:, 0:n]
        if len(shape) == 2:
            v = v.rearrange("p (a b) -> p a b", b=shape[1])
        elif len(shape) == 3:
            v = v.rearrange("p (a b c) -> p a b c", b=shape[1], c=shape[2])
        return v
    g.sb = sb

    g.ident = sb([128], BF16)
    g.identf = sb([128], F32)
    g.epsT = sb([1], F32)
    bconst = P.buf("const")
    g.bconst = bconst
    P.dma("sp", g.identf, I.ident, writes=[bconst])
    P.op("dve", lambda e: e.tensor_copy(out=g.ident, in_=g.identf), reads=[bconst], writes=[bconst])
    P.op("pool", lambda e: e.memset(g.epsT, EPS), writes=[bconst])
    g.persist_off = g.sb_off

    layer0(g)
    if DEBUG is None:
        g.final.extend(g.out_stores)
    elif not g.final:
        g.final.extend(g.dbg_stores + g.out_stores)

    P.emit(g.final)
    return nc, P


ROWS = ["ln1_g0", "ln1_b0", "ln2_g0", "ln2_b0", "ln1_g1", "ln1_b1", "ln2_g1", "ln2_b1",
        "modb_g1_0", "modb_g2_0", "modb_g1_1", "modb_g2_1"]
NROWS = len(ROWS)
NCOLS = 64
ARENA_BYTES = 212480


def mod_params(g, l):
    nc, P, I, sb = g.nc, g.P, g.I, g.sb
    modA = sb([48, 2], F32)
    G = sb([2, D], F32)
    mark = g.sb_off
    cv = sb([8, 2], F32)
    sc = sb([8, 2], F32)
    screp = sb([8, 128], F32)
    mbc = sb([48], F32)
    wblk = [sb([8, 512], F32) for _ in range(2)]
    grow = sb([2, D], F32)
    b_modA, b_G, b_cv, b_rep = P.buf(), P.buf(), P.buf(), P.buf()
    b_w = [P.buf(), P.buf()]
    b_pm, b_pg = P.buf(), P.buf()
    pM = g.ps[:, 0, 0:96].rearrange("p (j s) -> p j s", s=2)
    P.dma("sp", cv, I.cvec, writes=[b_cv])
    P.dma("sp", mbc, I.modb_col[l], writes=[b_cv])
    r1 = ROWS.index(f"modb_g1_{l}")
    P.dma("sp", grow[:, 0, :], I.rows[r1], writes=[b_cv])
    P.dma("sp", grow[:, 1, :], I.rows[r1 + 1], writes=[b_cv])
    P.op("act", lambda e: e.activation(out=sc, in_=cv, func=AF.Silu), reads=[b_cv], writes=[b_cv])
    for k in range(8):
        P.op("dve", lambda e, k=k: e.tensor_copy(out=screp[:, k, :], in_=sc[:, k, 0:1].to_broadcast([128, 128])),
             reads=[b_cv], writes=[b_rep])
    wv = I.mod_w[l].rearrange("(k p) n -> p k n", p=128)
    for blk in range(12):
        w = wblk[blk % 2]
        bw = b_w[blk % 2]
        P.dma("sp", w, wv[:, :, blk * 512:(blk + 1) * 512], writes=[bw])
        for jj in range(4):
            j = blk * 4 + jj
            for k in range(8):
                P.op("pe", lambda e, k=k, jj=jj, j=j, w=w: e.matmul(
                    pM[:, j, :], lhsT=w[:, k, jj * 128:(jj + 1) * 128], rhs=sc[:, k, :],
                    start=(k == 0), stop=(k == 7)), reads=[bw, b_cv], writes=[b_pm])
        if blk in (4, 5, 10, 11):
            gi = 0 if blk < 6 else 1
            half = blk % 2
            pG = g.ps[:, 1 + half, :]
            for k in range(8):
                P.op("pe", lambda e, k=k, w=w, pG=pG: e.matmul(
                    pG, lhsT=screp[:, k, :], rhs=w[:, k, :], start=(k == 0), stop=(k == 7)),
                    reads=[bw, b_rep], writes=[b_pg])
            P.op("dve", lambda e, gi=gi, half=half, pG=pG: e.tensor_tensor(
                out=G[:, gi, half * 512:(half + 1) * 512], in0=pG, in1=grow[:, gi, half * 512:(half + 1) * 512],
                op=ALU.add), reads=[b_pg, b_cv], writes=[b_G])
    P.op("dve", lambda e: e.tensor_tensor(out=modA, in0=pM, in1=mbc.unsqueeze(2).to_broadcast([128, 48, 2]),
                                          op=ALU.add), reads=[b_pm, b_cv], writes=[b_modA])
    for lo in (8, 32):
        P.op("dve", lambda e, lo=lo: e.tensor_scalar(out=modA[:, lo:lo + 8, :], in0=modA[:, lo:lo + 8, :],
                                                    scalar1=1.0, scalar2=None, op0=ALU.add),
             reads=[b_modA], writes=[b_modA])
    g.sb_off = mark
    P.barrier()
    return modA, G, b_modA, b_G


def ln_to_featmajor(g, src_tiles, ntile, hT, b_hT, scale_cols, bias_cols, b_mod, xt, b_xt, xn, b_xn, tagbufs):
    nc, P = g.nc, g.P
    b_st, b_mv, b_pT = tagbufs
    st, mv, sd = g.ln_st, g.ln_mv, g.ln_sd
    for i in range(ntile):
        for hh in range(2):
            P.op("dve", lambda e, i=i, hh=hh: e.bn_stats(out=st[:, hh, :], in_=xt[i][:, hh * 512:(hh + 1) * 512]),
                 reads=[b_xt[i]], writes=[b_st])
        P.op("dve", lambda e, i=i: e.bn_aggr(out=mv[:, i, :], in_=st), reads=[b_st], writes=[b_mv])
    P.op("act", lambda e: e.activation(out=sd[:, 0:ntile], in_=mv[:, 0:ntile, 1], func=AF.Sqrt,
                                       bias=g.epsT[:, 0:1], scale=1.0), reads=[b_mv, g.bconst], writes=[b_mv])
    P.op("dve", lambda e: e.reciprocal(out=sd[:, 0:ntile], in_=sd[:, 0:ntile]), reads=[b_mv], writes=[b_mv])
    for i in range(ntile):
        P.op("dve", lambda e, i=i: e.tensor_scalar(out=xn[i], in0=xt[i], scalar1=mv[:, i, 0:1],
                                                   scalar2=sd[:, i:i + 1], op0=ALU.subtract, op1=ALU.mult),
             reads=[b_xt[i], b_mv], writes=[b_xn[i]])
    pT = g.ps[:, 0:4, :].bitcast(BF16).rearrange("p b (h n) -> p (b h) n", h=2)
    n = ntile * 128
    for j in range(4):
        for k in (2 * j, 2 * j + 1):
            for i in range(ntile):
                P.op("pe", lambda e, k=k, i=i: e.transpose(out=pT[:, k, i * 128:(i + 1) * 128],
                                                          in_=xn[i][:, k * 128:(k + 1) * 128], identity=g.ident),
                     reads=[b_xn[i], g.bconst], writes=[b_pT[j]])
        for k in (2 * j, 2 * j + 1):
            P.op("act", lambda e, k=k, n=n: e.activation(out=hT[:, k, 0:n], in_=pT[:, k, 0:n], func=AF.Identity,
                                                        scale=scale_cols[k], bias=bias_cols[k]),
                 reads=[b_pT[j], b_mod], writes=[b_hT])


def layer0(g):
    nc, P, I, S, sb = g.nc, g.P, g.I, g.S, g.sb
    g.sb_off = g.persist_off
    modA, G, b_modA, b_G = mod_params(g, 0)
    g.modA0, g.G0, g.b_modA0, g.b_G0 = modA, G, b_modA, b_G
    if DEBUG == "mod":
        d = nc.dram_tensor("dbg_modA", [128, 96], F32, kind="ExternalOutput").ap()
        d2 = nc.dram_tensor("dbg_G", [128, 2 * D], F32, kind="ExternalOutput").ap()
        g.final.append(P.dma("sp", d, modA.rearrange("p j s -> p (j s)"), reads=[b_modA]))
        g.final.append(P.dma("sp", d2, G.rearrange("p a b -> p (a b)"), reads=[b_G]))
        return
    phase_mark = g.sb_off
    NCOLW = 3072
    W = sb([8, NCOLW], BF16)
    b_W = P.buf()
    if DEBUG != "A0":
        P.dma("pool", W[:, :, 0:2048], I.da_w_in.rearrange("(k p) n -> p k n", p=128), writes=[b_W])
        P.dma("pool", W[:, :, 2048:3072], I.da_w_perm.rearrange("(k p) n -> p k n", p=128), writes=[b_W])
    g.ln_st = sb([2, 6], F32)
    g.ln_mv = sb([4, 2], F32)
    g.ln_sd = sb([4], F32)
    xts = [[sb([D], F32) for _ in range(4)] for _ in range(2)]
    b_xts = [[P.buf() for _ in range(4)] for _ in range(2)]
    xn = [sb([D], BF16) for _ in range(4)]
    b_xn = [P.buf() for _ in range(4)]
    hT = sb([8, 512], BF16)
    b_hT = P.buf()
    tag = (P.buf(), P.buf(), g.bk)
    ctab = [sb([512], F32) for _ in range(2)]
    stab = [sb([512], F32) for _ in range(2)]
    b_tab = [P.buf(), P.buf()]
    t1 = sb([512], F32)
    t2 = sb([512], F32)
    b_t1, b_t2 = P.buf(), P.buf()
    qo = [sb([512], BF16) for _ in range(2)]
    b_qo = [P.buf(), P.buf()]
    vo = [sb([512], BF16) for _ in range(2)]
    b_vo = [P.buf(), P.buf()]
    pA, pB = g.ps[:, 4, :], g.ps[:, 5, :]
    b_pA, b_pB = g.bk[4], g.bk[5]
    pV = [g.ps[:, 6, :], g.ps[:, 7, :]]
    b_pV = [g.bk[6], g.bk[7]]
    b_scr = P.buf("scrA")
    g.b_scrA = b_scr
    nblk = 17
    if DEBUG in ("A1", "A0"):
        nblk = 2
    qcnt = 0
    vcnt = 0
    for blk in range(nblk):
        ctxb = (blk == 0)
        ntile = 2 if ctxb else 4
        n = ntile * 128
        par = blk % 2
        src = I.ctx if ctxb else I.x
        t0 = 0 if ctxb else (blk - 1) * 512
        kpos = 0 if ctxb else NCTX + t0
        for i in range(ntile):
            P.dma("sp", xts[par][i], src[t0 + i * 128:t0 + (i + 1) * 128, :], writes=[b_xts[par][i]])
        P.dma("sp", ctab[par][:, 0:n], I.rope_c[:, kpos:kpos + n], writes=[b_tab[par]])
        P.dma("sp", stab[par][:, 0:n], I.rope_s[:, kpos:kpos + n], writes=[b_tab[par]])
        s = 1 if ctxb else 0
        ln_to_featmajor(g, None, ntile, hT, b_hT,
                        [modA[:, 8 + k, s:s + 1] for k in range(8)], [modA[:, k, s:s + 1] for k in range(8)],
                        b_modA, xts[par], b_xts[par], xn, b_xn, tag)
        if S.hT_dbg is not None and not ctxb:
            if not hasattr(g, "hTf"):
                g.hTf = sb([8, 512], F32)
                g.b_hTf = P.buf()
            P.op("dve", lambda e: e.tensor_copy(out=g.hTf, in_=hT), reads=[b_hT], writes=[g.b_hTf])
            g.final.append(P.dma("sp", S.hT_dbg[:, :, t0:t0 + n], g.hTf[:, :, 0:n], reads=[g.b_hTf]))
        if DEBUG == "A0":
            continue
        for which in ((1,) if ctxb else (0, 1)):
            for h in range(4):
                c0 = which * 512 + h * 128
                c1 = 2048 + which * 512 + h * 128
                for k in range(8):
                    P.op("pe", lambda e, k=k, c0=c0, n=n: e.matmul(pA[:, 0:n], lhsT=W[:, k, c0:c0 + 128], rhs=hT[:, k, 0:n],
                                                               start=(k == 0), stop=(k == 7)),
                         reads=[b_W, b_hT], writes=[b_pA])
                for k in range(8):
                    P.op("pe", lambda e, k=k, c1=c1, n=n: e.matmul(pB[:, 0:n], lhsT=W[:, k, c1:c1 + 128], rhs=hT[:, k, 0:n],
                                                               start=(k == 0), stop=(k == 7)),
                         reads=[b_W, b_hT], writes=[b_pB])
                P.op("dve", lambda e, n=n, par=par: e.tensor_tensor(out=t1[:, 0:n], in0=pA[:, 0:n], in1=ctab[par][:, 0:n], op=ALU.mult),
                     reads=[b_pA, b_tab[par]], writes=[b_t1])
                P.op("dve", lambda e, n=n, par=par: e.tensor_tensor(out=t2[:, 0:n], in0=pB[:, 0:n], in1=stab[par][:, 0:n], op=ALU.mult),
                     reads=[b_pB, b_tab[par]], writes=[b_t2])
                qb = qcnt % 2
                qcnt += 1
                P.op("pool", lambda e, n=n, qb=qb: e.tensor_tensor(out=qo[qb][:, 0:n], in0=t1[:, 0:n], in1=t2[:, 0:n], op=ALU.add),
                     reads=[b_t1, b_t2], writes=[b_qo[qb]])
                dst = S.QT[h][:, t0:t0 + n] if which == 0 else S.KT[h][:, kpos:kpos + n]
                g.dbg_stores.append(P.dma("pool", dst, qo[qb][:, 0:n], reads=[b_qo[qb]], writes=[b_scr]))
        for i in range(ntile):
            for which in ((0,) if ctxb else (0, 1)):
                c0 = 1024 + which * 512
                vb = vcnt % 2
                vcnt += 1
                for k in range(8):
                    P.op("pe", lambda e, k=k, i=i, c0=c0, vb=vb: e.matmul(pV[vb], lhsT=hT[:, k, i * 128:(i + 1) * 128],
                                                                     rhs=W[:, k, c0:c0 + 512], start=(k == 0), stop=(k == 7)),
                         reads=[b_W, b_hT], writes=[b_pV[vb]])
                P.op("act", lambda e, vb=vb: e.activation(out=vo[vb], in_=pV[vb], func=AF.Copy),
                     reads=[b_pV[vb]], writes=[b_vo[vb]])
                if which == 0:
                    dst = S.V[kpos + i * 128:kpos + (i + 1) * 128, :]
                else:
                    dst = S.F[t0 + i * 128:t0 + (i + 1) * 128, :]
                g.dbg_stores.append(P.dma("act", dst, vo[vb], reads=[b_vo[vb]], writes=[b_scr]))
    if DEBUG in ("A1", "A0"):
        g.final.extend(g.dbg_stores)
        return
    g.sb_off = phase_mark
    P.barrier()
    attention(g)


def attention(g):
    nc, P, I, S, sb = g.nc, g.P, g.I, g.S, g.sb
    mark = g.sb_off
    KT = sb([NK], BF16)
    Vh = sb([66, 128], BF16)
    b_KT, b_Vh = P.buf(), P.buf()
    QTb = [sb([512], BF16) for _ in range(2)]
    b_Q = [P.buf(), P.buf()]
    Pm = [[sb([512], BF16) for _ in range(2)] for _ in range(2)]
    b_Pm = [[P.buf(), P.buf()], [P.buf(), P.buf()]]
    ones = sb([128], BF16)
    onesf = sb([128], F32)
    ones32 = sb([128], F32)
    zacc = sb([512], F32)
    b_zacc = P.buf()
    lamt = sb([4, 64], F32)
    lt = sb([2, 64], F32)
    ls = sb([2], F32)
    nlam = sb([1], F32)
    gp = sb([1], F32)
    b_c = P.buf()
    P.op("pool", lambda e: e.memset(ones, 1.0), writes=[b_c])
    P.op("pool", lambda e: e.memset(onesf, 1.0 / 128.0), writes=[b_c])
    P.op("pool", lambda e: e.memset(ones32, 1.0), writes=[b_c])
    P.dma("sp", lamt, I.lam, writes=[b_c])
    P.dma("sp", gp, I.cols[0], writes=[b_c])
    P.op("dve", lambda e: e.tensor_tensor(out=lt, in0=lamt[:, 0:4:2, :], in1=lamt[:, 1:4:2, :], op=ALU.mult),
         reads=[b_c], writes=[b_c])
    P.op("dve", lambda e: e.tensor_reduce(out=ls, in_=lt, axis=mybir.AxisListType.X, op=ALU.add), reads=[b_c], writes=[b_c])
    P.op("act", lambda e: e.activation(out=ls, in_=ls, func=AF.Exp), reads=[b_c], writes=[b_c])
    P.op("dve", lambda e: e.tensor_tensor(out=nlam, in0=ls[:, 1:2], in1=ls[:, 0:1], op=ALU.subtract), reads=[b_c], writes=[b_c])
    P.op("dve", lambda e: e.tensor_scalar(out=nlam, in0=nlam, scalar1=-(0.8 - 0.6), scalar2=None, op0=ALU.add), reads=[b_c], writes=[b_c])
    P.op("dve", lambda e: e.tensor_scalar(out=gp, in0=gp, scalar1=1.0 - (0.8 - 0.6), scalar2=None, op0=ALU.mult), reads=[b_c], writes=[b_c])
    r1 = sb([512], F32)
    o1 = sb([512], F32)
    o2 = sb([512], F32)
    sq = sb([512], F32)
    rs = sb([512], F32)
    ob = [sb([512], BF16) for _ in range(2)]
    b_r1, b_o1, b_o2, b_sq, b_rs = P.buf(), P.buf(), P.buf(), P.buf(), P.buf()
    b_ob = [P.buf(), P.buf()]
    bk = [P.buf() for _ in range(8)]
    ps = g.ps
    heads = range(4)
    qbs = range(16)
    if DEBUG == "B1":
        heads, qbs = [0], [0]
    if DEBUG in ("C1", "D1", "H1", "H2", "H3"):
        heads = []
    cnt = 0
    for h in heads:
        P.dma("sp", KT, S.KT[h], reads=[g.b_scrA], writes=[b_KT])
        P.dma("sp", Vh, S.V[:, h * 128:(h + 1) * 128].rearrange("(t p) e -> p t e", p=128), reads=[g.b_scrA], writes=[b_Vh])
        for qb in qbs:
            Q = QTb[cnt % 2]
            bq = b_Q[cnt % 2]
            obuf = ob[cnt % 2]
            b_obuf = b_ob[cnt % 2]
            cnt += 1
            P.dma("sp", Q, S.QT[h][:, qb * 512:(qb + 1) * 512], reads=[g.b_scrA], writes=[bq])
            def s_ops(kt):
                for c in range(2):
                    sbk = c * 2 + kt % 2
                    P.op("pe", lambda e, c=c, kt=kt, sbk=sbk, Q=Q: e.matmul(
                        ps[:, sbk, :], lhsT=KT[c * 64:(c + 1) * 64, kt * 128:(kt + 1) * 128], rhs=Q[c * 64:(c + 1) * 64, :],
                        start=True, stop=True), reads=[b_KT, bq], writes=[bk[sbk]])

            def e_ops(kt):
                for c in range(2):
                    sbk = c * 2 + kt % 2
                    pm = Pm[c][kt % 2]
                    P.op("act", lambda e, sbk=sbk, pm=pm: e.activation(out=pm, in_=ps[:, sbk, :], func=AF.Exp, scale=0.125),
                         reads=[bk[sbk]], writes=[b_Pm[c][kt % 2]])

            def pv_ops(kt):
                for c in range(2):
                    pm = Pm[c][kt % 2]
                    bpm = b_Pm[c][kt % 2]
                    P.op("pe", lambda e, c=c, kt=kt, pm=pm: e.matmul(ps[:, 4 + 2 * c, :], lhsT=Vh[:, kt, :], rhs=pm,
                                                                  start=(kt == 0), stop=(kt == 65)),
                         reads=[b_Vh, bpm], writes=[bk[4 + 2 * c]])
                    if c == 0:
                        P.op("pe", lambda e, c=c, kt=kt, pm=pm: e.matmul(ps[:, 5 + 2 * c, :], lhsT=ones, rhs=pm,
                                                                      start=(kt == 0), stop=(kt == 65)),
                             reads=[b_c, bpm], writes=[bk[5 + 2 * c]])
                    elif kt == 0:
                        P.op("dve", lambda e, pm=pm: e.tensor_copy(out=zacc, in_=pm), reads=[bpm], writes=[b_zacc])
                    else:
                        P.op("dve", lambda e, pm=pm: e.tensor_tensor(out=zacc, in0=zacc, in1=pm, op=ALU.add),
                             reads=[bpm, b_zacc], writes=[b_zacc])
            s_ops(0)
            for kt in range(66):
                if kt + 1 < 66:
                    s_ops(kt + 1)
                e_ops(kt)
                pv_ops(kt)
            P.op("pe", lambda e: e.matmul(ps[:, 7, :], lhsT=ones32, rhs=zacc, start=True, stop=True),
                 reads=[b_c, b_zacc], writes=[bk[7]])
            P.op("dve", lambda e: e.reciprocal(out=r1, in_=ps[:, 5, :]), reads=[bk[5]], writes=[b_r1])
            P.op("dve", lambda e: e.tensor_tensor(out=o1, in0=ps[:, 4, :], in1=r1, op=ALU.mult), reads=[bk[4], b_r1], writes=[b_o1])
            P.op("dve", lambda e: e.reciprocal(out=r1, in_=ps[:, 7, :]), reads=[bk[7], b_r1], writes=[b_r1])
            P.op("dve", lambda e: e.tensor_tensor(out=o2, in0=ps[:, 6, :], in1=r1, op=ALU.mult), reads=[bk[6], b_r1], writes=[b_o2])
            P.op("dve", lambda e: e.scalar_tensor_tensor(out=o1, in0=o2, scalar=nlam[:, 0:1], in1=o1, op0=ALU.mult, op1=ALU.add),
                 reads=[b_o2, b_o1, b_c], writes=[b_o1])
            P.op("pool", lambda e: e.tensor_tensor(out=sq, in0=o1, in1=o1, op=ALU.mult), reads=[b_o1], writes=[b_sq])
            P.op("pe", lambda e: e.matmul(ps[:, 0, :], lhsT=onesf, rhs=sq, start=True, stop=True), reads=[b_sq, b_c], writes=[bk[0]])
            P.op("act", lambda e: e.activation(out=rs, in_=ps[:, 0, :], func=AF.Ln, bias=g.epsT[:, 0:1], scale=1.0),
                 reads=[bk[0], g.bconst], writes=[b_rs])
            P.op("act", lambda e: e.activation(out=rs, in_=rs, func=AF.Exp, scale=-0.5), reads=[b_rs], writes=[b_rs])
            P.op("dve", lambda e, obuf=obuf: e.scalar_tensor_tensor(out=obuf, in0=o1, scalar=gp[:, 0:1], in1=rs, op0=ALU.mult, op1=ALU.mult),
                 reads=[b_o1, b_rs, b_c], writes=[b_obuf])
            g.dbg_stores.append(P.dma("pool", S.mixT[h][:, qb * 512:(qb + 1) * 512], obuf, reads=[b_obuf], writes=[g.b_scrB]))
    if DEBUG == "B1":
        g.final.extend(g.dbg_stores)
        return
    g.sb_off = mark
    P.barrier()
    fourier(g)


def fourier(g):
    nc, P, I, S, sb = g.nc, g.P, g.I, g.S, g.sb
    mark = g.sb_off
    F64 = sb([128], BF16)
    MT = sb([64, 2, 256], BF16)
    CS = sb([2, 128], BF16)
    b_t = P.buf()
    P.dma("pool", F64[0:64, :], I.f64, writes=[b_t])
    P.dma("pool", MT, I.fmt, writes=[b_t])
    P.dma("pool", CS, I.fcs, writes=[b_t])
    G = sb([128, 128], BF16)
    Y = sb([2, 64, 128], BF16)
    X = sb([64, 256], BF16)
    fm = sb([T], BF16)
    b_G, b_Y, b_X, b_fm = P.buf(), P.buf(), P.buf(), P.buf()
    bk = [P.buf() for _ in range(8)]
    ps = g.ps
    groups = range(4)
    if DEBUG == "C1":
        groups = [0]
    if DEBUG in ("D1", "H1", "H2", "H3"):
        groups = []
    for gi in groups:
        P.dma("sp", G[0:64], S.F[:, gi * 128:(gi + 1) * 128].rearrange("(a p) c -> a p c", p=128),
              reads=[g.b_scrA], writes=[b_G])
        for c0 in range(0, 128, 4):
            b = (c0 // 4) % 2
            for cc in range(4):
                P.op("pe", lambda e, c=c0 + cc, cc=cc, b=b: e.matmul(ps[:, b, cc * 128:(cc + 1) * 128], lhsT=G[0:64, :, c],
                                                                 rhs=F64[0:64, :], start=True, stop=True),
                     reads=[b_G, b_t], writes=[bk[b]])
            P.op("act", lambda e, c0=c0, b=b: e.activation(
                out=Y[:, :, :, c0:c0 + 4], in_=ps[:, b, :].rearrange("p (cc r k) -> p r k cc", cc=4, r=2),
                func=AF.Copy), reads=[bk[b]], writes=[b_Y])
        for k1 in range(64):
            b = 2 + (k1 // 2) % 2
            o = (k1 % 2) * 256
            P.op("pe", lambda e, k1=k1, b=b, o=o: e.matmul(ps[:, b, o:o + 256], lhsT=Y[:, 0, k1, :], rhs=MT[:, k1, 0, :],
                                                       start=True, stop=False), reads=[b_Y, b_t], writes=[bk[b]])
            P.op("pe", lambda e, k1=k1, b=b, o=o: e.matmul(ps[:, b, o:o + 256], lhsT=Y[:, 1, k1, :], rhs=MT[:, k1, 1, :],
                                                       start=False, stop=True), reads=[b_Y, b_t], writes=[bk[b]])
            if k1 % 2 == 1:
                P.op("dve", lambda e, k1=k1, b=b: e.tensor_copy(out=X[:, k1 - 1:k1 + 1, :],
                                                              in_=ps[:, b, :].rearrange("p (a x) -> p a x", a=2)),
                     reads=[bk[b]], writes=[b_X])
        fmv = fm.rearrange("p (k2 k1) -> p k1 k2", k1=64)
        for q in range(16):
            b = 4 + q % 2
            P.op("pe", lambda e, q=q, b=b: e.matmul(ps[:, b, :], lhsT=CS[:, 0, :], rhs=X[:, 4 * q:4 * q + 4, 0:128],
                                                start=True, stop=False), reads=[b_X, b_t], writes=[bk[b]])
            P.op("pe", lambda e, q=q, b=b: e.matmul(ps[:, b, :], lhsT=CS[:, 1, :], rhs=X[:, 4 * q:4 * q + 4, 128:256],
                                                start=False, stop=True), reads=[b_X, b_t], writes=[bk[b]])
            P.op("act", lambda e, q=q, b=b: e.activation(out=fmv[:, 4 * q:4 * q + 4, :],
                                                     in_=ps[:, b, :].rearrange("p (a x) -> p a x", a=4), func=AF.Copy),
                 reads=[bk[b]], writes=[b_fm])
        g.dbg_stores.append(P.dma("act", S.mixT[4 + gi], fm, reads=[b_fm], writes=[g.b_scrB]))
    if DEBUG == "C1":
        g.final.extend(g.dbg_stores)
        return
    g.sb_off = mark
    P.barrier()
    phase_D(g, 0, I.x, I.da_w_out, g.modA0, g.G0, g.b_modA0, g.b_G0)
    phase_E(g, 0, S.x2, g.modA0, g.G0, g.b_modA0, g.b_G0)
    if DEBUG in ("D1",):
        return
    layer1(g)


def fm_to_tok(g, src, b_src, dst_rows, tb, b_tb, bank, b_scr):
    P = g.P
    pT = g.ps[:, bank, :].bitcast(BF16)[:, 0:512].rearrange("p (j c) -> p j c", c=128)
    for j in range(4):
        P.op("pe", lambda e, j=j: e.transpose(out=pT[:, j, :], in_=src[:, j * 128:(j + 1) * 128], identity=g.ident),
             reads=[b_src, g.bconst], writes=[g.bk[bank]])
    P.op("act", lambda e: e.activation(out=tb, in_=pT, func=AF.Copy), reads=[g.bk[bank]], writes=[b_tb])
    g.dbg_stores.append(P.dma("act", dst_rows, tb, reads=[b_tb], writes=[b_scr]))


def layer1(g):
    nc, P, I, S, sb = g.nc, g.P, g.I, g.S, g.sb
    g.sb_off = g.persist_off
    P.barrier()
    modA, G, b_modA, b_G = mod_params(g, 1)
    g.knorm = sb([8], F32)
    g.b_knorm = P.buf()
    mark = g.sb_off
    g.ln_st = sb([2, 6], F32)
    g.ln_mv = sb([4, 2], F32)
    g.ln_sd = sb([4], F32)
    xts = [sb([D], F32) for _ in range(4)]
    b_xts = [P.buf() for _ in range(4)]
    xn = [sb([D], BF16) for _ in range(4)]
    b_xn = [P.buf() for _ in range(4)]
    hT = sb([8, 512], BF16)
    b_hT = P.buf()
    nblk = 16 if DEBUG not in ("H1", "H2", "H3") else 1
    g.b_scrH = P.buf()
    for blk in range(nblk):
        t0 = blk * 512
        for i in range(4):
            P.dma("sp", xts[i], S.x2[t0 + i * 128:t0 + (i + 1) * 128, :], reads=[g.b_scrX], writes=[b_xts[i]])
        ln_to_featmajor(g, None, 4, hT, b_hT, [modA[:, 8 + k, 0:1] for k in range(8)], [modA[:, k, 0:1] for k in range(8)],
                        b_modA, xts, b_xts, xn, b_xn, (P.buf(), P.buf(), g.bk))
        P.dma("act", S.h2T[:, :, 1 + t0:1 + t0 + 512].rearrange("k p n -> p k n"), hT, reads=[b_hT], writes=[g.b_scrH])
    g.sb_off = mark
    P.barrier()
    W = sb([8, 3 * D], BF16)
    cw = sb([24, 4], F32)
    b_w, b_rows = P.buf(), P.buf()
    for k in range(8):
        P.dma("pool", W[:, k, :], I.hy_w_in[k * 128:(k + 1) * 128, :], writes=[b_w])
    P.dma("sp", cw, I.hy_cw, writes=[b_rows])
    hb = sb([8, 514], BF16)
    b_hb = P.buf()
    cb = [sb([512], F32) for _ in range(3)]
    b_cb = [P.buf() for _ in range(3)]
    x0b = sb([512], BF16)
    vvb = sb([512], BF16)
    b_x0b, b_vvb = P.buf(), P.buf()
    tb = sb([4, 128], BF16)
    b_tb = P.buf()
    ps, bk = g.ps, g.bk
    for blk in range(nblk):
        t0 = blk * 512
        P.dma("sp", hb, S.h2T[:, :, t0:t0 + 514].rearrange("k p n -> p k n"), reads=[g.b_scrH], writes=[b_hb])
        for i in range(8):
            for part in range(3):
                ch = part * 8 + i
                ub = (3 * i + part) % 2
                hcol = (3 * i + part) % 64
                hbk = 2 if ub == 0 else 4
                for k in range(8):
                    P.op("pe", lambda e, k=k, ch=ch, ub=ub: e.matmul(ps[:, ub, :], lhsT=W[:, k, ch * 128:(ch + 1) * 128],
                                                                 rhs=hb[:, k, 1:513], start=(k == 0), stop=(k == 7)),
                         reads=[b_w, b_hb], writes=[bk[ub]])
                for k in range(8):
                    P.op("pe", lambda e, k=k, ch=ch, hcol=hcol, hbk=hbk: e.matmul(ps[:, hbk, 2 * hcol:2 * hcol + 2], lhsT=W[:, k, ch * 128:(ch + 1) * 128],
                                                                     rhs=hb[:, k, 0:514:513], start=(k == 0), stop=(k == 7)),
                         reads=[b_w, b_hb], writes=[bk[hbk]])
                c = cb[part]
                bc = b_cb[part]
                uA = ps[:, ub, :]
                uB = ps[:, hbk, 2 * hcol:2 * hcol + 2]
                P.op("act", lambda e, c=c, uA=uA, ch=ch: e.activation(out=c, in_=uA, func=AF.Identity, scale=cw[:, ch, 1:2], bias=cw[:, ch, 3:4]),
                     reads=[bk[ub], b_rows], writes=[bc])
                P.op("dve", lambda e, c=c, uA=uA, ch=ch: e.scalar_tensor_tensor(out=c[:, 1:512], in0=uA[:, 0:511], scalar=cw[:, ch, 0:1], in1=c[:, 1:512],
                                                                          op0=ALU.mult, op1=ALU.add), reads=[bk[ub], b_rows, bc], writes=[bc])
                P.op("dve", lambda e, c=c, uA=uA, ch=ch: e.scalar_tensor_tensor(out=c[:, 0:511], in0=uA[:, 1:512], scalar=cw[:, ch, 2:3], in1=c[:, 0:511],
                                                                          op0=ALU.mult, op1=ALU.add), reads=[bk[ub], b_rows, bc], writes=[bc])
                P.op("dve", lambda e, c=c, uB=uB, ch=ch: e.scalar_tensor_tensor(out=c[:, 0:1], in0=uB[:, 0:1], scalar=cw[:, ch, 0:1], in1=c[:, 0:1],
                                                                          op0=ALU.mult, op1=ALU.add), reads=[bk[hbk], b_rows, bc], writes=[bc])
                P.op("dve", lambda e, c=c, uB=uB, ch=ch: e.scalar_tensor_tensor(out=c[:, 511:512], in0=uB[:, 1:2], scalar=cw[:, ch, 2:3], in1=c[:, 511:512],
                                                                          op0=ALU.mult, op1=ALU.add), reads=[bk[hbk], b_rows, bc], writes=[bc])
            P.op("act", lambda e: e.activation(out=x0b, in_=cb[0], func=AF.Copy), reads=[b_cb[0]], writes=[b_x0b])
            P.op("pool", lambda e: e.tensor_tensor(out=vvb, in0=cb[2], in1=cb[1], op=ALU.mult), reads=[b_cb[2], b_cb[1]], writes=[b_vvb])
            g.dbg_stores.append(P.dma("act", S.x0T[i][:, t0:t0 + 512], x0b, reads=[b_x0b], writes=[g.b_scrH]))
            g.dbg_stores.append(P.dma("pool", S.vT[i][:, t0:t0 + 512], vvb, reads=[b_vvb], writes=[g.b_scrH]))
            fm_to_tok(g, vvb, b_vvb, S.vtok[t0:t0 + 512, i * 128:(i + 1) * 128].rearrange("(j p) c -> p j c", p=128),
                      tb, b_tb, 3, g.b_scrH)
    if DEBUG == "H3":
        zt_ = sb([T - 512], BF16)
        b_z_ = P.buf()
        P.op("pool", lambda e: e.memset(zt_, 0.0), writes=[b_z_])
        for r in range(4, 64):
            P.dma("pool", S.vtok[r * 128:(r + 1) * 128, :], zt_[:, 0:1024], reads=[b_z_], writes=[g.b_scrH])
        P.dma("pool", S.x0T[0][:, 512:T], zt_, reads=[b_z_], writes=[g.b_scrH])
        P.dma("pool", S.vT[0][:, 512:T], zt_, reads=[b_z_], writes=[g.b_scrH])
    g.sb_off = mark
    P.barrier()
    if DEBUG == "H1":
        return
    hyena_filter(g)
    if DEBUG == "H2":
        return
    hyena_conv(g)
    if DEBUG == "H3":
        return
    phase_D(g, 1, S.x2, I.hy_w_out, modA, G, b_modA, b_G)
    phase_E(g, 1, g.out, modA, G, b_modA, b_G)


def dft16k(g, planes, A, consumer, groups, b_src, preload=None):
    nc, P, I, S, sb = g.nc, g.P, g.I, g.S, g.sb
    npl = len(planes)
    G = [sb([128, 128], BF16) for _ in range(npl)]
    Y = sb([2, 128, 128], BF16)
    mt = [sb([2, 256], BF16) for _ in range(4)]
    b_G, b_Y = P.buf(), P.buf()
    b_mt = [P.buf() for _ in range(4)]
    ps, bk = g.ps, g.bk
    for gi in groups:
        for pl in range(npl):
            v = planes[pl][:, gi * 128:(gi + 1) * 128].rearrange("(a p) c -> a p c", p=128)
            for q in range(4):
                P.dma("sp", G[pl][0:A, q * 32:(q + 1) * 32, :], v[:, q * 32:(q + 1) * 32, :], reads=[b_src], writes=[b_G])
        for c0 in range(0, 128, 2):
            b = (c0 // 2) % 2
            for cc in range(2):
                for pl in range(npl):
                    P.op("pe", lambda e, c=c0 + cc, cc=cc, b=b, pl=pl: e.matmul(
                        ps[:, b, cc * 256:(cc + 1) * 256], lhsT=G[pl][0:A, :, c], rhs=g.F1[0:A, pl, :],
                        start=(pl == 0), stop=(pl == npl - 1)), reads=[b_G, g.b_dft], writes=[bk[b]])
            P.op("act", lambda e, c0=c0, b=b: e.activation(
                out=Y[:, :, :, c0:c0 + 2], in_=ps[:, b, :].rearrange("p (cc r k) -> p r k cc", cc=2, r=2),
                func=AF.Copy), reads=[bk[b]], writes=[b_Y])
        def ld(k1):
            P.dma("sp", mt[k1 % 4], S.MTb[k1].rearrange("p (v n) -> p v n", v=2), reads=[g.b_dft], writes=[b_mt[k1 % 4]])
            if preload is not None:
                preload(gi, k1)
        for k1 in range(3):
            ld(k1)
        for k1 in range(128):
            par = k1 % 4
            b = k1 % 4
            if k1 + 3 < 128:
                ld(k1 + 3)
            P.op("pe", lambda e, k1=k1, b=b, par=par: e.matmul(ps[:, b, 0:256], lhsT=Y[:, 0, k1, :], rhs=mt[par][:, 0, :],
                                                           start=True, stop=False), reads=[b_Y, b_mt[par]], writes=[bk[b]])
            P.op("pe", lambda e, k1=k1, b=b, par=par: e.matmul(ps[:, b, 0:256], lhsT=Y[:, 1, k1, :], rhs=mt[par][:, 1, :],
                                                           start=False, stop=True), reads=[b_Y, b_mt[par]], writes=[bk[b]])
            consumer(gi, k1, ps[:, b, 0:256], b)


def hyena_conv(g):
    nc, P, I, S, sb = g.nc, g.P, g.I, g.S, g.sb
    ps, bk = g.ps, g.bk
    base = g.sb_off
    groups = range(8) if DEBUG != "H3" else [0]
    g.F1 = sb([2, 256], BF16)
    g.b_dft = P.buf()
    P.dma("pool", g.F1, I.dft_f1, writes=[g.b_dft])
    mark = g.sb_off
    mf = [sb([512], F32) for _ in range(2)]
    mb = [sb([512], BF16) for _ in range(2)]
    b_mf, b_mb = [P.buf(), P.buf()], [P.buf(), P.buf()]
    for k1 in range(128):
        par = k1 % 2
        P.dma("sp", mf[par], I.dft_mt[k1], writes=[b_mf[par]])
        P.op("act", lambda e, par=par: e.activation(out=mb[par], in_=mf[par], func=AF.Copy), reads=[b_mf[par]], writes=[b_mb[par]])
        P.dma("act", S.MTb[k1], mb[par], reads=[b_mb[par]], writes=[g.b_dft])
    g.sb_off = mark
    P.barrier()
    kt = [sb([256], BF16) for _ in range(2)]
    b_kt = [P.buf(), P.buf()]
    g.b_scrKS = P.buf()

    def cons_filter(gi, k1, pz, bank):
        par = k1 % 2
        P.op("act", lambda e: e.activation(out=kt[par], in_=pz, func=AF.Copy), reads=[bk[bank]], writes=[b_kt[par]])
        P.dma("act", S.KS[gi][:, k1, :], kt[par], reads=[b_kt[par]], writes=[g.b_scrKS])
    dft16k(g, [S.ktok], 128, cons_filter, groups, g.b_scrK)
    g.sb_off = mark
    P.barrier()
    ksb = [sb([256], BF16) for _ in range(4)]
    b_ksb = [P.buf() for _ in range(4)]
    tq = [[sb([128], F32) for _ in range(4)] for _ in range(2)]
    b_tq = [[P.buf() for _ in range(4)] for _ in range(2)]
    zz = [sb([2, 128], BF16) for _ in range(4)]
    b_zz = [P.buf() for _ in range(4)]
    zt = [sb([2, 128], BF16) for _ in range(4)]
    b_zt = [P.buf() for _ in range(4)]
    g.b_scrZ = P.buf()

    def cons_signal(gi, k1, pz, bank):
        par = k1 % 4
        ta, tb_, tc, td = tq[k1 % 2]
        b_ta, b_tb2, b_tc, b_td = b_tq[k1 % 2]
        K = ksb[k1 % 4]
        b_K = b_ksb[k1 % 4]
        xr, xi = pz[:, 0:128], pz[:, 128:256]
        kr, ki = K[:, 0:128], K[:, 128:256]
        P.op("dve", lambda e: e.tensor_tensor(out=ta, in0=xr, in1=kr, op=ALU.mult), reads=[bk[bank], b_K], writes=[b_ta])
        P.op("dve", lambda e: e.tensor_tensor(out=tb_, in0=xi, in1=ki, op=ALU.mult), reads=[bk[bank], b_K], writes=[b_tb2])
        P.op("dve", lambda e: e.tensor_tensor(out=tc, in0=xr, in1=ki, op=ALU.mult), reads=[bk[bank], b_K], writes=[b_tc])
        P.op("dve", lambda e: e.tensor_tensor(out=td, in0=xi, in1=kr, op=ALU.mult), reads=[bk[bank], b_K], writes=[b_td])
        Z = zz[par]
        P.op("pool", lambda e: e.tensor_tensor(out=Z[:, 0, :], in0=ta, in1=tb_, op=ALU.subtract), reads=[b_ta, b_tb2], writes=[b_zz[par]])
        P.op("dve", lambda e: e.scalar_tensor_tensor(out=Z[:, 1, :], in0=tc, scalar=-1.0, in1=td, op0=ALU.mult, op1=ALU.subtract),
             reads=[b_tc, b_td], writes=[b_zz[par]])
        tbank = 4 + par
        pT = ps[:, tbank, :].bitcast(BF16)[:, 0:256].rearrange("p (v c) -> p v c", v=2)
        for v in range(2):
            P.op("pe", lambda e, v=v: e.transpose(out=pT[:, v, :], in_=Z[:, v, :], identity=g.ident),
                 reads=[b_zz[par], g.bconst], writes=[bk[tbank]])
        P.op("act", lambda e: e.activation(out=zt[par], in_=pT, func=AF.Copy), reads=[bk[tbank]], writes=[b_zt[par]])
        for v in range(2):
            dst = S.Ztok[v][:, gi * 128:(gi + 1) * 128].rearrange("(k2 k1) c -> k1 k2 c", k1=128)[k1]
            P.dma("act", dst, zt[par][:, v, :], reads=[b_zt[par]], writes=[g.b_scrZ])
    def pre_signal(gi, k1):
        P.dma("sp", ksb[k1 % 4], S.KS[gi][:, k1, :], reads=[g.b_scrKS], writes=[b_ksb[k1 % 4]])
    dft16k(g, [S.vtok], 64, cons_signal, groups, g.b_scrH, preload=pre_signal)
    g.sb_off = mark
    P.barrier()
    yt = sb([T], F32)
    vt = sb([T], BF16)
    x0t = sb([T], BF16)
    ot = vt
    hc = sb([8, 2], F32)
    rk = sb([8], F32)
    b_yt, b_vt, b_ot, b_c = P.buf(), P.buf(), P.buf(), P.buf()
    P.dma("sp", hc, I.hy_cols, writes=[b_c])
    P.op("dve", lambda e: e.reciprocal(out=rk, in_=g.knorm), reads=[g.b_knorm], writes=[b_c])
    P.op("dve", lambda e: e.tensor_scalar(out=rk, in0=rk, scalar1=1.0 / 16384.0, scalar2=None, op0=ALU.mult), reads=[b_c], writes=[b_c])
    ytv = yt.rearrange("p (n2 n1) -> p n1 n2", n1=128)

    def cons_inv(gi, k1, pz, bank):
        P.op("act", lambda e: e.activation(out=ytv[:, k1, :], in_=pz[:, 0:64], func=AF.Copy), reads=[bk[bank]], writes=[b_yt])
        if k1 == 127:
            P.dma("sp", vt, S.vT[gi], reads=[g.b_scrH], writes=[b_vt])
            P.dma("sp", x0t, S.x0T[gi], reads=[g.b_scrH], writes=[b_vt])
            if S.y_dbg is not None and gi == 0:
                g.dbg_stores.append(P.dma("sp", S.y_dbg, yt, reads=[b_yt]))
            P.op("dve", lambda e: e.tensor_scalar(out=yt, in0=yt, scalar1=rk[:, gi:gi + 1], scalar2=None, op0=ALU.mult),
                 reads=[b_yt, b_c], writes=[b_yt])
            P.op("dve", lambda e: e.scalar_tensor_tensor(out=yt, in0=vt, scalar=hc[:, gi, 0:1], in1=yt, op0=ALU.mult, op1=ALU.add),
                 reads=[b_vt, b_yt, b_c], writes=[b_yt])
            P.op("dve", lambda e: e.tensor_tensor(out=ot, in0=yt, in1=x0t, op=ALU.mult), reads=[b_yt, b_vt], writes=[b_vt])
            g.dbg_stores.append(P.dma("pool", S.mixT[gi], ot, reads=[b_vt], writes=[g.b_scrB]))
    dft16k(g, [S.Ztok[0], S.Ztok[1]], 128, cons_inv, groups, g.b_scrZ)
    g.sb_off = base
    P.barrier()


def hyena_filter(g):
    nc, P, I, S, sb = g.nc, g.P, g.I, g.S, g.sb
    mark = g.sb_off
    w1 = sb([64], F32)
    w23 = sb([2, 64], F32)
    w4 = sb([2 * D], F32)
    fc = sb([4], F32)
    sc = sb([4], F32)
    hc = sb([8, 2], F32)
    b_w = P.buf()
    P.dma("sp", w1[0:33, :], I.hyf_w1, writes=[b_w])
    P.dma("sp", w23[0:64], I.hyf_w23.rearrange("l k n -> k l n"), writes=[b_w])
    P.dma("sp", w4[0:64, :], I.hyf_w4, writes=[b_w])
    P.dma("sp", fc[0:64, :], I.hyf_cols, writes=[b_w])
    P.dma("sp", hc, I.hy_cols, writes=[b_w])
    P.op("dve", lambda e: e.tensor_scalar(out=sc[0:64, 0:1], in0=fc[0:64, 0:1], scalar1=1.0 / 3.0, scalar2=None, op0=ALU.mult),
         reads=[b_w], writes=[b_w])
    P.op("dve", lambda e: e.tensor_scalar(out=sc[0:64, 1:4], in0=fc[0:64, 1:4], scalar1=sc[0:64, 0:1], scalar2=None, op0=ALU.mult),
         reads=[b_w], writes=[b_w])
    P.op("pool", lambda e: e.memset(g.knorm, 0.0), writes=[g.b_knorm])
    zt = [sb([512], F32) for _ in range(2)]
    tv = [sb([512], F32) for _ in range(2)]
    b_zt = [P.buf(), P.buf()]
    hs = sb([512], F32)
    ht = sb([512], F32)
    hx = sb([512], F32)
    b_hs, b_ht, b_hx = P.buf(), P.buf(), P.buf()
    dec = [sb([512], F32) for _ in range(2)]
    kr = [sb([512], F32) for _ in range(2)]
    krb = [sb([512], BF16) for _ in range(2)]
    part = [sb([1], F32) for _ in range(2)]
    b_dec, b_kr, b_krb, b_part = [[P.buf(), P.buf()] for _ in range(4)]
    tb = [sb([4, 128], BF16) for _ in range(2)]
    b_tb = [P.buf(), P.buf()]
    ps, bk = g.ps, g.bk
    g.b_scrK = P.buf()
    it = 0
    for dr in range(2):
        for blk in range(16):
            n0 = blk * 512
            par = it % 2
            it += 1
            P.dma("sp", zt[par][0:33, :], I.hyf_z[dr][:, n0:n0 + 512], writes=[b_zt[par]])
            P.dma("sp", tv[par], I.hyf_t[dr][:, n0:n0 + 512], writes=[b_zt[par]])
            src, b_src = zt[par], b_zt[par]
            kdim = 33
            for layer in range(3):
                lw = w1[0:33, :] if layer == 0 else w23[0:64, layer - 1, :]
                P.op("pe", lambda e, lw=lw, src=src, kdim=kdim: e.matmul(ps[0:64, 0, :], lhsT=lw, rhs=src[0:kdim, :], start=True, stop=True),
                     reads=[b_w, b_src], writes=[bk[0]])
                P.op("act", lambda e, layer=layer: e.activation(out=hs[0:64, :], in_=ps[0:64, 0, :], func=AF.Sin,
                                                              scale=sc[0:64, 0:1], bias=sc[0:64, 1 + layer:2 + layer]),
                     reads=[bk[0], b_w], writes=[b_hs])
                P.op("dve", lambda e: e.tensor_tensor(out=ht[0:64, :], in0=hs[0:64, :], in1=hs[0:64, :], op=ALU.mult), reads=[b_hs], writes=[b_ht])
                P.op("dve", lambda e: e.tensor_scalar(out=ht[0:64, :], in0=ht[0:64, :], scalar1=-4.0, scalar2=3.0, op0=ALU.mult, op1=ALU.add),
                     reads=[b_ht], writes=[b_ht])
                P.op("dve", lambda e: e.tensor_tensor(out=hx[0:64, :], in0=hs[0:64, :], in1=ht[0:64, :], op=ALU.mult), reads=[b_hs, b_ht], writes=[b_hx])
                src, b_src, kdim = hx, b_hx, 64
            for ci in range(8):
                bank = 1 + ci % 2
                c0 = dr * D + ci * 128
                P.op("pe", lambda e, c0=c0, bank=bank: e.matmul(ps[:, bank, :], lhsT=w4[0:64, c0:c0 + 128], rhs=hx[0:64, :], start=True, stop=True),
                     reads=[b_w, b_hx], writes=[bk[bank]])
                q = ci % 2
                P.op("act", lambda e, ci=ci, par=par, q=q: e.activation(out=dec[q], in_=tv[par], func=AF.Exp, scale=hc[:, ci, 1:2]),
                     reads=[b_zt[par], b_w], writes=[b_dec[q]])
                P.op("dve", lambda e, bank=bank, q=q: e.tensor_tensor(out=kr[q], in0=ps[:, bank, :], in1=dec[q], op=ALU.mult),
                     reads=[bk[bank], b_dec[q]], writes=[b_kr[q]])
                if dr == 1 and blk == 0:
                    P.op("dve", lambda e, q=q: e.memset(kr[q][:, 0:1], 0.0), reads=[b_kr[q]], writes=[b_kr[q]])
                P.op("dve", lambda e, q=q: e.tensor_reduce(out=part[q], in_=kr[q], axis=mybir.AxisListType.X, op=ALU.add, apply_absolute_value=True),
                     reads=[b_kr[q]], writes=[b_part[q]])
                P.op("dve", lambda e, ci=ci, q=q: e.tensor_tensor(out=g.knorm[:, ci:ci + 1], in0=g.knorm[:, ci:ci + 1], in1=part[q], op=ALU.add),
                     reads=[b_part[q], g.b_knorm], writes=[g.b_knorm])
                P.op("act", lambda e, q=q: e.activation(out=krb[q], in_=kr[q], func=AF.Copy), reads=[b_kr[q]], writes=[b_krb[q]])
                r0 = dr * T + n0
                fm_to_tok(g, krb[ci % 2], b_krb[ci % 2], S.ktok[r0:r0 + 512, ci * 128:(ci + 1) * 128].rearrange("(j p) c -> p j c", p=128),
                          tb[ci % 2], b_tb[ci % 2], 3 + (ci % 2), g.b_scrK)
    if S.knorm_dbg is not None:
        g.dbg_stores.append(P.dma("sp", S.knorm_dbg, g.knorm, reads=[g.b_knorm]))
    g.sb_off = mark
    P.barrier()


def ln_stats(g, tile, b_tile, mv, sd, b_s):
    P = g.P
    st = g.ln_st
    for hh in range(2):
        P.op("dve", lambda e, hh=hh: e.bn_stats(out=st[:, hh, :], in_=tile[:, hh * 512:(hh + 1) * 512]),
             reads=[b_tile], writes=[b_s])
    P.op("dve", lambda e: e.bn_aggr(out=mv, in_=st), reads=[b_s], writes=[b_s])
    P.op("act", lambda e: e.activation(out=sd, in_=mv[:, 1:2], func=AF.Sqrt, bias=g.epsT[:, 0:1], scale=1.0),
         reads=[b_s, g.bconst], writes=[b_s])
    P.op("dve", lambda e: e.reciprocal(out=sd, in_=sd), reads=[b_s], writes=[b_s])


def resid_ln(g, pY, b_pY, xt, b_xt, Grow, b_G, lng, lnb, b_rows, tmp, b_tmp, mv, sd, b_s):
    P = g.P
    P.op("dve", lambda e: e.tensor_tensor(out=tmp, in0=pY, in1=Grow, op=ALU.mult), reads=b_pY + [b_G], writes=[b_tmp])
    P.op("dve", lambda e: e.scalar_tensor_tensor(out=xt, in0=xt, scalar=ALPHA, in1=tmp, op0=ALU.mult, op1=ALU.add),
         reads=[b_xt, b_tmp], writes=[b_xt])
    ln_stats(g, xt, b_xt, mv, sd, b_s)
    P.op("dve", lambda e: e.tensor_scalar(out=tmp, in0=xt, scalar1=mv[:, 0:1], scalar2=sd[:, 0:1],
                                          op0=ALU.subtract, op1=ALU.mult), reads=[b_xt, b_s], writes=[b_tmp])
    P.op("pool", lambda e: e.tensor_tensor(out=tmp, in0=tmp, in1=lng, op=ALU.mult), reads=[b_tmp, b_rows], writes=[b_tmp])
    P.op("pool", lambda e: e.tensor_tensor(out=xt, in0=tmp, in1=lnb, op=ALU.add), reads=[b_tmp, b_rows], writes=[b_xt])


def phase_D(g, l, x_src, wout_ap, modA, G, b_modA, b_G):
    nc, P, I, S, sb = g.nc, g.P, g.I, g.S, g.sb
    mark = g.sb_off
    Wo = sb([8, D], BF16)
    lng, lnb = sb([D], F32), sb([D], F32)
    b_w, b_rows = P.buf(), P.buf()
    P.dma("pool", Wo, wout_ap.rearrange("(k p) n -> p k n", p=128), writes=[b_w])
    r0 = ROWS.index(f"ln1_g{l}")
    P.dma("sp", lng, I.rows[r0], writes=[b_rows])
    P.dma("sp", lnb, I.rows[r0 + 1], writes=[b_rows])
    g.ln_st = sb([2, 6], F32)
    g.ln_mv = sb([4, 2], F32)
    g.ln_sd = sb([4], F32)
    mv, sd = sb([2], F32), sb([1], F32)
    b_s = P.buf()
    mixb = sb([8, 512], BF16)
    b_mix = P.buf()
    xts = [sb([D], F32) for _ in range(4)]
    b_xts = [P.buf() for _ in range(4)]
    xn = [sb([D], BF16) for _ in range(4)]
    b_xn = [P.buf() for _ in range(4)]
    tmp = sb([D], F32)
    b_tmp = P.buf()
    hT = sb([8, 512], BF16)
    b_hT = P.buf()
    zt = sb([8, 2], BF16)
    b_z = P.buf()
    P.op("pool", lambda e: e.memset(zt, 0.0), writes=[b_z])
    g.b_scrD = P.buf()
    P.dma("pool", S.h2T[:, :, 0:1].rearrange("k p o -> p k o"), zt[:, :, 0:1], reads=[b_z], writes=[g.b_scrD], allow_slow_non_contiguous=True)
    P.dma("pool", S.h2T[:, :, T + 1:T + 2].rearrange("k p o -> p k o"), zt[:, :, 1:2], reads=[b_z], writes=[g.b_scrD], allow_slow_non_contiguous=True)
    bk = g.bk
    pY = g.ps[:, 6:8, :].rearrange("p a b -> p (a b)")
    nblk = 16 if DEBUG not in ("D1", "H1", "H2", "H3") else 1
    if DEBUG in ("D1", "H1", "H2", "H3") and l == 0:
        zz = sb([8, 512], BF16)
        P.op("pool", lambda e: e.memset(zz, 0.0), writes=[b_z])
        P.dma("pool", S.mixT[:, :, 0:512].rearrange("k p n -> p k n"), zz, reads=[b_z], writes=[g.b_scrB])
        P.dma("pool", S.h2T[:, :, 513:514].rearrange("k p o -> p k o"), zt[:, :, 0:1], reads=[b_z], writes=[g.b_scrD], allow_slow_non_contiguous=True)
    for blk in range(nblk):
        t0 = blk * 512
        P.dma("sp", mixb, S.mixT[:, :, t0:t0 + 512].rearrange("k p n -> p k n"), reads=[g.b_scrB], writes=[b_mix])
        for i in range(4):
            P.dma("sp", xts[i], x_src[t0 + i * 128:t0 + (i + 1) * 128, :], reads=[g.b_scrX], writes=[b_xts[i]])
            for half in range(2):
                for k in range(8):
                    P.op("pe", lambda e, k=k, i=i, half=half: e.matmul(
                        g.ps[:, 6 + half, :], lhsT=mixb[:, k, i * 128:(i + 1) * 128], rhs=Wo[:, k, half * 512:(half + 1) * 512],
                        start=(k == 0), stop=(k == 7)), reads=[b_mix, b_w], writes=[bk[6 + half]])
            resid_ln(g, pY, [bk[6], bk[7]], xts[i], b_xts[i], G[:, 0, :], b_G, lng, lnb, b_rows, tmp, b_tmp, mv, sd, b_s)
            g.dbg_stores.append(P.dma("pool", S.x1[t0 + i * 128:t0 + (i + 1) * 128, :], xts[i], reads=[b_xts[i]], writes=[g.b_scrD]))
        ln_to_featmajor(g, None, 4, hT, b_hT, [modA[:, 32 + k, 0:1] for k in range(8)], [modA[:, 24 + k, 0:1] for k in range(8)],
                        b_modA, xts, b_xts, xn, b_xn, (P.buf(), P.buf(), bk))
        g.dbg_stores.append(P.dma("act", S.h2T[:, :, 1 + t0:1 + t0 + 512].rearrange("k p n -> p k n"), hT, reads=[b_hT], writes=[g.b_scrD]))
    g.sb_off = mark
    P.barrier()


def phase_E(g, l, dst, modA, G, b_modA, b_G):
    nc, P, I, S, sb = g.nc, g.P, g.I, g.S, g.sb
    mark = g.sb_off
    W1 = sb([8, 2 * FF], BF16)
    W2 = sb([22, D], BF16)
    cw = sb([44, 4], F32)
    lng, lnb = sb([D], F32), sb([D], F32)
    b_w, b_rows = P.buf(), P.buf()
    for k in range(8):
        P.dma("pool", W1[:, k, :], I.ffn_w_in[l][k * 128:(k + 1) * 128, :], writes=[b_w])
    P.dma("pool", W2, I.ffn_w_out[l].rearrange("(k p) n -> p k n", p=128), writes=[b_w])
    P.dma("sp", cw, I.ffn_cw[l], writes=[b_rows])
    r0 = ROWS.index(f"ln2_g{l}")
    P.dma("sp", lng, I.rows[r0], writes=[b_rows])
    P.dma("sp", lnb, I.rows[r0 + 1], writes=[b_rows])
    g.ln_st = sb([2, 6], F32)
    mv, sd = sb([2], F32), sb([1], F32)
    b_s = P.buf()
    hb = sb([8, 514], BF16)
    b_hb = P.buf()
    hid = sb([22, 512], BF16)
    b_hid = P.buf()
    cb4 = [[sb([512], F32) for _ in range(2)] for _ in range(2)]
    b_cb4 = [[P.buf(), P.buf()], [P.buf(), P.buf()]]
    ga2 = [sb([512], F32) for _ in range(2)]
    b_ga2 = [P.buf(), P.buf()]
    xt = sb([D], F32)
    b_xt = P.buf()
    tmp = sb([D], F32)
    b_tmp = P.buf()
    bk = g.bk
    ps = g.ps
    pY = ps[:, 6:8, :].rearrange("p a b -> p (a b)")
    nblk = 16 if DEBUG not in ("D1", "H1", "H2", "H3") else 1
    for blk in range(nblk):
        t0 = blk * 512
        P.dma("sp", hb, S.h2T[:, :, t0:t0 + 514].rearrange("k p n -> p k n"), reads=[g.b_scrD], writes=[b_hb])
        for i in range(22):
            for part in range(2):
                ch = part * 22 + i
                ub = (2 * i + part) % 2
                hcol = (2 * i + part) % 64
                hbk = 2 + ub
                for k in range(8):
                    P.op("pe", lambda e, k=k, ch=ch, ub=ub: e.matmul(ps[:, ub, :], lhsT=W1[:, k, ch * 128:(ch + 1) * 128],
                                                                 rhs=hb[:, k, 1:513], start=(k == 0), stop=(k == 7)),
                         reads=[b_w, b_hb], writes=[bk[ub]])
                for k in range(8):
                    P.op("pe", lambda e, k=k, ch=ch, hcol=hcol, hbk=hbk: e.matmul(ps[:, hbk, 2 * hcol:2 * hcol + 2], lhsT=W1[:, k, ch * 128:(ch + 1) * 128],
                                                                     rhs=hb[:, k, 0:514:513], start=(k == 0), stop=(k == 7)),
                         reads=[b_w, b_hb], writes=[bk[hbk]])
                c = cb4[i % 2][part]
                bc = b_cb4[i % 2][part]
                uA = ps[:, ub, :]
                uB = ps[:, hbk, 2 * hcol:2 * hcol + 2]
                P.op("act", lambda e, c=c, uA=uA, ch=ch: e.activation(out=c, in_=uA, func=AF.Identity, scale=cw[:, ch, 1:2], bias=cw[:, ch, 3:4]),
                     reads=[bk[ub], b_rows], writes=[bc])
                P.op("dve", lambda e, c=c, uA=uA, ch=ch: e.scalar_tensor_tensor(out=c[:, 1:512], in0=uA[:, 0:511], scalar=cw[:, ch, 0:1], in1=c[:, 1:512],
                                                                          op0=ALU.mult, op1=ALU.add), reads=[bk[ub], b_rows, bc], writes=[bc])
                P.op("dve", lambda e, c=c, uA=uA, ch=ch: e.scalar_tensor_tensor(out=c[:, 0:511], in0=uA[:, 1:512], scalar=cw[:, ch, 2:3], in1=c[:, 0:511],
                                                                          op0=ALU.mult, op1=ALU.add), reads=[bk[ub], b_rows, bc], writes=[bc])
                P.op("dve", lambda e, c=c, uB=uB, ch=ch: e.scalar_tensor_tensor(out=c[:, 0:1], in0=uB[:, 0:1], scalar=cw[:, ch, 0:1], in1=c[:, 0:1],
                                                                          op0=ALU.mult, op1=ALU.add), reads=[bk[hbk], b_rows, bc], writes=[bc])
                P.op("dve", lambda e, c=c, uB=uB, ch=ch: e.scalar_tensor_tensor(out=c[:, 511:512], in0=uB[:, 1:2], scalar=cw[:, ch, 2:3], in1=c[:, 511:512],
                                                                          op0=ALU.mult, op1=ALU.add), reads=[bk[hbk], b_rows, bc], writes=[bc])
            P.op("act", lambda e, i=i: e.activation(out=ga2[i % 2], in_=cb4[i % 2][0], func=AF.Gelu), reads=[b_cb4[i % 2][0]], writes=[b_ga2[i % 2]])
            P.op("pool", lambda e, i=i: e.tensor_tensor(out=hid[:, i, :], in0=ga2[i % 2], in1=cb4[i % 2][1], op=ALU.mult),
                 reads=[b_ga2[i % 2], b_cb4[i % 2][1]], writes=[b_hid])
        for it in range(4):
            P.dma("sp", xt, S.x1[t0 + it * 128:t0 + (it + 1) * 128, :], reads=[g.b_scrD], writes=[b_xt])
            for half in range(2):
                for i in range(22):
                    P.op("pe", lambda e, i=i, it=it, half=half: e.matmul(ps[:, 6 + half, :], lhsT=hid[:, i, it * 128:(it + 1) * 128],
                                                                     rhs=W2[:, i, half * 512:(half + 1) * 512], start=(i == 0), stop=(i == 21)),
                         reads=[b_hid, b_w], writes=[bk[6 + half]])
            resid_ln(g, pY, [bk[6], bk[7]], xt, b_xt, G[:, 1, :], b_G, lng, lnb, b_rows, tmp, b_tmp, mv, sd, b_s)
            g.out_stores.append(P.dma("pool", dst[t0 + it * 128:t0 + (it + 1) * 128, :], xt, reads=[b_xt], writes=[g.b_scrX]))
    g.sb_off = mark
    P.barrier()


def rope_tables():
    pos = np.arange(T)
    pr = (pos // 64).astype(np.float32)
    pc = (pos % 64).astype(np.float32)
    inv = (np.float32(10000.0) ** (-np.arange(0, 32, 2, dtype=np.float32) / np.float32(32))).astype(np.float32)
    C = np.ones((64, NK), np.float32)
    Sg = np.zeros((64, NK), np.float32)
    for d in range(64):
        p = pr if d < 32 else pc
        a = (p * inv[d % 16]).astype(np.float32)
        C[d, NCTX:] = np.cos(a)
        sgn = -1.0 if (d % 32) < 16 else 1.0
        Sg[d, NCTX:] = sgn * np.sin(a)
    return np.concatenate([C, C], 0), np.concatenate([Sg, Sg], 0)


def rope_perm_cols():
    idx = np.arange(512)
    d = idx % 64
    partner = np.where((d % 32) < 16, d + 16, d - 16)
    return (idx // 64) * 64 + partner


def make_inputs(inp):
    f32 = np.float32
    common = {}
    common["mod_w"] = np.ascontiguousarray(inp["mod_w"], f32)
    common["modb_col"] = np.ascontiguousarray(inp["mod_b"].reshape(2, 48, 128).transpose(0, 2, 1), f32)
    rows = {}
    for l in range(2):
        rows[f"ln1_g{l}"] = inp["ln1_g"][l]
        rows[f"ln1_b{l}"] = inp["ln1_b"][l]
        rows[f"ln2_g{l}"] = inp["ln2_g"][l]
        rows[f"ln2_b{l}"] = inp["ln2_b"][l]
        rows[f"modb_g1_{l}"] = inp["mod_b"][l, 2 * D:3 * D]
        rows[f"modb_g2_{l}"] = inp["mod_b"][l, 5 * D:6 * D]
    common["rows"] = np.ascontiguousarray(
        np.stack([np.broadcast_to(rows[r][None, :], (128, D)) for r in ROWS]), f32)
    common["cols"] = np.zeros((NCOLS, 128, 1), f32)
    common["cols"][0, :, 0] = inp["da_subln_g"][0]
    common["ident"] = np.eye(128, dtype=f32)
    w_in = np.asarray(inp["da_w_in"][0], f32)
    common["da_w_in"] = np.ascontiguousarray(w_in)
    pc = rope_perm_cols()
    common["da_w_perm"] = np.ascontiguousarray(np.concatenate([w_in[:, 0:512][:, pc], w_in[:, 512:1024][:, pc]], 1))
    common["da_w_out"] = np.ascontiguousarray(inp["da_w_out"][0], f32)
    C, Sg = rope_tables()
    common["rope_c"] = C
    common["rope_s"] = Sg
    lamv = np.stack([inp["da_lam_q1"][0], inp["da_lam_k1"][0], inp["da_lam_q2"][0], inp["da_lam_k2"][0]])
    common["lam"] = np.ascontiguousarray(np.broadcast_to(lamv[None], (128, 4, 64)), f32)
    a = np.arange(64)[:, None].astype(np.float64)
    k1 = np.arange(64)[None, :].astype(np.float64)
    th = 2 * np.pi * a * k1 / 64
    common["f64"] = np.concatenate([np.cos(th), -np.sin(th)], 1).astype(f32)
    p = np.arange(128)[:, None, None].astype(np.float64)
    kk = (np.arange(64)[None, :, None] + 64 * np.arange(128)[None, None, :]).astype(np.float64)
    th = 2 * np.pi * p * kk / 8192
    mr, mi = np.cos(th), -np.sin(th)
    common["fmt"] = np.stack([np.concatenate([mr, mi], 2), np.concatenate([-mi, mr], 2)], 2).astype(f32)
    c = np.arange(128)[:, None].astype(np.float64)
    th = 2 * np.pi * c * c.T / 128
    sc = 1.0 / np.sqrt(8192.0 * 128.0)
    common["fcs"] = np.stack([np.cos(th) * sc, np.sin(th) * sc], 1).astype(f32)
    common["ffn_w_in"] = np.ascontiguousarray(inp["ffn_w_in"], f32)
    common["ffn_w_out"] = np.ascontiguousarray(inp["ffn_w_out"], f32)
    cwb = np.concatenate([inp["ffn_conv_w"], inp["ffn_conv_b"][:, None, :]], 1)
    common["ffn_cw"] = np.ascontiguousarray(cwb.reshape(2, 4, 44, 128).transpose(0, 3, 2, 1), f32)
    common["hy_w_in"] = np.ascontiguousarray(inp["hy_w_in"][0], f32)
    common["hy_w_out"] = np.ascontiguousarray(inp["hy_w_out"][0], f32)
    hcw = np.concatenate([inp["hy_conv_w"][0], inp["hy_conv_b"][0][None, :]], 0)
    common["hy_cw"] = np.ascontiguousarray(hcw.reshape(4, 24, 128).transpose(2, 1, 0), f32)
    min_decay = math.log(1e-2) / 1.5
    max_decay = math.log(1e-2) / 0.3
    deltas = np.abs(np.linspace(min_decay, max_decay, D, dtype=f32))
    common["hy_cols"] = np.ascontiguousarray(
        np.stack([inp["hy_d"][0].reshape(8, 128).T, -deltas.reshape(8, 128).T], -1), f32)
    L = T
    tt = np.linspace(0.0, 1.0, L, dtype=f32)
    wv = (f32(2.0 * math.pi) * np.arange(L, dtype=f32) / f32(L)).astype(f32)
    fb = np.linspace(1e-4, 15, 16, dtype=f32)
    ang = (fb[None, :] * wv[:, None]).astype(f32)
    z = np.concatenate([tt[:, None], np.cos(ang), -np.sin(ang)], -1).astype(f32)
    idx = (L - np.arange(L)) % L
    idx[0] = 0
    zr = z[idx]
    common["hyf_z"] = np.ascontiguousarray(np.stack([z.T, zr.T]), f32)
    common["hyf_t"] = np.ascontiguousarray(np.stack([np.broadcast_to(tt[None], (128, L)),
                                                     np.broadcast_to(tt[idx][None], (128, L))]), f32)
    common["hyf_w1"] = np.ascontiguousarray(inp["hy_f_w1"][0], f32)
    common["hyf_w23"] = np.ascontiguousarray(np.stack([inp["hy_f_w2"][0], inp["hy_f_w3"][0]]), f32)
    common["hyf_w4"] = np.ascontiguousarray(inp["hy_f_w4"][0], f32)
    common["hyf_cols"] = np.ascontiguousarray(np.stack([inp["hy_f_freq"][0], inp["hy_f_b1"][0], inp["hy_f_b2"][0],
                                                        inp["hy_f_b3"][0]], -1), f32)
    a = np.arange(128)[:, None].astype(np.float64)
    kk1 = np.arange(128)[None, :].astype(np.float64)
    th = 2 * np.pi * a * kk1 / 128
    common["dft_f1"] = np.stack([np.concatenate([np.cos(th), -np.sin(th)], 1),
                                 np.concatenate([np.sin(th), np.cos(th)], 1)], 1).astype(f32)
    pp = np.arange(128)[None, :, None].astype(np.float64)
    kfull = (np.arange(128)[:, None, None] + 128 * np.arange(128)[None, None, :]).astype(np.float64)
    th = 2 * np.pi * pp * kfull / 16384.0
    mr, mi = np.cos(th), -np.sin(th)
    common["dft_mt"] = np.concatenate([mr, mi, -mi, mr], 2).astype(f32)
    maps = []
    for b in range(4):
        m = dict(common)
        m["x"] = np.ascontiguousarray(inp["x"][b], f32)
        m["ctx"] = np.ascontiguousarray(inp["ctx"][b], f32)
        cv = np.stack([inp["c"][b].reshape(8, 128).T, inp["c_ctx"].reshape(8, 128).T], -1)
        m["cvec"] = np.ascontiguousarray(cv, f32)
        maps.append(m)
    return maps


def kernel(**inputs):
    inp = {k: np.asarray(v) for k, v in inputs.items()}
    nc, _ = build_program()
    maps = make_inputs(inp)
    res = run_bass_kernel_spmd(nc, maps, core_ids=[0, 1, 2, 3])
    return np.stack([np.asarray(r["out"], np.float32) for r in res.results], 0)
```
